# Optimizing a Trainium2 kernel written in Bass

```python
import jax, jax.numpy as jnp
from jax import lax
import numpy as np

D_MODEL = 2048
BATCH = 2
SEQ = 4096
DEPTH = 4
DEC_BATCH = 32
DEC_SEQ = 4
PAST_LEN = 16384
PAGE_SIZE = 128

N_MIXERS = 2
N_ATTN_LAYERS = (DEPTH + 1) // 2
N_CONV_LAYERS = DEPTH // 2
HEAD_DIM = 64
N_Q_HEADS = D_MODEL // HEAD_DIM
N_KV_HEADS = 8
GQA_GROUP = N_Q_HEADS // N_KV_HEADS
WINDOW = 128
ROPE_THETA = 10000.0
CONV_WIDTH = 3
CONV_DIM = D_MODEL
PEER_HEADS = 8
PEER_KEYS = 128
PEER_EXPERTS = PEER_KEYS * PEER_KEYS
PEER_QDIM = 256
PEER_HALF = PEER_QDIM // 2
PEER_TOPK = 16
PEER_CHUNK = 128
LN_EPS = 1e-5
NEG_INF = -1e30
DEEPNORM_ALPHA = (2 * DEPTH) ** 0.25
DEEPNORM_BETA = (8 * DEPTH) ** -0.25

kernel_name = "hybrid_swa_sink_shortconv_peer_deepnorm_step"


def layer_norm(x, g, b):
    xf = x.astype(jnp.float32)
    mu = jnp.mean(xf, axis=-1, keepdims=True)
    xc = xf - mu
    var = jnp.mean(xc * xc, axis=-1, keepdims=True)
    out = xc * lax.rsqrt(var + LN_EPS) * g.astype(jnp.float32) + b.astype(jnp.float32)
    return out.astype(x.dtype)


def rope(x, pos):
    half = HEAD_DIM // 2
    inv = jnp.power(ROPE_THETA, -jnp.arange(half, dtype=jnp.float32) * 2.0 / HEAD_DIM)
    ang = pos.astype(jnp.float32)[:, None] * inv[None, :]
    cos = jnp.cos(ang)[:, None, :]
    sin = jnp.sin(ang)[:, None, :]
    xf = x.astype(jnp.float32)
    x1, x2 = xf[..., :half], xf[..., half:]
    out = jnp.concatenate([x1 * cos - x2 * sin, x2 * cos + x1 * sin], axis=-1)
    return out.astype(x.dtype)


def attn_project(x, w_qkv, b_qkv, pos):
    b, t, _ = x.shape
    qkv = x @ w_qkv + b_qkv
    q, k, v = jnp.split(qkv, [N_Q_HEADS * HEAD_DIM, (N_Q_HEADS + N_KV_HEADS) * HEAD_DIM], axis=-1)
    q = rope(q.reshape(b, t, N_Q_HEADS, HEAD_DIM), pos).reshape(b, t, N_KV_HEADS, GQA_GROUP, HEAD_DIM)
    k = rope(k.reshape(b, t, N_KV_HEADS, HEAD_DIM), pos)
    v = v.reshape(b, t, N_KV_HEADS, HEAD_DIM)
    return q, k, v


def sink_attention(q, k, v, mask, sink):
    s = jnp.einsum('...qkgd,...skd->...kgqs', q, k, preferred_element_type=jnp.float32) * (HEAD_DIM ** -0.5)
    s = jnp.where(mask[..., None, None, :, :], s, NEG_INF)
    sink_l = sink.astype(jnp.float32)[:, :, None, None]
    m = jnp.maximum(jnp.max(s, axis=-1, keepdims=True), sink_l)
    p = jnp.exp(s - m)
    denom = jnp.sum(p, axis=-1, keepdims=True) + jnp.exp(sink_l - m)
    w = (p / denom).astype(v.dtype)
    return jnp.einsum('...kgqs,...skd->...qkgd', w, v)


def swa_prompt(x, w_qkv, b_qkv, w_o, b_o, sink):
    b, t, _ = x.shape
    q, k, v = attn_project(x, w_qkv, b_qkv, jnp.arange(t))
    nb = t // WINDOW
    qb = q.reshape(b, nb, WINDOW, N_KV_HEADS, GQA_GROUP, HEAD_DIM)

    def band(z):
        zb = z.reshape(b, nb, WINDOW, N_KV_HEADS, HEAD_DIM)
        prev = jnp.pad(zb, ((0, 0), (1, 0), (0, 0), (0, 0), (0, 0)))[:, :-1]
        return jnp.concatenate([prev, zb], axis=2)

    i = jnp.arange(WINDOW)[None, :, None]
    j = jnp.arange(2 * WINDOW)[None, None, :]
    blk = jnp.arange(nb)[:, None, None]
    dist = i + WINDOW - j
    kpos = blk * WINDOW - WINDOW + j
    mask = (dist >= 0) & (dist <= WINDOW) & (kpos >= 0)
    o = sink_attention(qb, band(k), band(v), mask, sink)
    y = o.reshape(b, t, N_Q_HEADS * HEAD_DIM) @ w_o + b_o
    return y, k[:, -WINDOW:], v[:, -WINDOW:]


def swa_sample(x, ck, cv, w_qkv, b_qkv, w_o, b_o, sink):
    b, t, _ = x.shape
    q, k, v = attn_project(x, w_qkv, b_qkv, PAST_LEN + jnp.arange(t))
    kk = jnp.concatenate([ck, k], axis=1)
    vv = jnp.concatenate([cv, v], axis=1)
    i = jnp.arange(t)[:, None]
    j = jnp.arange(WINDOW + t)[None, :]
    dist = i + WINDOW - j
    mask = (dist >= 0) & (dist <= WINDOW)
    o = sink_attention(q, kk, vv, mask, sink)
    y = o.reshape(b, t, N_Q_HEADS * HEAD_DIM) @ w_o + b_o
    return y, kk[:, -WINDOW:], vv[:, -WINDOW:]


def short_conv(x, state, w_in, conv_w, w_out):
    t = x.shape[1]
    gate_b, gate_c, h = jnp.split(x @ w_in, 3, axis=-1)
    u = gate_c * h
    up = jnp.concatenate([state, u], axis=1)
    conv = up[:, 0:t] * conv_w[0]
    for tap in range(1, CONV_WIDTH):
        conv = conv + up[:, tap:tap + t] * conv_w[tap]
    y = (gate_b * conv) @ w_out
    return y, up[:, -(CONV_WIDTH - 1):]


def peer(x, w_q, sub_keys, u, v):
    shp = x.shape
    xf = x.reshape(-1, D_MODEL)
    n = xf.shape[0]
    n_chunks = -(-n // PEER_CHUNK)
    xp = jnp.pad(xf, ((0, n_chunks * PEER_CHUNK - n), (0, 0))).reshape(n_chunks, PEER_CHUNK, D_MODEL)

    def chunk(xc):
        q = (xc @ w_q).reshape(PEER_CHUNK, PEER_HEADS, 2, PEER_HALF)
        s = jnp.einsum('chpd,hpnd->chpn', q, sub_keys, preferred_element_type=jnp.float32)
        sv, si = lax.top_k(s, PEER_TOPK)
        cand = (sv[:, :, 0, :, None] + sv[:, :, 1, None, :]).reshape(PEER_CHUNK, PEER_HEADS, -1)
        cidx = (si[:, :, 0, :, None] * PEER_KEYS + si[:, :, 1, None, :]).reshape(PEER_CHUNK, PEER_HEADS, -1)
        fv, fp = lax.top_k(cand, PEER_TOPK)
        eidx = jnp.take_along_axis(cidx, fp, axis=-1)
        g = jax.nn.softmax(fv, axis=-1)
        hid = jnp.einsum('chkd,cd->chk', u[eidx], xc, preferred_element_type=jnp.float32)
        a = (jax.nn.gelu(hid, approximate=False) * g).astype(xc.dtype)
        return jnp.einsum('chk,chkd->cd', a, v[eidx])

    out = lax.map(chunk, xp).reshape(-1, D_MODEL)[:n]
    return out.reshape(shp)


def setup_inputs(seed: int = 0) -> dict:
    key = jax.random.key(seed)
    ks = jax.random.split(key, 20)
    f32 = jnp.float32
    nrm = lambda k, shape, scale: jax.random.normal(k, shape, f32) * scale
    qkv_cols = (N_Q_HEADS + 2 * N_KV_HEADS) * HEAD_DIM
    col_scale = jnp.concatenate([
        jnp.ones(((N_Q_HEADS + N_KV_HEADS) * HEAD_DIM,), f32),
        jnp.full((N_KV_HEADS * HEAD_DIM,), DEEPNORM_BETA, f32)])
    return {
        "x_prompt": nrm(ks[0], (BATCH, SEQ, D_MODEL), 1.0),
        "x_sample": nrm(ks[1], (DEC_BATCH, DEC_SEQ, D_MODEL), 1.0),
        "cache_k": nrm(ks[2], (N_ATTN_LAYERS, DEC_BATCH, WINDOW, N_KV_HEADS, HEAD_DIM), 1.0),
        "cache_v": nrm(ks[3], (N_ATTN_LAYERS, DEC_BATCH, WINDOW, N_KV_HEADS, HEAD_DIM), 1.0),
        "state_conv": nrm(ks[4], (N_CONV_LAYERS, DEC_BATCH, CONV_WIDTH - 1, CONV_DIM), 1.0),
        "w_qkv": nrm(ks[5], (N_ATTN_LAYERS, D_MODEL, qkv_cols), D_MODEL ** -0.5) * col_scale,
        "b_qkv": nrm(ks[6], (N_ATTN_LAYERS, qkv_cols), 0.02),
        "w_o": nrm(ks[7], (N_ATTN_LAYERS, N_Q_HEADS * HEAD_DIM, D_MODEL), DEEPNORM_BETA * (N_Q_HEADS * HEAD_DIM) ** -0.5),
        "b_o": nrm(ks[8], (N_ATTN_LAYERS, D_MODEL), 0.02),
        "attn_sinks": nrm(ks[9], (N_ATTN_LAYERS, N_KV_HEADS, GQA_GROUP), 1.0),
        "w_conv_in": nrm(ks[10], (N_CONV_LAYERS, D_MODEL, 3 * CONV_DIM), D_MODEL ** -0.5),
        "conv_w": nrm(ks[11], (N_CONV_LAYERS, CONV_WIDTH, CONV_DIM), CONV_WIDTH ** -0.5),
        "w_conv_out": nrm(ks[12], (N_CONV_LAYERS, CONV_DIM, D_MODEL), DEEPNORM_BETA * CONV_DIM ** -0.5),
        "w_peer_q": nrm(ks[13], (DEPTH, D_MODEL, PEER_HEADS * PEER_QDIM), D_MODEL ** -0.5),
        "peer_sub_keys": nrm(ks[14], (DEPTH, PEER_HEADS, 2, PEER_KEYS, PEER_HALF), PEER_HALF ** -0.5),
        "peer_u": nrm(ks[15], (DEPTH, PEER_EXPERTS, D_MODEL), D_MODEL ** -0.5),
        "peer_v": nrm(ks[16], (DEPTH, PEER_EXPERTS, D_MODEL), DEEPNORM_BETA * PEER_HEADS ** -0.5),
        "ln_g": 1.0 + nrm(ks[17], (DEPTH, 2, D_MODEL), 0.02),
        "ln_b": nrm(ks[18], (DEPTH, 2, D_MODEL), 0.02),
    }


def reference(x_prompt, x_sample, cache_k, cache_v, state_conv, w_qkv, b_qkv, w_o, b_o,
              attn_sinks, w_conv_in, conv_w, w_conv_out, w_peer_q, peer_sub_keys,
              peer_u, peer_v, ln_g, ln_b):
    yp, ys = x_prompt, x_sample
    new_kp, new_vp, new_cp, new_ks, new_vs, new_cs = [], [], [], [], [], []
    zero_conv = jnp.zeros((x_prompt.shape[0], CONV_WIDTH - 1, CONV_DIM), x_prompt.dtype)
    for layer in range(DEPTH):
        j = layer // N_MIXERS
        if layer % N_MIXERS == 0:
            mp, kp, vp = swa_prompt(yp, w_qkv[j], b_qkv[j], w_o[j], b_o[j], attn_sinks[j])
            ms, k_s, v_s = swa_sample(ys, cache_k[j], cache_v[j], w_qkv[j], b_qkv[j], w_o[j], b_o[j], attn_sinks[j])
            new_kp.append(kp)
            new_vp.append(vp)
            new_ks.append(k_s)
            new_vs.append(v_s)
        else:
            mp, cp = short_conv(yp, zero_conv, w_conv_in[j], conv_w[j], w_conv_out[j])
            ms, cs = short_conv(ys, state_conv[j], w_conv_in[j], conv_w[j], w_conv_out[j])
            new_cp.append(cp)
            new_cs.append(cs)
        yp = layer_norm(DEEPNORM_ALPHA * yp + mp, ln_g[layer, 0], ln_b[layer, 0])
        ys = layer_norm(DEEPNORM_ALPHA * ys + ms, ln_g[layer, 0], ln_b[layer, 0])
        fp = peer(yp, w_peer_q[layer], peer_sub_keys[layer], peer_u[layer], peer_v[layer])
        fs = peer(ys, w_peer_q[layer], peer_sub_keys[layer], peer_u[layer], peer_v[layer])
        yp = layer_norm(DEEPNORM_ALPHA * yp + fp, ln_g[layer, 1], ln_b[layer, 1])
        ys = layer_norm(DEEPNORM_ALPHA * ys + fs, ln_g[layer, 1], ln_b[layer, 1])
    return (yp, ys, jnp.stack(new_kp), jnp.stack(new_vp), jnp.stack(new_cp),
            jnp.stack(new_ks), jnp.stack(new_vs), jnp.stack(new_cs))
```

```python
import types
import numpy as np
from contextlib import ExitStack
import concourse.bass as bass
import concourse.mybir as mybir
from concourse.bass_utils import run_bass_kernel_spmd

F32 = mybir.dt.float32
BF16 = mybir.dt.bfloat16
I32 = mybir.dt.int32
U32 = mybir.dt.uint32
AF = mybir.ActivationFunctionType
ALU = mybir.AluOpType
AX = mybir.AxisListType

D = 2048
NCH = 12
NPB = 11
SCH = 11
FIRST_OWN = 3
ALPHA = float(8 ** 0.25)
LN_EPS = 1e-5
NEG = -1e30
GROUPS = [[0, 1], [2, 3], [4, 5], [6, 7], [8, 9], [10], [11]]
ENGS = ("pe", "act", "dve", "pool", "sp")


def _freeze(fn):
    if fn.__closure__ is None:
        return fn
    cells = []
    for c in fn.__closure__:
        try:
            cells.append(types.CellType(c.cell_contents))
        except ValueError:
            cells.append(c)
    return types.FunctionType(fn.__code__, fn.__globals__, fn.__name__, fn.__defaults__, tuple(cells))


class Prog:
    def __init__(self, nc, ring=16):
        self.nc = nc
        self.ops = []
        self.last_write = {}
        self.readers = {}
        self.ring = ring
        self.pending_barrier = {}

    def op(self, eng, fn, reads=(), writes=(), dma=False):
        deps = set()
        for r in reads:
            if r in self.last_write:
                deps.add(self.last_write[r])
        for w in writes:
            if w in self.last_write:
                deps.add(self.last_write[w])
            for rd in self.readers.get(w, ()):
                deps.add(rd)
        if eng in self.pending_barrier:
            deps.update(self.pending_barrier.pop(eng))
        i = len(self.ops)
        self.ops.append(dict(eng=eng, fn=_freeze(fn), deps=sorted(deps), dma=dma, sig=None))
        for r in reads:
            self.readers.setdefault(r, []).append(i)
        for w in writes:
            self.last_write[w] = i
            self.readers[w] = []
        return i

    def dma(self, eng, fn, reads=(), writes=()):
        return self.op(eng, fn, reads, writes, dma=True)

    def barrier(self):
        last = {}
        dmas = []
        for i, o in enumerate(self.ops):
            if o["dma"]:
                dmas.append(i)
            else:
                last[o["eng"]] = i
        start = getattr(self, "_bar_start", 0)
        dd = [i for i in dmas if i >= start]
        self._bar_start = len(self.ops)
        deps = set(last.values()) | set(dd)
        for e in ENGS:
            s = set(self.pending_barrier.get(e, ()))
            self.pending_barrier[e] = s | deps

    def emit(self, stack):
        nc = self.nc
        ops = self.ops
        needed = [False] * len(ops)
        for o in ops:
            for d in o["deps"]:
                p = ops[d]
                if p["dma"]:
                    continue
                if p["eng"] == "pe" and o["eng"] == "pe" and not o["dma"]:
                    continue
                needed[d] = True
        esem = {e: stack.enter_context(nc.semaphore("s_" + e)) for e in ENGS}
        dsem = {e: [stack.enter_context(nc.semaphore("d_%s%d" % (e, k))) for k in range(self.ring)]
                for e in ("sp", "pool", "act")}
        cnt = {e: 0 for e in ENGS}
        dcnt = {e: 0 for e in dsem}
        dval = {e: [0] * self.ring for e in dsem}
        for i, o in enumerate(ops):
            if o["dma"]:
                e = o["eng"]
                k = dcnt[e] % self.ring
                dcnt[e] += 1
                o["prev"] = (dsem[e][k], dval[e][k])
                dval[e][k] += 16
                o["sig"] = (dsem[e][k], dval[e][k])
            elif needed[i]:
                cnt[o["eng"]] += 1
                o["sig"] = (esem[o["eng"]], cnt[o["eng"]])
        per = {e: [] for e in ENGS}
        for i, o in enumerate(ops):
            per[o["eng"]].append(i)
        final = []
        for e in dsem:
            for k in range(self.ring):
                if dval[e][k]:
                    final.append((dsem[e][k], dval[e][k]))

        def run(e, engh):
            waited = {}
            for i in per[e]:
                o = ops[i]
                ws = []
                if o["dma"] and o["prev"][1] > 0:
                    ws.append(o["prev"])
                for d in o["deps"]:
                    p = ops[d]
                    if p["sig"] is None:
                        continue
                    if (not p["dma"]) and p["eng"] == "pe" and e == "pe" and not o["dma"]:
                        continue
                    ws.append(p["sig"])
                best = {}
                for s, v in ws:
                    key = id(s)
                    if waited.get(key, 0) >= v:
                        continue
                    if key not in best or best[key][1] < v:
                        best[key] = (s, v)
                for key, (s, v) in best.items():
                    engh.wait_ge(s, v)
                    waited[key] = v
                ins = o["fn"](engh)
                if o["sig"] is not None:
                    ins.then_inc(o["sig"][0], 16 if o["dma"] else 1)
            if e == "sp":
                for s, v in final:
                    engh.wait_ge(s, v)

        block = stack.enter_context(nc.Block())

        @block.tensor
        def _(t):
            run("pe", t)

        @block.scalar
        def _(t):
            run("act", t)

        @block.vector
        def _(t):
            run("dve", t)

        @block.gpsimd
        def _(t):
            run("pool", t)

        @block.sync
        def _(t):
            run("sp", t)


def build_program(n_layers=4, debug=False):
    nc = bass.Bass("TRN2", target_bir_lowering=False)

    def din(name, shape, dtype=F32):
        return nc.dram_tensor(name, shape, dtype, kind="ExternalInput").ap()

    def dout(name, shape, dtype=F32):
        return nc.dram_tensor(name, shape, dtype, kind="ExternalOutput").ap()

    xin = din("xin", [NCH, 128, D])
    cs_d = din("cs", [128, NCH, 64])
    masks_d = din("masks", [128, 2, 256])
    rowmask_d = din("rowmask", [128, 4])
    hv_d = din("hv", [128, 1])
    ck_d = din("ck", [2, 4, 128, 512])
    cv_d = din("cv", [2, 4, 128, 512])
    stc_d = din("stc", [2, 8, D])
    wqkv_d = din("w_qkv", [2, D, 3072])
    bqkv_d = din("b_qkv", [2, 3072])
    wo_d = din("w_o", [2, D, D])
    bo_d = din("b_o", [2, D])
    sinks_d = din("attn_sinks", [2, 32])
    wci_d = din("w_conv_in", [2, D, 6144])
    cwT_d = din("conv_wT", [2, 128, 16, 3])
    wco_d = din("w_conv_out", [2, D, D])
    wpq_d = din("w_peer_q", [4, D, D])
    psk_d = din("peer_sub_keys", [4, 16, 128, 128])
    pu_d = din("peer_u", [4, 16384, D])
    pv_d = din("peer_v", [4, 16384, D])
    lng_d = din("ln_g", [4, 2, D])
    lnb_d = din("ln_b", [4, 2, D])

    y_o = dout("y", [9, 128, D])
    kp_o = dout("kp", [2, 128, 512])
    vp_o = dout("vp", [2, 128, 512])
    cp_o = dout("cp", [2, 2, D])
    ks_o = dout("ks", [2, 4, 128, 512])
    vs_o = dout("vs", [2, 4, 128, 512])
    cso_o = dout("cso", [2, 4, 2, D])
    dbg_o = dout("dbg", [9, 128, D]) if debug else None
    dbgO = dout("dbgO", [128, D]) if debug else None
    dbgE = dout("dbgE", [128, 128], I32) if debug else None
    dbgG = dout("dbgG", [128, 128]) if debug else None
    dbgH = dout("dbgH", [128, 128]) if debug else None
    dbgS = dout("dbgS", [128, 2048]) if debug else None
    dbgSP = dout("dbgSP", [128, 1024]) if debug else None
    dbgP = dout("dbgP", [128, 1024]) if debug else None
    dbgSM = dout("dbgSM", [128, 64]) if debug else None
    dbgU = dout("dbgU", [128, 2048], BF16) if debug else None

    uv_l = [nc.dram_tensor("uv_scratch%d" % l, [16384, 2, D], BF16, kind="Internal").ap() for l in range(4)]
    PEER_MIN_CHUNK = [1, 1, 2, 3]
    wqkv_b = nc.dram_tensor("wqkv_b", [2, 12, 128, 4096], BF16, kind="Internal").ap()
    wo_b = nc.dram_tensor("wo_b", [2, 8, 128, 4096], BF16, kind="Internal").ap()
    wco_b = nc.dram_tensor("wco_b", [2, 8, 128, 4096], BF16, kind="Internal").ap()
    wpq_b = nc.dram_tensor("wpq_b", [4, 8, 128, 4096], BF16, kind="Internal").ap()
    wci_b = nc.dram_tensor("wci_b", [2, 16, 128, 6144], BF16, kind="Internal").ap()

    st = ExitStack()
    with st:
        def sb(name, shape, dt=F32):
            return st.enter_context(nc.sbuf_tensor(name, shape, dt))

        xres = sb("xres", [128, NCH, D])
        identf = sb("identf", [128, 128])
        identb = sb("identb", [128, 128], BF16)
        masks = sb("masks_sb", [128, 3, 256])
        rowmask = sb("rowmask_sb", [128, 4])
        hv = sb("hv_sb", [128, 1])
        negbig = sb("negbig", [128, 1])
        cs = sb("cs_sb", [128, NCH, 64])
        iota16 = sb("iota16", [128, 16])
        xT = sb("xT", [128, 16, 256], BF16)
        wt = [sb("wt%d" % i, [128, 16, 256], BF16) for i in range(2)]
        lnp = sb("lnp", [128, 2, D])
        stats = sb("stats", [128, 4, 6])
        mv = sb("mv", [128, 2])
        rstd = sb("rstd", [128, 1])
        epsb = sb("epsb", [128, 1])
        ARENA = 16560
        arena = sb("arena", [128, ARENA])
        pb = [st.enter_context(nc.psum_tensor("pb%d" % i, [128, 512], F32)) for i in range(8)]

        P = Prog(nc)
        cur = [0]

        def a_reset():
            cur[0] = 0

        def a_f32(n):
            a = cur[0]
            cur[0] += n
            assert cur[0] <= ARENA, cur[0]
            return arena[:, a:a + n]

        def a_bf16(n):
            n32 = (n + 1) // 2
            return a_f32(n32).bitcast(BF16)

        wq = [0]

        P.dma("sp", lambda e: e.dma_start(out=xres[:], in_=xin.rearrange("c p d -> p c d")), writes=["xres%d" % c for c in range(NCH)])
        P.dma("sp", lambda e: e.dma_start(out=masks[:, 0:2, :], in_=masks_d), writes=["masks"])
        P.dma("sp", lambda e: e.dma_start(out=rowmask[:], in_=rowmask_d), writes=["rowmask"])
        P.dma("sp", lambda e: e.dma_start(out=hv[:], in_=hv_d), writes=["hv"])
        P.dma("sp", lambda e: e.dma_start(out=cs[:], in_=cs_d), writes=["cs"])
        P.op("pool", lambda e: e.memset(epsb[:], LN_EPS), writes=["epsb"])
        P.op("pool", lambda e: e.iota(identf[:], pattern=[[1, 128]], base=0, channel_multiplier=-1,
                                      allow_small_or_imprecise_dtypes=True), writes=["identf"])
        P.op("pool", lambda e: e.iota(iota16[:], pattern=[[1, 16]], base=0, channel_multiplier=0,
                                      allow_small_or_imprecise_dtypes=True), writes=["iota16"])
        P.op("dve", lambda e: e.tensor_scalar(out=identb[:], in0=identf[:], scalar1=0.0, scalar2=None, op0=ALU.is_equal),
             reads=["identf"], writes=["identb"])
        P.op("dve", lambda e: e.tensor_scalar(out=identf[:], in0=identf[:], scalar1=0.0, scalar2=None, op0=ALU.is_equal),
             reads=["identf"], writes=["identf"])
        P.op("dve", lambda e: e.tensor_scalar(out=negbig[:], in0=hv[:], scalar1=-1.0, scalar2=1e30, op0=ALU.add, op1=ALU.mult),
             reads=["hv"], writes=["negbig"])
        P.op("dve", lambda e: e.tensor_copy(out=masks[:, 2, :], in_=masks[:, 0, :]), reads=["masks"], writes=["masks"])
        P.op("dve", lambda e: e.tensor_scalar(out=masks[:, 2, 0:128], in0=masks[:, 2, 0:128], scalar1=negbig[:, 0:1], scalar2=None, op0=ALU.add),
             reads=["masks", "negbig"], writes=["masks"])

        def xr(c):
            return "xres%d" % c

        def table_conv_ops(l):
            ops_ = []
            for src, w, nm in ((pu_d, 0, "ub%d" % l), (pv_d, 1, "vb%d" % l)):
                for r in range(8):
                    def go(src=src, w=w, nm=nm, r=r, l=l):
                        P.dma("pool", lambda e: e.dma_start(out=uv_l[l][r * 2048:(r + 1) * 2048, w, :],
                                                            in_=src[l, r * 2048:(r + 1) * 2048, :]), writes=[nm + "_%d" % r])
                    ops_.append(go)
            return ops_

        def conv_w256(src2d, dst_tiles, nm, ntile):
            for ct in range(ntile):
                P.dma("pool", lambda e, ct=ct: e.dma_start(out=dst_tiles[ct].rearrange("p (kc n) -> p kc n", kc=16),
                                                           in_=src2d[:, ct * 256:(ct + 1) * 256].rearrange("(kc p) n -> p kc n", p=128)),
                      writes=["%s_%d" % (nm, ct)])

        def conv_wci(j):
            for cc in range(16):
                src = wci_d[j].rearrange("(kc p) (a c m) -> p kc a c m", p=128, a=3, m=128)[:, :, :, cc, :]
                dst = wci_b[j, cc].rearrange("p (kc a m) -> p kc a m", kc=16, a=3)
                for a in range(3):
                    P.dma("pool", lambda e, src=src, dst=dst, a=a: e.dma_start(out=dst[:, :, a, :], in_=src[:, :, a, :]), writes=["cwci%d_%d_%d" % (j, cc, a)])

        conv_w256(wqkv_d[0], wqkv_b[0], "cqkv0", 12)
        conv_w256(wo_d[0], wo_b[0], "co0", 8)
        conv_w256(wpq_d[0], wpq_b[0], "cpq0", 8)
        for go in table_conv_ops(0):
            go()
        if n_layers > 1:
            conv_wci(0)
            conv_w256(wco_d[0], wco_b[0], "cco0", 8)
            conv_w256(wpq_d[1], wpq_b[1], "cpq1", 8)
        if n_layers > 2:
            conv_w256(wqkv_d[1], wqkv_b[1], "cqkv1", 12)
            conv_w256(wo_d[1], wo_b[1], "co1", 8)
            conv_w256(wpq_d[2], wpq_b[2], "cpq2", 8)
        if n_layers > 3:
            conv_wci(1)
            conv_w256(wco_d[1], wco_b[1], "cco1", 8)
            conv_w256(wpq_d[3], wpq_b[3], "cpq3", 8)

        xT_alias = []

        def build_xT(grp):
            k = 0
            for gi, c in enumerate(grp):
                for q in range(4):
                    bank = 6 + (k % 2)
                    k += 1
                    for r in range(4):
                        kc = q * 4 + r
                        P.op("pe", lambda e, c=c, kc=kc, bank=bank, r=r: e.transpose(
                            out=pb[bank][:, r * 128:(r + 1) * 128], in_=xres[:, c, kc * 128:(kc + 1) * 128], identity=identf[:]),
                            reads=[xr(c), "identf"], writes=["pb%d" % bank])
                    dst = xT[:, q * 4:(q + 1) * 4, gi * 128:(gi + 1) * 128]
                    src = pb[bank][:, :].rearrange("p (r t) -> p r t", r=4)
                    if q % 2 == 0:
                        P.op("act", lambda e, dst=dst, src=src: e.copy(out=dst, in_=src), reads=["pb%d" % bank], writes=["xT"] + xT_alias)
                    else:
                        P.op("dve", lambda e, dst=dst, src=src: e.tensor_copy(out=dst, in_=src), reads=["pb%d" % bank], writes=["xT"] + xT_alias)

        wt_alias = {0: [], 1: []}

        def load_wt(tile_ap, res):
            i = wq[0] % 2
            wq[0] += 1
            P.dma("sp", lambda e, i=i, tile_ap=tile_ap: e.dma_start(out=wt[i][:].rearrange("p k n -> p (k n)"), in_=tile_ap),
                  reads=[res], writes=["wt%d" % i] + wt_alias[i])
            return i

        def load_bcast(dst, src_row, res):
            P.dma("sp", lambda e: e.dma_start(out=dst, in_=src_row.partition_broadcast(128)), writes=[res])

        def layer_norm(c, layer, sub):
            for q in range(4):
                P.op("dve", lambda e, q=q: e.bn_stats(out=stats[:, q, :], in_=xres[:, c, q * 512:(q + 1) * 512]),
                     reads=[xr(c)], writes=["stats"])
            P.op("dve", lambda e: e.bn_aggr(out=mv[:], in_=stats[:].rearrange("p a b -> p (a b)")), reads=["stats"], writes=["mv"])
            P.op("act", lambda e: e.activation(out=rstd[:], in_=mv[:, 1:2], func=AF.Ln, bias=epsb[:, 0:1], scale=1.0),
                 reads=["mv", "epsb"], writes=["rstd"])
            P.op("act", lambda e: e.activation(out=rstd[:], in_=rstd[:], func=AF.Exp, scale=-0.5), reads=["rstd"], writes=["rstd"])
            P.op("dve", lambda e: e.tensor_scalar(out=xres[:, c, :], in0=xres[:, c, :], scalar1=mv[:, 0:1], scalar2=rstd[:, 0:1],
                                                  op0=ALU.subtract, op1=ALU.mult), reads=[xr(c), "mv", "rstd"], writes=[xr(c)])
            P.op("pool", lambda e: e.tensor_tensor(out=xres[:, c, :], in0=xres[:, c, :], in1=lnp[:, 0, :], op=ALU.mult),
                 reads=[xr(c), "lnp"], writes=[xr(c)])
            P.op("pool", lambda e: e.tensor_tensor(out=xres[:, c, :], in0=xres[:, c, :], in1=lnp[:, 1, :], op=ALU.add),
                 reads=[xr(c), "lnp"], writes=[xr(c)])

        def load_ln(layer, sub, extra=()):
            P.dma("sp", lambda e: e.dma_start(out=lnp[:, 0, :], in_=lng_d[layer, sub].partition_broadcast(128)), writes=["lnp"] + list(extra))
            P.dma("sp", lambda e: e.dma_start(out=lnp[:, 1, :], in_=lnb_d[layer, sub].partition_broadcast(128)), writes=["lnp"] + list(extra))

        def out_proj(grp, lhsT_of, lhs_res, w_ap, bias_row, btile, skip=()):
            if all(c in skip for c in grp):
                return
            k = 0
            for ct in range(8):
                wi = load_wt(w_ap[0][ct], "%s_%d" % (w_ap[1], ct))
                if bias_row is not None:
                    bi = ct % 2
                    load_bcast(btile[bi], bias_row[ct * 256:(ct + 1) * 256], "btile%d" % bi)
                for gi, c in enumerate(grp):
                    if c in skip:
                        continue
                    bank = 4 + (k % 2)
                    k += 1
                    for kc in range(16):
                        P.op("pe", lambda e, gi=gi, kc=kc, bank=bank, wi=wi: e.matmul(
                            pb[bank][:, 0:256], lhsT=lhsT_of(gi, kc), rhs=wt[wi][:, kc, :], start=(kc == 0), stop=(kc == 15)),
                            reads=[lhs_res, "wt%d" % wi], writes=["pb%d" % bank])
                    xs = xres[:, c, ct * 256:(ct + 1) * 256]
                    P.op("dve", lambda e, xs=xs, bank=bank: e.scalar_tensor_tensor(
                        out=xs, in0=xs, scalar=ALPHA, in1=pb[bank][:, 0:256], op0=ALU.mult, op1=ALU.add),
                        reads=[xr(c), "pb%d" % bank], writes=[xr(c)])
                    if bias_row is not None:
                        P.op("pool", lambda e, xs=xs, bi=bi: e.tensor_tensor(out=xs, in0=xs, in1=btile[bi], op=ALU.add),
                             reads=[xr(c), "btile%d" % bi], writes=[xr(c)])

        def attention_layer(layer):
            j = layer // 2
            P.barrier()
            a_reset()
            btile = [a_f32(256) for _ in range(2)]
            tmp = [a_f32(256) for _ in range(2)]
            qkr = [a_f32(256) for _ in range(2)]
            rt = [a_f32(128).rearrange("p (h d) -> p h d", d=32) for _ in range(4)]
            QT = a_bf16(2 * 32 * 128)[0:64, :].rearrange("p (g h t) -> p g h t", g=2, h=32)
            KT = [a_bf16(8 * 128)[0:64, :].rearrange("p (h t) -> p h t", h=8) for _ in range(4)]
            Vb = [a_bf16(512) for _ in range(4)]
            SP = [a_f32(1024).rearrange("p (h s) -> p h s", h=4)] * 2
            PT = [a_bf16(8 * 128).rearrange("p (a q) -> p a q", a=8)] * 2
            Oall = a_f32(D)
            OT = a_bf16(16 * 256).rearrange("p (k t) -> p k t", k=16)
            vf = a_f32(512)
            ckf = a_f32(512)
            cvf = vf
            qkb = [a_bf16(256) for _ in range(2)]
            ckb = a_bf16(512)
            pbb = [pb[i][:, :].bitcast(BF16) for i in range(8)]
            sinkb = a_f32(32)
            sm = a_f32(64).rearrange("p (a b) -> p a b", b=4)
            load_bcast(sinkb, sinks_d[j], "sinkb")
            load_ln(layer, 0)
            P.op("pool", lambda e: e.memset(KT[3][:], 0.0), writes=["KT3"])
            P.op("pool", lambda e: e.memset(Vb[3][:], 0.0), writes=["Vb3"])
            prev_slot = [3]
            slot_ctr = [0]

            def softmax_pv(c, gi, kt_prev, v_prev, kt_cur, v_cur, mask_idx, rscale, accumulate, tag):
                for kh in range(8):
                    sb_i = kh % 2
                    S = [pb[0 + 2 * sb_i], pb[1 + 2 * sb_i]]
                    for g4 in range(4):
                        h = kh * 4 + g4
                        bank = S[g4 // 2]
                        off = (g4 % 2) * 256
                        P.op("pe", lambda e, h=h, bank=bank, off=off: e.matmul(
                            bank[:, off:off + 128], lhsT=QT[:, gi, h, :], rhs=kt_prev[0][:, kh, :], start=True, stop=True),
                            reads=["QT", kt_prev[1]], writes=["S%d" % sb_i])
                        P.op("pe", lambda e, h=h, bank=bank, off=off: e.matmul(
                            bank[:, off + 128:off + 256], lhsT=QT[:, gi, h, :], rhs=kt_cur[0][:, kh, :], start=True, stop=True),
                            reads=["QT", kt_cur[1]], writes=["S%d" % sb_i])
                    sp = SP[sb_i]
                    spr = "SP0"
                    mk = masks[:, mask_idx, :]
                    for half in range(2):
                        P.op("dve", lambda e, half=half, sp=sp, S=S, mk=mk: e.scalar_tensor_tensor(
                            out=sp[:, 2 * half:2 * half + 2, :], in0=S[half][:, :].rearrange("p (h s) -> p h s", h=2), scalar=0.125,
                            in1=mk.unsqueeze(1).to_broadcast([128, 2, 256]), op0=ALU.mult, op1=ALU.add),
                            reads=["S%d" % sb_i, "masks"], writes=[spr])
                    st_ = "sm"
                    if debug and layer == 0 and c == NPB - 1 and kh == 0:
                        P.dma("sp", lambda e, sp=sp: e.dma_start(out=dbgSP, in_=sp.rearrange("p h s -> p (h s)")), reads=[spr])
                    P.op("dve", lambda e, sp=sp: e.tensor_reduce(out=sm[:, 0, :], in_=sp, axis=AX.X, op=ALU.max), reads=[spr], writes=[st_])
                    P.op("dve", lambda e, kh=kh: e.tensor_tensor(out=sm[:, 1, :], in0=sm[:, 0, :], in1=sinkb[:, kh * 4:kh * 4 + 4], op=ALU.max),
                         reads=[st_, "sinkb"], writes=[st_])
                    P.op("dve", lambda e: e.tensor_scalar(out=sm[:, 2, :], in0=sm[:, 1, :], scalar1=-1.0, scalar2=None, op0=ALU.mult),
                         reads=[st_], writes=[st_])
                    for g4 in range(4):
                        P.op("act", lambda e, g4=g4, sp=sp: e.activation(out=sp[:, g4, :], in_=sp[:, g4, :], func=AF.Exp,
                                                                         bias=sm[:, 2, g4:g4 + 1], scale=1.0, accum_out=sm[:, 3, g4:g4 + 1]),
                             reads=[spr, st_], writes=[spr, "smsum"])
                    P.op("dve", lambda e, kh=kh: e.tensor_tensor(out=sm[:, 4, :], in0=sinkb[:, kh * 4:kh * 4 + 4], in1=sm[:, 1, :], op=ALU.subtract),
                         reads=[st_, "sinkb"], writes=["sm4"])
                    P.op("act", lambda e: e.activation(out=sm[:, 4, :], in_=sm[:, 4, :], func=AF.Exp), reads=["sm4"], writes=["sm4"])
                    P.op("dve", lambda e: e.tensor_tensor(out=sm[:, 5, :], in0=sm[:, 4, :], in1=sm[:, 3, :], op=ALU.add),
                         reads=["sm4", "smsum"], writes=["sm5"])
                    P.op("dve", lambda e: e.reciprocal(out=sm[:, 6, :], in_=sm[:, 5, :]), reads=["sm5"], writes=["sm6"])
                    if rscale is not None:
                        P.op("dve", lambda e: e.tensor_scalar(out=sm[:, 6, :], in0=sm[:, 6, :], scalar1=rscale, scalar2=None, op0=ALU.mult),
                             reads=["sm6", "rowmask"], writes=["sm6"])
                    if debug and layer == 0 and c == NPB - 1 and kh == 0:
                        P.dma("sp", lambda e, sp=sp: e.dma_start(out=dbgP, in_=sp.rearrange("p h s -> p (h s)")), reads=[spr])
                        P.dma("sp", lambda e: e.dma_start(out=dbgSM, in_=sm.rearrange("p a b -> p (a b)")), reads=["sm", "smsum", "sm4", "sm5", "sm6"])
                    pt = PT[sb_i]
                    ptr = "PT0"
                    for g4 in range(4):
                        bank = pb[4 + (g4 // 2)]
                        for kb in range(2):
                            off = ((g4 % 2) * 2 + kb) * 128
                            P.op("pe", lambda e, g4=g4, kb=kb, bank=bank, off=off, sp=sp: e.transpose(
                                out=bank[:, off:off + 128], in_=sp[:, g4, kb * 128:(kb + 1) * 128], identity=identf[:]),
                                reads=[spr, "identf"], writes=["pb%d" % (4 + g4 // 2)])
                    P.op("act", lambda e, pt=pt: e.copy(out=pt[:, 0:4, :], in_=pb[4][:, :].rearrange("p (a q) -> p a q", a=4)),
                         reads=["pb4"], writes=[ptr])
                    P.op("dve", lambda e, pt=pt: e.tensor_copy(out=pt[:, 4:8, :], in_=pb[5][:, :].rearrange("p (a q) -> p a q", a=4)),
                         reads=["pb5"], writes=[ptr])
                    for g4 in range(4):
                        P.op("pe", lambda e, g4=g4, pt=pt: e.matmul(pb[6][:, g4 * 64:(g4 + 1) * 64], lhsT=pt[:, g4 * 2, :],
                                                                  rhs=v_prev[0][:, kh * 64:(kh + 1) * 64], start=True, stop=False),
                             reads=[ptr, v_prev[1]], writes=["pb6"])
                        P.op("pe", lambda e, g4=g4, pt=pt: e.matmul(pb[6][:, g4 * 64:(g4 + 1) * 64], lhsT=pt[:, g4 * 2 + 1, :],
                                                                  rhs=v_cur[0][:, kh * 64:(kh + 1) * 64], start=False, stop=True),
                             reads=[ptr, v_cur[1]], writes=["pb6"])
                    osl = Oall[:, kh * 256:(kh + 1) * 256].rearrange("p (g d) -> p g d", g=4)
                    opv = pb[6][:, 0:256].rearrange("p (g d) -> p g d", g=4)
                    rb = sm[:, 6, :].unsqueeze(2).to_broadcast([128, 4, 64])
                    if not accumulate:
                        P.op("dve", lambda e, osl=osl, opv=opv, rb=rb: e.tensor_tensor(out=osl, in0=opv, in1=rb, op=ALU.mult),
                             reads=["pb6", "sm6"], writes=["Oall"])
                    else:
                        t4 = tmp[0].rearrange("p (g d) -> p g d", g=4)
                        P.op("dve", lambda e, t4=t4, opv=opv, rb=rb: e.tensor_tensor(out=t4, in0=opv, in1=rb, op=ALU.mult),
                             reads=["pb6", "sm6"], writes=["tmp0"])
                        P.op("dve", lambda e, t4=t4, osl=osl: e.tensor_tensor(out=osl, in0=osl, in1=t4, op=ALU.add),
                             reads=["tmp0", "Oall"], writes=["Oall"])

            kv_only = {0} if layer == 0 else {1}
            groups_l = GROUPS if layer == 0 else [[1]] + GROUPS[1:]
            for grp in groups_l:
                build_xT(grp)
                cur_slots = []
                for gi, c in enumerate(grp):
                    s_ = slot_ctr[0] % 3
                    slot_ctr[0] += 1
                    cur_slots.append(s_)
                k = 0
                for ct in range(12):
                    wi = load_wt(wqkv_b[j, ct], "cqkv%d_%d" % (j, ct))
                    bi = ct % 2
                    load_bcast(btile[bi], bqkv_d[j, ct * 256:(ct + 1) * 256], "btile%d" % bi)
                    for gi, c in enumerate(grp):
                        bank = 4 + (k % 2)
                        k += 1
                        for kc in range(16):
                            P.op("pe", lambda e, gi=gi, kc=kc, bank=bank, wi=wi: e.matmul(
                                pb[bank][:, 0:256], lhsT=xT[:, kc, gi * 128:(gi + 1) * 128], rhs=wt[wi][:, kc, :],
                                start=(kc == 0), stop=(kc == 15)), reads=["xT", "wt%d" % wi], writes=["pb%d" % bank])
                        ti = k % 2
                        if ct < 10:
                            P.op("dve", lambda e, ti=ti, bank=bank, bi=bi: e.tensor_tensor(out=tmp[ti], in0=pb[bank][:, 0:256], in1=btile[bi], op=ALU.add),
                                 reads=["pb%d" % bank, "btile%d" % bi], writes=["tmp%d" % ti])
                            tv = tmp[ti].rearrange("p (h two d) -> p h two d", h=4, two=2)
                            x1 = tv[:, :, 0, :]
                            x2 = tv[:, :, 1, :]
                            cosb = cs[:, c, 0:32].unsqueeze(1).to_broadcast([128, 4, 32])
                            sinb = cs[:, c, 32:64].unsqueeze(1).to_broadcast([128, 4, 32])
                            qv = qkr[ti].rearrange("p (h two d) -> p h two d", h=4, two=2)
                            rr = ["rt0", "rt1", "rt2", "rt3"]
                            P.op("pool", lambda e, x1=x1, cosb=cosb: e.tensor_tensor(out=rt[0], in0=x1, in1=cosb, op=ALU.mult), reads=["tmp%d" % ti, "cs"], writes=[rr[0]])
                            P.op("pool", lambda e, x2=x2, sinb=sinb: e.tensor_tensor(out=rt[1], in0=x2, in1=sinb, op=ALU.mult), reads=["tmp%d" % ti, "cs"], writes=[rr[1]])
                            P.op("dve", lambda e, x2=x2, cosb=cosb: e.tensor_tensor(out=rt[2], in0=x2, in1=cosb, op=ALU.mult), reads=["tmp%d" % ti, "cs"], writes=[rr[2]])
                            P.op("dve", lambda e, x1=x1, sinb=sinb: e.tensor_tensor(out=rt[3], in0=x1, in1=sinb, op=ALU.mult), reads=["tmp%d" % ti, "cs"], writes=[rr[3]])
                            P.op("pool", lambda e, qv=qv: e.tensor_tensor(out=qv[:, :, 0, :], in0=rt[0], in1=rt[1], op=ALU.subtract),
                                 reads=[rr[0], rr[1]], writes=["qkr%d" % ti])
                            P.op("dve", lambda e, qv=qv: e.tensor_tensor(out=qv[:, :, 1, :], in0=rt[2], in1=rt[3], op=ALU.add),
                                 reads=[rr[2], rr[3]], writes=["qkr%d" % ti])
                            tb = 6 + (k % 2)
                            P.op("act", lambda e, ti=ti: e.copy(out=qkb[ti], in_=qkr[ti]), reads=["qkr%d" % ti], writes=["qkb%d" % ti])
                            for hh in range(4):
                                P.op("pe", lambda e, hh=hh, tb=tb, ti=ti: e.transpose(
                                    out=pbb[tb][0:64, hh * 128:(hh + 1) * 128], in_=qkb[ti][:, hh * 64:(hh + 1) * 64], identity=identb[:]),
                                    reads=["qkb%d" % ti, "identb"], writes=["pb%d" % tb])
                            src = pbb[tb][0:64, 0:512].rearrange("p (h t) -> p h t", h=4)
                            if ct < 8:
                                P.op("act", lambda e, src=src, gi=gi, ct=ct: e.copy(out=QT[:, gi, ct * 4:(ct + 1) * 4, :], in_=src),
                                     reads=["pb%d" % tb], writes=["QT"])
                            else:
                                s_ = cur_slots[gi]
                                hb = (ct - 8) * 4
                                P.op("act", lambda e, src=src, s_=s_, hb=hb: e.copy(out=KT[s_][:, hb:hb + 4, :], in_=src),
                                     reads=["pb%d" % tb], writes=["KT%d" % s_])
                                if c == NPB - 1:
                                    P.dma("sp", lambda e, ti=ti, hb=hb: e.dma_start(out=kp_o[j, :, hb * 64:(hb + 4) * 64], in_=qkr[ti]),
                                          reads=["qkr%d" % ti])
                                if c == SCH:
                                    for s in range(4):
                                        P.dma("sp", lambda e, ti=ti, hb=hb, s=s: e.dma_start(
                                            out=ks_o[j, s, 124:128, hb * 64:(hb + 4) * 64], in_=qkr[ti][4 * s:4 * s + 4, :]),
                                            reads=["qkr%d" % ti])
                        else:
                            s_ = cur_slots[gi]
                            vo = (ct - 10) * 256
                            P.op("dve", lambda e, bank=bank, bi=bi, vo=vo: e.tensor_tensor(out=vf[:, vo:vo + 256], in0=pb[bank][:, 0:256], in1=btile[bi], op=ALU.add),
                                 reads=["pb%d" % bank, "btile%d" % bi], writes=["vf"])
                            P.op("act", lambda e, s_=s_, vo=vo: e.copy(out=Vb[s_][:, vo:vo + 256], in_=vf[:, vo:vo + 256]),
                                 reads=["vf"], writes=["Vb%d" % s_])
                            if c == NPB - 1:
                                P.dma("sp", lambda e, vo=vo: e.dma_start(out=vp_o[j, :, vo:vo + 256], in_=vf[:, vo:vo + 256]), reads=["vf"])
                            if c == SCH:
                                for s in range(4):
                                    P.dma("sp", lambda e, vo=vo, s=s: e.dma_start(out=vs_o[j, s, 124:128, vo:vo + 256], in_=vf[4 * s:4 * s + 4, vo:vo + 256]),
                                          reads=["vf"])
                for gi, c in enumerate(grp):
                    s_ = cur_slots[gi]
                    ktc = (KT[s_], "KT%d" % s_)
                    vc = (Vb[s_], "Vb%d" % s_)
                    if c in kv_only:
                        prev_slot[0] = s_
                        continue
                    if c != SCH:
                        ps_ = prev_slot[0]
                        ktp = (KT[ps_], "KT%d" % ps_)
                        vp_ = (Vb[ps_], "Vb%d" % ps_)
                        softmax_pv(c, gi, ktp, vp_, ktc, vc, 2 if c == FIRST_OWN else 0, None, False, "p")
                        prev_slot[0] = s_
                    else:
                        for s in range(4):
                            P.dma("sp", lambda e, s=s: e.dma_start(out=ks_o[j, s, 0:124, :], in_=ck_d[j, s, 4:128, :]))
                            P.dma("sp", lambda e, s=s: e.dma_start(out=vs_o[j, s, 0:124, :], in_=cv_d[j, s, 4:128, :]))
                            P.dma("sp", lambda e, s=s: e.dma_start(out=ckf, in_=ck_d[j, s]), writes=["ckf"])
                            P.dma("sp", lambda e, s=s: e.dma_start(out=cvf, in_=cv_d[j, s]), writes=["vf"])
                            P.op("act", lambda e: e.copy(out=ckb, in_=ckf), reads=["ckf"], writes=["ckb"])
                            for half in range(2):
                                for hh in range(4):
                                    P.op("pe", lambda e, hh=hh, half=half: e.transpose(
                                        out=pbb[7][0:64, hh * 128:(hh + 1) * 128], in_=ckb[:, (half * 4 + hh) * 64:(half * 4 + hh + 1) * 64], identity=identb[:]),
                                        reads=["ckb", "identb"], writes=["pb7"])
                                P.op("act", lambda e, half=half: e.copy(out=KT[3][:, half * 4:half * 4 + 4, :],
                                                                      in_=pbb[7][0:64, 0:512].rearrange("p (h t) -> p h t", h=4)),
                                     reads=["pb7"], writes=["KT3"])
                            P.op("act", lambda e: e.copy(out=Vb[3][:], in_=cvf), reads=["vf"], writes=["Vb3"])
                            softmax_pv(c, gi, (KT[3], "KT3"), (Vb[3], "Vb3"), ktc, vc, 1, rowmask[:, s:s + 1], s > 0, "s")
                    if debug and layer == 0 and c == NPB - 1:
                        P.dma("sp", lambda e: e.dma_start(out=dbgO, in_=Oall), reads=["Oall"])
                    for q in range(4):
                        bank = 4 + (q % 2)
                        for r in range(4):
                            kc = q * 4 + r
                            P.op("pe", lambda e, kc=kc, bank=bank, r=r: e.transpose(
                                out=pb[bank][:, r * 128:(r + 1) * 128], in_=Oall[:, kc * 128:(kc + 1) * 128], identity=identf[:]),
                                reads=["Oall", "identf"], writes=["pb%d" % bank])
                        P.op("act", lambda e, q=q, bank=bank, gi=gi: e.copy(out=OT[:, q * 4:(q + 1) * 4, gi * 128:(gi + 1) * 128],
                                                                          in_=pb[bank][:, :].rearrange("p (r t) -> p r t", r=4)),
                             reads=["pb%d" % bank], writes=["OT"])
                out_proj(grp, lambda gi, kc: OT[:, kc, gi * 128:(gi + 1) * 128], "OT", (wo_b[j], "co%d" % j), bo_d[j], btile, skip=kv_only)
                for c in grp:
                    if c not in kv_only:
                        layer_norm(c, layer, 0)

        def conv_layer(layer):
            j = layer // 2
            P.barrier()
            a_reset()
            w3 = [a_bf16(16 * 3 * 128).rearrange("p (k a m) -> p k a m", k=16, a=3) for _ in range(2)]
            hsb = a_f32(256)
            ubuf = [a_f32(260) for _ in range(2)]
            ctmp = [a_f32(256) for _ in range(2)]
            zT = a_bf16(16 * 256).rearrange("p (k t) -> p k t", k=16)
            ucarry = a_f32(32).rearrange("p (k t) -> p k t", t=2)
            cw = a_f32(48).rearrange("p (k t) -> p k t", t=3)
            stT = a_f32(128).rearrange("p (k s) -> p k s", s=8)
            strow = a_f32(D)
            ubs = [a_f32(24).rearrange("p (s t) -> p s t", t=6) for _ in range(2)]
            usout = a_f32(128).rearrange("p (k s t) -> p k s t", s=4, t=2)
            utr = a_f32(128)
            utro = a_f32(128)
            load_ln(layer, 0)
            P.dma("sp", lambda e: e.dma_start(out=cw, in_=cwT_d[j]), writes=["cw"])
            P.op("pool", lambda e: e.memset(ucarry, 0.0), writes=["ucarry"])
            P.dma("sp", lambda e: e.dma_start(out=strow[0:8, :], in_=stc_d[j]), writes=["strow"])
            for q in range(4):
                for r in range(4):
                    kc = q * 4 + r
                    P.op("pe", lambda e, kc=kc, r=r: e.transpose(out=pb[7][:, r * 8:(r + 1) * 8], in_=strow[0:8, kc * 128:(kc + 1) * 128],
                                                                identity=identf[0:8, 0:8]), reads=["strow", "identf"], writes=["pb7"])
                P.op("act", lambda e, q=q: e.copy(out=stT[:, q * 4:(q + 1) * 4, :], in_=pb[7][:, 0:32].rearrange("p (r s) -> p r s", r=4)),
                     reads=["pb7"], writes=["stT"])
            wk = [0]
            u_only = set() if layer == 1 else {2}
            groups_l = ([[1]] + GROUPS[1:]) if layer == 1 else ([[2], [3]] + GROUPS[2:])
            for grp in groups_l:
                build_xT(grp)
                nt = 128 * len(grp)
                sample = (grp[0] == SCH)
                if sample:
                    P.op("pool", lambda e: e.memset(zT[:], 0.0), writes=["zT"])
                for cc in range(16):
                    wi = wk[0] % 2
                    wk[0] += 1
                    P.dma("sp", lambda e, wi=wi, cc=cc: e.dma_start(out=w3[wi].rearrange("p k a m -> p (k a m)"), in_=wci_b[j, cc]),
                          reads=["cwci%d_%d_%d" % (j, cc, a) for a in range(3)], writes=["w3_%d" % wi])
                    for a in range(3):
                        for kc in range(16):
                            P.op("pe", lambda e, a=a, kc=kc, wi=wi: e.matmul(pb[a][:, 0:nt], lhsT=w3[wi][:, kc, a, :], rhs=xT[:, kc, 0:nt],
                                                                            start=(kc == 0), stop=(kc == 15)),
                                 reads=["xT", "w3_%d" % wi], writes=["pb%d" % a])
                    ui = cc % 2
                    P.op("act", lambda e: e.copy(out=hsb[:, 0:nt], in_=pb[2][:, 0:nt]), reads=["pb2"], writes=["hsb"])
                    if not sample:
                        ub = ubuf[ui]
                        ur = "ubuf%d" % ui
                        P.op("pool", lambda e, ub=ub, cc=cc: e.tensor_copy(out=ub[:, 0:2], in_=ucarry[:, cc, :]), reads=["ucarry"], writes=[ur])
                        P.op("dve", lambda e, ub=ub: e.tensor_tensor(out=ub[:, 2:2 + nt], in0=pb[1][:, 0:nt], in1=hsb[:, 0:nt], op=ALU.mult),
                             reads=["pb1", "hsb"], writes=[ur])
                        if 2 in grp:
                            o2 = 2 + grp.index(2) * 128 + 126
                            P.op("dve", lambda e, ub=ub, o2=o2: e.tensor_scalar(out=ub[:, o2:o2 + 2], in0=ub[:, o2:o2 + 2], scalar1=hv[:, 0:1], scalar2=None, op0=ALU.mult),
                                 reads=[ur, "hv"], writes=[ur])
                        P.op("pool", lambda e, ub=ub, cc=cc: e.tensor_copy(out=ucarry[:, cc, :], in_=ub[:, nt:nt + 2]), reads=[ur], writes=["ucarry"])
                        ct_ = ctmp[ui]
                        cr = "ctmp%d" % ui
                        P.op("dve", lambda e, ub=ub, ct_=ct_, cc=cc: e.tensor_scalar(out=ct_[:, 0:nt], in0=ub[:, 0:nt], scalar1=cw[:, cc, 0:1], scalar2=None, op0=ALU.mult),
                             reads=[ur, "cw"], writes=[cr])
                        P.op("dve", lambda e, ub=ub, ct_=ct_, cc=cc: e.scalar_tensor_tensor(out=ct_[:, 0:nt], in0=ub[:, 1:1 + nt], scalar=cw[:, cc, 1:2], in1=ct_[:, 0:nt],
                                                                                      op0=ALU.mult, op1=ALU.add), reads=[ur, "cw", cr], writes=[cr])
                        P.op("dve", lambda e, ub=ub, ct_=ct_, cc=cc: e.scalar_tensor_tensor(out=ct_[:, 0:nt], in0=ub[:, 2:2 + nt], scalar=cw[:, cc, 2:3], in1=ct_[:, 0:nt],
                                                                                      op0=ALU.mult, op1=ALU.add), reads=[ur, "cw", cr], writes=[cr])
                        P.op("dve", lambda e, ct_=ct_, cc=cc: e.tensor_tensor(out=zT[:, cc, 0:nt], in0=pb[0][:, 0:nt], in1=ct_[:, 0:nt], op=ALU.mult),
                             reads=["pb0", cr], writes=["zT"])
                    else:
                        ub = ubs[ui]
                        ur = "ubs%d" % ui
                        P.op("pool", lambda e, ub=ub, cc=cc: e.tensor_copy(out=ub[:, :, 0:2], in_=stT[:, cc, :].rearrange("p (s t) -> p s t", t=2)),
                             reads=["stT"], writes=[ur])
                        P.op("dve", lambda e, ub=ub: e.tensor_tensor(out=ub[:, :, 2:6], in0=pb[1][:, 0:16].rearrange("p (s t) -> p s t", t=4),
                                                                     in1=hsb[:, 0:16].rearrange("p (s t) -> p s t", t=4), op=ALU.mult),
                             reads=["pb1", "hsb"], writes=[ur])
                        P.op("pool", lambda e, ub=ub, cc=cc: e.tensor_copy(out=usout[:, cc, :, :], in_=ub[:, :, 4:6]), reads=[ur], writes=["usout"])
                        ct_ = ctmp[ui][:, 0:16].rearrange("p (s t) -> p s t", t=4)
                        cr = "ctmp%d" % ui
                        P.op("dve", lambda e, ub=ub, ct_=ct_, cc=cc: e.tensor_scalar(out=ct_, in0=ub[:, :, 0:4], scalar1=cw[:, cc, 0:1], scalar2=None, op0=ALU.mult),
                             reads=[ur, "cw"], writes=[cr])
                        P.op("dve", lambda e, ub=ub, ct_=ct_, cc=cc: e.scalar_tensor_tensor(out=ct_, in0=ub[:, :, 1:5], scalar=cw[:, cc, 1:2], in1=ct_,
                                                                                      op0=ALU.mult, op1=ALU.add), reads=[ur, "cw", cr], writes=[cr])
                        P.op("dve", lambda e, ub=ub, ct_=ct_, cc=cc: e.scalar_tensor_tensor(out=ct_, in0=ub[:, :, 2:6], scalar=cw[:, cc, 2:3], in1=ct_,
                                                                                      op0=ALU.mult, op1=ALU.add), reads=[ur, "cw", cr], writes=[cr])
                        P.op("dve", lambda e, ct_=ct_, cc=cc: e.tensor_tensor(out=zT[:, cc, 0:16].rearrange("p (s t) -> p s t", t=4),
                                                                             in0=pb[0][:, 0:16].rearrange("p (s t) -> p s t", t=4), in1=ct_, op=ALU.mult),
                             reads=["pb0", cr], writes=["zT"])
                if grp[-1] == NPB - 1:
                    P.op("pool", lambda e: e.memset(utr, 0.0), writes=["utr"])
                    P.op("pool", lambda e: e.tensor_copy(out=utr[:, 0:32].rearrange("p (t k) -> p t k", t=2), in_=ucarry.rearrange("p k t -> p t k")),
                         reads=["ucarry"], writes=["utr"])
                    P.op("pe", lambda e: e.transpose(out=pb[3][:, 0:128], in_=utr, identity=identf[:]), reads=["utr", "identf"], writes=["pb3"])
                    P.op("act", lambda e: e.copy(out=utro, in_=pb[3][:, 0:128]), reads=["pb3"], writes=["utro"])
                    P.dma("sp", lambda e: e.dma_start(out=cp_o[j].rearrange("t (k p) -> (t k) p", p=128), in_=utro[0:32, :]), reads=["utro"])
                if sample:
                    P.op("pool", lambda e: e.tensor_copy(out=utr.rearrange("p (s t k) -> p s t k", s=4, t=2), in_=usout.rearrange("p k s t -> p s t k")),
                         reads=["usout"], writes=["utr"])
                    P.op("pe", lambda e: e.transpose(out=pb[3][:, 0:128], in_=utr, identity=identf[:]), reads=["utr", "identf"], writes=["pb3"])
                    P.op("act", lambda e: e.copy(out=utro, in_=pb[3][:, 0:128]), reads=["pb3"], writes=["utro"])
                    P.dma("sp", lambda e: e.dma_start(out=cso_o[j].rearrange("s t (k p) -> (s t k) p", p=128), in_=utro), reads=["utro"])
                out_proj(grp, lambda gi, kc: zT[:, kc, gi * 128:(gi + 1) * 128], "zT", (wco_b[j], "cco%d" % j), None, None, skip=u_only)
                for c in grp:
                    if c not in u_only:
                        layer_norm(c, layer, 0)

        def peer_layer(layer):
            P.barrier()
            a_reset()
            qT = a_bf16(16 * 256).rearrange("p (k t) -> p k t", k=16)
            keysT = a_bf16(16 * 128).rearrange("p (k n) -> p k n", k=16)
            big = a_f32(2048)
            sc = a_f32(2048)
            work = a_f32(256)
            sv = a_f32(256).rearrange("p (a k) -> p a k", k=16)
            si = a_f32(256).bitcast(U32).rearrange("p (a k) -> p a k", k=16)
            sif = a_f32(256).rearrange("p (a k) -> p a k", k=16)
            fv = a_f32(128).rearrange("p (h k) -> p h k", k=16)
            fpu = a_f32(128).bitcast(U32).rearrange("p (h k) -> p h k", k=16)
            k1u = a_f32(128).bitcast(U32).rearrange("p (h k) -> p h k", k=16)
            k2u = a_f32(128).bitcast(U32).rearrange("p (h k) -> p h k", k=16)
            k1f = a_f32(128).rearrange("p (h k) -> p h k", k=16)
            k2f = a_f32(128).rearrange("p (h k) -> p h k", k=16)
            i1f = a_f32(128).rearrange("p (h k) -> p h k", k=16)
            i2f = a_f32(128).rearrange("p (h k) -> p h k", k=16)
            eidx = a_f32(128).bitcast(I32)
            gate = a_f32(128).rearrange("p (h k) -> p h k", k=16)
            gsm = a_f32(16).rearrange("p (a h) -> p a h", h=8)
            hid = a_f32(128)
            ug = [a_bf16(2 * D) for _ in range(2)]
            ug += [wt[0][:, :, :].rearrange("p k n -> p (k n)"), wt[1][:, :, :].rearrange("p k n -> p (k n)"),
                   xT[:, :, :].rearrange("p k n -> p (k n)"), sc.bitcast(BF16), big.bitcast(BF16),
                   lnp[:, 0, :].bitcast(BF16), lnp[:, 1, :].bitcast(BF16)]
            NG = len(ug)
            NPRIV = 2
            wt_alias[0] = ["ug2"]
            wt_alias[1] = ["ug3"]
            xT_alias[:] = ["ug4"]
            SC_AL = ["ug5"]
            BIG_AL = ["ug6"]
            LNP_AL = ["ug7", "ug8"]
            junk = a_bf16(D)
            xb = a_bf16(D)
            diag = [a_bf16(128) for _ in range(4)]
            dgf = [a_f32(128) for _ in range(4)]
            if layer == 0:
                for b_ in range(NPRIV):
                    P.op("pool", lambda e, b_=b_: e.memset(ug[b_], 0.0), writes=["ug%d" % b_])
            for half in range(2):
                P.dma("sp", lambda e, half=half: e.dma_start(out=big.rearrange("p (a d) -> p a d", a=16)[:, half * 8:(half + 1) * 8, :],
                                                          in_=psk_d[layer, half * 8:(half + 1) * 8].rearrange("a n d -> n a d")), writes=["big"] + BIG_AL)
            for q in range(4):
                for r in range(4):
                    hp = q * 4 + r
                    P.op("pe", lambda e, hp=hp, r=r: e.transpose(out=pb[7][:, r * 128:(r + 1) * 128], in_=big[:, hp * 128:(hp + 1) * 128], identity=identf[:]),
                         reads=["big", "identf"], writes=["pb7"])
                P.op("act", lambda e, q=q: e.copy(out=keysT[:, q * 4:(q + 1) * 4, :], in_=pb[7][:, :].rearrange("p (r n) -> p r n", r=4)),
                     reads=["pb7"], writes=["keysT"])
            pending = table_conv_ops(layer + 1) if layer + 1 < n_layers else []
            for grp in GROUPS:
                if all(c < PEER_MIN_CHUNK[layer] for c in grp):
                    continue
                build_xT(grp)
                nt = 128 * len(grp)
                for ct in range(8):
                    wi = load_wt(wpq_b[layer, ct], "cpq%d_%d" % (layer, ct))
                    for m in range(2):
                        hp = ct * 2 + m
                        bank = 4 + (hp % 2)
                        for kc in range(16):
                            P.op("pe", lambda e, kc=kc, m=m, bank=bank, wi=wi: e.matmul(pb[bank][:, 0:nt], lhsT=wt[wi][:, kc, m * 128:(m + 1) * 128],
                                                                                      rhs=xT[:, kc, 0:nt], start=(kc == 0), stop=(kc == 15)),
                                 reads=["xT", "wt%d" % wi], writes=["pb%d" % bank])
                        P.op("act", lambda e, hp=hp, bank=bank: e.copy(out=qT[:, hp, 0:nt], in_=pb[bank][:, 0:nt]), reads=["pb%d" % bank], writes=["qT"])
                for gi, c in enumerate(grp):
                    if c < PEER_MIN_CHUNK[layer]:
                        continue
                    for _ in range(2):
                        if pending:
                            pending.pop(0)()
                    npart = 16 if c == SCH else 128
                    ub_res = ["ub%d_%d" % (layer, r) for r in range(8)]
                    vb_res = ["vb%d_%d" % (layer, r) for r in range(8)]
                    for hp in range(16):
                        bank = hp // 4
                        off = (hp % 4) * 128
                        P.op("pe", lambda e, hp=hp, bank=bank, off=off: e.matmul(pb[bank][:, off:off + 128], lhsT=qT[:, hp, gi * 128:(gi + 1) * 128],
                                                                               rhs=keysT[:, hp, :], start=True, stop=True),
                             reads=["qT", "keysT"], writes=["pb%d" % bank])
                    for b4 in range(4):
                        P.op("act", lambda e, b4=b4: e.copy(out=sc[:, b4 * 512:(b4 + 1) * 512], in_=pb[b4][:, :]), reads=["pb%d" % b4], writes=["sc"] + SC_AL)
                    if debug and layer == 0 and c == NPB - 1:
                        P.dma("sp", lambda e: e.dma_start(out=dbgS, in_=sc), reads=["sc"])
                    for hp in range(16):
                        s_in = sc[:, hp * 128:(hp + 1) * 128]
                        P.op("dve", lambda e, hp=hp, s_in=s_in: e.max(out=sv[:, hp, 0:8], in_=s_in), reads=["sc"], writes=["sv"])
                        P.op("dve", lambda e, hp=hp, s_in=s_in: e.max_index(out=si[:, hp, 0:8], in_max=sv[:, hp, 0:8], in_values=s_in), reads=["sc", "sv"], writes=["si"])
                        P.op("dve", lambda e, hp=hp, s_in=s_in: e.match_replace(out=work[:, 0:128], in_to_replace=sv[:, hp, 0:8], in_values=s_in, imm_value=NEG),
                             reads=["sc", "sv"], writes=["work"])
                        P.op("dve", lambda e, hp=hp: e.max(out=sv[:, hp, 8:16], in_=work[:, 0:128]), reads=["work"], writes=["sv"])
                        P.op("dve", lambda e, hp=hp: e.max_index(out=si[:, hp, 8:16], in_max=sv[:, hp, 8:16], in_values=work[:, 0:128]), reads=["work", "sv"], writes=["si"])
                    P.op("dve", lambda e: e.tensor_copy(out=sif, in_=si), reads=["si"], writes=["sif"])
                    svv = sv.rearrange("p (h two) k -> p h two k", two=2)
                    cand = big.rearrange("p (h a b) -> p h a b", h=8, a=16)
                    P.op("dve", lambda e, svv=svv, cand=cand: e.tensor_tensor(out=cand, in0=svv[:, :, 0, :].unsqueeze(3).to_broadcast([128, 8, 16, 16]),
                                                                            in1=svv[:, :, 1, :].unsqueeze(2).to_broadcast([128, 8, 16, 16]), op=ALU.add),
                         reads=["sv"], writes=["big"] + BIG_AL)
                    for h in range(8):
                        c_in = big[:, h * 256:(h + 1) * 256]
                        P.op("dve", lambda e, h=h, c_in=c_in: e.max(out=fv[:, h, 0:8], in_=c_in), reads=["big"], writes=["fv"])
                        P.op("dve", lambda e, h=h, c_in=c_in: e.max_index(out=fpu[:, h, 0:8], in_max=fv[:, h, 0:8], in_values=c_in), reads=["big", "fv"], writes=["fpu"])
                        P.op("dve", lambda e, h=h, c_in=c_in: e.match_replace(out=work, in_to_replace=fv[:, h, 0:8], in_values=c_in, imm_value=NEG),
                             reads=["big", "fv"], writes=["work"])
                        P.op("dve", lambda e, h=h: e.max(out=fv[:, h, 8:16], in_=work), reads=["work"], writes=["fv"])
                        P.op("dve", lambda e, h=h: e.max_index(out=fpu[:, h, 8:16], in_max=fv[:, h, 8:16], in_values=work), reads=["work", "fv"], writes=["fpu"])
                    P.op("dve", lambda e: e.tensor_single_scalar(out=k1u, in_=fpu, scalar=4, op=ALU.logical_shift_right), reads=["fpu"], writes=["k1u"])
                    P.op("dve", lambda e: e.tensor_single_scalar(out=k2u, in_=fpu, scalar=15, op=ALU.bitwise_and), reads=["fpu"], writes=["k2u"])
                    P.op("dve", lambda e: e.tensor_copy(out=k1f, in_=k1u), reads=["k1u"], writes=["k1f"])
                    P.op("dve", lambda e: e.tensor_copy(out=k2f, in_=k2u), reads=["k2u"], writes=["k2f"])
                    oh = sc.rearrange("p (h a b) -> p h a b", h=8, a=16)
                    siv = sif.rearrange("p (h two) k -> p h two k", two=2)
                    io = iota16[:, :].unsqueeze(1).unsqueeze(1).to_broadcast([128, 8, 16, 16])
                    for which, kf, dst in ((0, k1f, i1f), (1, k2f, i2f)):
                        P.op("dve", lambda e, kf=kf, oh=oh, io=io: e.tensor_tensor(out=oh, in0=kf.unsqueeze(3).to_broadcast([128, 8, 16, 16]), in1=io, op=ALU.is_equal),
                             reads=["k1f", "k2f", "iota16", "sc"], writes=["sc"])
                        P.op("dve", lambda e, which=which, oh=oh, siv=siv: e.tensor_tensor(out=oh, in0=oh, in1=siv[:, :, which, :].unsqueeze(2).to_broadcast([128, 8, 16, 16]), op=ALU.mult),
                             reads=["sc", "sif"], writes=["sc"])
                        P.op("dve", lambda e, oh=oh, dst=dst: e.tensor_reduce(out=dst, in_=oh, axis=AX.X, op=ALU.add), reads=["sc"], writes=["i12"])
                    P.op("dve", lambda e: e.scalar_tensor_tensor(out=i1f, in0=i1f, scalar=128.0, in1=i2f, op0=ALU.mult, op1=ALU.add), reads=["i12"], writes=["i12"])
                    P.op("dve", lambda e: e.tensor_scalar(out=i1f, in0=i1f, scalar1=0.0, scalar2=None, op0=ALU.add), reads=["i12"], writes=["i12"])
                    P.op("dve", lambda e: e.tensor_copy(out=eidx, in_=i1f.rearrange("p h k -> p (h k)")), reads=["i12"], writes=["eidx"])
                    P.op("dve", lambda e: e.tensor_tensor(out=gate, in0=fv, in1=fv[:, :, 0:1].to_broadcast([128, 8, 16]), op=ALU.subtract), reads=["fv"], writes=["gate"])
                    P.op("act", lambda e: e.activation(out=gate, in_=gate, func=AF.Exp), reads=["gate"], writes=["gate"])
                    P.op("dve", lambda e: e.tensor_reduce(out=gsm[:, 0, :], in_=gate, axis=AX.X, op=ALU.add), reads=["gate"], writes=["gsm"])
                    P.op("dve", lambda e: e.reciprocal(out=gsm[:, 1, :], in_=gsm[:, 0, :]), reads=["gsm"], writes=["gsm"])
                    P.op("dve", lambda e: e.tensor_tensor(out=gate, in0=gate, in1=gsm[:, 1, :].unsqueeze(2).to_broadcast([128, 8, 16]), op=ALU.mult),
                         reads=["gate", "gsm"], writes=["gate"])
                    P.op("act", lambda e: e.copy(out=xb, in_=xres[:, c, :]), reads=[xr(c)], writes=["xb"])
                    uv_res = ["ub%d_%d" % (layer, r) for r in range(8)] + ["vb%d_%d" % (layer, r) for r in range(8)]
                    gflat = gate.rearrange("p h k -> p (h k)")
                    SKEW = 0
                    tails = []

                    def make_tail(jj, b, dgi):
                        def tail():
                            P.op("act", lambda e: e.activation(out=diag[dgi], in_=dgf[dgi], func=AF.Copy, scale=gflat[:, jj:jj + 1]),
                                 reads=["dgf%d" % dgi, "gate"], writes=["diag%d" % dgi])
                            for q in range(4):
                                P.op("pe", lambda e, q=q: e.matmul(pb[q][:, :], lhsT=diag[dgi], rhs=ug[b][:, D + q * 512:D + (q + 1) * 512],
                                                                  start=(jj == 0), stop=(jj == 127)),
                                     reads=["diag%d" % dgi, "ug%d" % b], writes=["pb%d" % q])
                        return tail

                    for jj in range(128):
                        b = jj % (NPRIV if c == SCH else NG)
                        dgi = jj % 4
                        if SKEW and len(tails) >= SKEW:
                            tails.pop(0)()
                        P.dma("pool", lambda e, jj=jj, b=b: e.indirect_dma_start(out=ug[b][0:npart, :], out_offset=None,
                                                                             in_=uv_l[layer].rearrange("e w d -> e (w d)"),
                                                                             in_offset=bass.IndirectOffsetOnAxis(ap=eidx[0:npart, jj:jj + 1], axis=0)),
                              reads=["eidx"] + uv_res, writes=["ug%d" % b])
                        P.op("dve", lambda e, jj=jj, b=b: e.scalar_tensor_tensor(out=junk, in0=ug[b][:, 0:D], scalar=1.0, in1=xb, op0=ALU.mult, op1=ALU.mult,
                                                                              accum_out=hid[:, jj:jj + 1]),
                             reads=["ug%d" % b, "xb"], writes=["junk", "hid%d" % (jj % 8)])
                        P.op("act", lambda e, jj=jj, dgi=dgi: e.activation(out=dgf[dgi], in_=identf[:], func=AF.Gelu, scale=hid[:, jj:jj + 1]),
                             reads=["hid%d" % (jj % 8), "identf"], writes=["dgf%d" % dgi])
                        tails.append(make_tail(jj, b, dgi))
                        if not SKEW:
                            tails.pop(0)()
                    while tails:
                        tails.pop(0)()
                    if debug and layer == 0 and c == NPB - 1:
                        P.dma("sp", lambda e: e.dma_start(out=dbgE, in_=eidx), reads=["eidx"])
                        P.dma("sp", lambda e: e.dma_start(out=dbgG, in_=gate.rearrange("p h k -> p (h k)")), reads=["gate"])
                        P.dma("sp", lambda e: e.dma_start(out=dbgH, in_=hid), reads=["hid%d" % i for i in range(8)])
                    for q in range(4):
                        xs = xres[:, c, q * 512:(q + 1) * 512]
                        P.op("dve", lambda e, xs=xs, q=q: e.scalar_tensor_tensor(out=xs, in0=xs, scalar=ALPHA, in1=pb[q][:, :], op0=ALU.mult, op1=ALU.add),
                             reads=[xr(c), "pb%d" % q], writes=[xr(c)])
                    load_ln(layer, 1, LNP_AL)
                    layer_norm(c, layer, 1)
            while pending:
                pending.pop(0)()
            wt_alias[0] = []
            wt_alias[1] = []
            xT_alias[:] = []

        for layer in range(n_layers):
            if layer % 2 == 0:
                attention_layer(layer)
            else:
                conv_layer(layer)
            if debug and layer == 0:
                P.barrier()
                for i in range(8):
                    P.dma("sp", lambda e, i=i: e.dma_start(out=dbg_o[i], in_=xres[:, FIRST_OWN + i, :]), reads=[xr(FIRST_OWN + i)])
                P.dma("sp", lambda e: e.dma_start(out=dbg_o[8], in_=xres[:, SCH, :]), reads=[xr(SCH)])
            peer_layer(layer)
        P.barrier()
        for i in range(8):
            P.dma("sp", lambda e, i=i: e.dma_start(out=y_o[i], in_=xres[:, FIRST_OWN + i, :]), reads=[xr(FIRST_OWN + i)])
        P.dma("sp", lambda e: e.dma_start(out=y_o[8], in_=xres[:, SCH, :]), reads=[xr(SCH)])
        P.emit(st)
    return nc


def _consts():
    i = np.arange(128)[:, None]
    s = np.arange(128)[None, :]
    maskP = np.concatenate([np.where(s >= i, 0.0, NEG), np.where(s <= i, 0.0, NEG)], axis=1)
    prevS = np.where(s >= (i % 4), 0.0, NEG)
    ownS = np.where((i < 16) & (s < 16) & (s // 4 == i // 4) & (s <= i), 0.0, NEG)
    maskS = np.concatenate([prevS, ownS], axis=1)
    masks = np.stack([maskP, maskS], axis=1).astype(np.float32)
    rowmask = ((i // 4 == np.arange(4)[None, :]) & (i < 16)).astype(np.float32)
    return masks, rowmask


def _rope_table(pos):
    inv = np.power(np.float32(10000.0), -np.arange(32, dtype=np.float32) * np.float32(2.0) / np.float32(64))
    ang = pos.astype(np.float32)[:, None] * inv[None, :]
    return np.concatenate([np.cos(ang), np.sin(ang)], axis=1).astype(np.float32)


def make_in_maps(inputs, cores=range(8)):
    f = lambda a: np.ascontiguousarray(np.asarray(a, dtype=np.float32))
    xp = f(inputs["x_prompt"])
    xs = f(inputs["x_sample"])
    ck = f(inputs["cache_k"]).reshape(2, 32, 128, 512)
    cv = f(inputs["cache_v"]).reshape(2, 32, 128, 512)
    stc = f(inputs["state_conv"])
    masks, rowmask = _consts()
    shared = dict(
        w_qkv=f(inputs["w_qkv"]), b_qkv=f(inputs["b_qkv"]), w_o=f(inputs["w_o"]), b_o=f(inputs["b_o"]),
        attn_sinks=f(inputs["attn_sinks"]).reshape(2, 32), w_conv_in=f(inputs["w_conv_in"]),
        conv_wT=np.ascontiguousarray(f(inputs["conv_w"]).reshape(2, 3, 16, 128).transpose(0, 3, 2, 1)),
        w_conv_out=f(inputs["w_conv_out"]), w_peer_q=f(inputs["w_peer_q"]),
        peer_sub_keys=f(inputs["peer_sub_keys"]).reshape(4, 16, 128, 128),
        peer_u=f(inputs["peer_u"]), peer_v=f(inputs["peer_v"]), ln_g=f(inputs["ln_g"]), ln_b=f(inputs["ln_b"]),
        masks=masks, rowmask=rowmask)
    maps = []
    for c in cores:
        b = c // 4
        b0 = (c % 4) * 8
        xin = np.zeros((NCH, 128, D), np.float32)
        cs = np.zeros((NCH, 128, 64), np.float32)
        for jj in range(NPB):
            g = b0 - 3 + jj
            pos = np.arange(128) + max(g, 0) * 128
            cs[jj] = _rope_table(pos)
            if g >= 0:
                xin[jj] = xp[b, g * 128:(g + 1) * 128]
        xin[SCH, 0:16] = xs[4 * c:4 * c + 4].reshape(16, D)
        cs[SCH] = _rope_table(16384 + (np.arange(128) % 4))
        m = dict(shared)
        m.update(xin=xin, cs=np.ascontiguousarray(cs.transpose(1, 0, 2)),
                 hv=np.full((128, 1), 0.0 if b0 == 0 else 1.0, np.float32),
                 ck=np.ascontiguousarray(ck[:, 4 * c:4 * c + 4]), cv=np.ascontiguousarray(cv[:, 4 * c:4 * c + 4]),
                 stc=np.ascontiguousarray(stc[:, 4 * c:4 * c + 4].reshape(2, 8, D)))
        maps.append(m)
    return maps


def assemble(results):
    yp = np.zeros((2, 4096, D), np.float32)
    ys = np.zeros((32, 4, D), np.float32)
    kp = np.zeros((2, 2, 128, 8, 64), np.float32)
    vp = np.zeros((2, 2, 128, 8, 64), np.float32)
    cp = np.zeros((2, 2, 2, D), np.float32)
    ks = np.zeros((2, 32, 128, 8, 64), np.float32)
    vs = np.zeros((2, 32, 128, 8, 64), np.float32)
    cso = np.zeros((2, 32, 2, D), np.float32)
    for c, r in enumerate(results):
        b = c // 4
        q = c % 4
        yp[b, q * 1024:(q + 1) * 1024] = r["y"][0:8].reshape(1024, D)
        ys[4 * c:4 * c + 4] = r["y"][8, 0:16].reshape(4, 4, D)
        if q == 3:
            kp[:, b] = r["kp"].reshape(2, 128, 8, 64)
            vp[:, b] = r["vp"].reshape(2, 128, 8, 64)
            cp[:, b] = r["cp"]
        ks[:, 4 * c:4 * c + 4] = r["ks"].reshape(2, 4, 128, 8, 64)
        vs[:, 4 * c:4 * c + 4] = r["vs"].reshape(2, 4, 128, 8, 64)
        cso[:, 4 * c:4 * c + 4] = r["cso"]
    return yp, ys, kp, vp, cp, ks, vs, cso


def kernel(**inputs):
    nc = build_program(4)
    maps = make_in_maps(inputs)
    res = run_bass_kernel_spmd(nc, maps, core_ids=list(range(8)))
    return assemble(res.results)
```

```python
import types
import numpy as np
from contextlib import ExitStack
import concourse.bass as bass
import concourse.mybir as mybir
from concourse.bass_utils import run_bass_kernel_spmd

F32 = mybir.dt.float32
BF16 = mybir.dt.bfloat16
I32 = mybir.dt.int32
U32 = mybir.dt.uint32
AF = mybir.ActivationFunctionType
ALU = mybir.AluOpType
AX = mybir.AxisListType

D = 2048
NCH = 12
NPB = 11
SCH = 11
FIRST_OWN = 3
ALPHA = float(8 ** 0.25)
LN_EPS = 1e-5
NEG = -1e30
GROUPS = [[0, 1], [2, 3], [4, 5], [6, 7], [8, 9], [10], [11]]
ENGS = ("pe", "act", "dve", "pool", "sp")


def _freeze(fn):
    if fn.__closure__ is None:
        return fn
    cells = []
    for c in fn.__closure__:
        try:
            cells.append(types.CellType(c.cell_contents))
        except ValueError:
            cells.append(c)
    return types.FunctionType(fn.__code__, fn.__globals__, fn.__name__, fn.__defaults__, tuple(cells))


class Prog:
    def __init__(self, nc, ring=16):
        self.nc = nc
        self.ops = []
        self.last_write = {}
        self.readers = {}
        self.ring = ring
        self.pending_barrier = {}

    def op(self, eng, fn, reads=(), writes=(), dma=False):
        deps = set()
        for r in reads:
            if r in self.last_write:
                deps.add(self.last_write[r])
        for w in writes:
            if w in self.last_write:
                deps.add(self.last_write[w])
            for rd in self.readers.get(w, ()):
                deps.add(rd)
        if eng in self.pending_barrier:
            deps.update(self.pending_barrier.pop(eng))
        i = len(self.ops)
        self.ops.append(dict(eng=eng, fn=_freeze(fn), deps=sorted(deps), dma=dma, sig=None))
        for r in reads:
            self.readers.setdefault(r, []).append(i)
        for w in writes:
            self.last_write[w] = i
            self.readers[w] = []
        return i

    def dma(self, eng, fn, reads=(), writes=()):
        return self.op(eng, fn, reads, writes, dma=True)

    def barrier(self):
        last = {}
        dmas = []
        for i, o in enumerate(self.ops):
            if o["dma"]:
                dmas.append(i)
            else:
                last[o["eng"]] = i
        start = getattr(self, "_bar_start", 0)
        dd = [i for i in dmas if i >= start]
        self._bar_start = len(self.ops)
        deps = set(last.values()) | set(dd)
        for e in ENGS:
            s = set(self.pending_barrier.get(e, ()))
            self.pending_barrier[e] = s | deps

    def emit(self, stack):
        nc = self.nc
        ops = self.ops
        needed = [False] * len(ops)
        for o in ops:
            for d in o["deps"]:
                p = ops[d]
                if p["dma"]:
                    continue
                if p["eng"] == "pe" and o["eng"] == "pe" and not o["dma"]:
                    continue
                needed[d] = True
        esem = {e: stack.enter_context(nc.semaphore("s_" + e)) for e in ENGS}
        dsem = {e: [stack.enter_context(nc.semaphore("d_%s%d" % (e, k))) for k in range(self.ring)]
                for e in ("sp", "pool", "act")}
        cnt = {e: 0 for e in ENGS}
        dcnt = {e: 0 for e in dsem}
        dval = {e: [0] * self.ring for e in dsem}
        for i, o in enumerate(ops):
            if o["dma"]:
                e = o["eng"]
                k = dcnt[e] % self.ring
                dcnt[e] += 1
                o["prev"] = (dsem[e][k], dval[e][k])
                dval[e][k] += 16
                o["sig"] = (dsem[e][k], dval[e][k])
            elif needed[i]:
                cnt[o["eng"]] += 1
                o["sig"] = (esem[o["eng"]], cnt[o["eng"]])
        per = {e: [] for e in ENGS}
        for i, o in enumerate(ops):
            per[o["eng"]].append(i)
        final = []
        for e in dsem:
            for k in range(self.ring):
                if dval[e][k]:
                    final.append((dsem[e][k], dval[e][k]))

        def run(e, engh):
            waited = {}
            for i in per[e]:
                o = ops[i]
                ws = []
                if o["dma"] and o["prev"][1] > 0:
                    ws.append(o["prev"])
                for d in o["deps"]:
                    p = ops[d]
                    if p["sig"] is None:
                        continue
                    if (not p["dma"]) and p["eng"] == "pe" and e == "pe" and not o["dma"]:
                        continue
                    ws.append(p["sig"])
                best = {}
                for s, v in ws:
                    key = id(s)
                    if waited.get(key, 0) >= v:
                        continue
                    if key not in best or best[key][1] < v:
                        best[key] = (s, v)
                for key, (s, v) in best.items():
                    engh.wait_ge(s, v)
                    waited[key] = v
                ins = o["fn"](engh)
                if o["sig"] is not None:
                    ins.then_inc(o["sig"][0], 16 if o["dma"] else 1)
            if e == "sp":
                for s, v in final:
                    engh.wait_ge(s, v)

        block = stack.enter_context(nc.Block())

        @block.tensor
        def _(t):
            run("pe", t)

        @block.scalar
        def _(t):
            run("act", t)

        @block.vector
        def _(t):
            run("dve", t)

        @block.gpsimd
        def _(t):
            run("pool", t)

        @block.sync
        def _(t):
            run("sp", t)


def build_program(n_layers=4, debug=False):
    nc = bass.Bass("TRN2", target_bir_lowering=False)

    def din(name, shape, dtype=F32):
        return nc.dram_tensor(name, shape, dtype, kind="ExternalInput").ap()

    def dout(name, shape, dtype=F32):
        return nc.dram_tensor(name, shape, dtype, kind="ExternalOutput").ap()

    xin = din("xin", [NCH, 128, D])
    cs_d = din("cs", [128, NCH, 64])
    masks_d = din("masks", [128, 2, 256])
    rowmask_d = din("rowmask", [128, 4])
    hv_d = din("hv", [128, 1])
    ck_d = din("ck", [2, 4, 128, 512])
    cv_d = din("cv", [2, 4, 128, 512])
    stc_d = din("stc", [2, 8, D])
    wqkv_d = din("w_qkv", [2, D, 3072])
    bqkv_d = din("b_qkv", [2, 3072])
    wo_d = din("w_o", [2, D, D])
    bo_d = din("b_o", [2, D])
    sinks_d = din("attn_sinks", [2, 32])
    wci_d = din("w_conv_in", [2, D, 6144])
    cwT_d = din("conv_wT", [2, 128, 16, 3])
    wco_d = din("w_conv_out", [2, D, D])
    wpq_d = din("w_peer_q", [4, D, D])
    psk_d = din("peer_sub_keys", [4, 16, 128, 128])
    pu_d = din("peer_u", [4, 16384, D])
    pv_d = din("peer_v", [4, 16384, D])
    lng_d = din("ln_g", [4, 2, D])
    lnb_d = din("ln_b", [4, 2, D])

    y_o = dout("y", [9, 128, D])
    kp_o = dout("kp", [2, 128, 512])
    vp_o = dout("vp", [2, 128, 512])
    cp_o = dout("cp", [2, 2, D])
    ks_o = dout("ks", [2, 4, 128, 512])
    vs_o = dout("vs", [2, 4, 128, 512])
    cso_o = dout("cso", [2, 4, 2, D])
    dbg_o = dout("dbg", [9, 128, D]) if debug else None
    dbgO = dout("dbgO", [128, D]) if debug else None
    dbgE = dout("dbgE", [128, 128], I32) if debug else None
    dbgG = dout("dbgG", [128, 128]) if debug else None
    dbgH = dout("dbgH", [128, 128]) if debug else None
    dbgS = dout("dbgS", [128, 2048]) if debug else None
    dbgSP = dout("dbgSP", [128, 1024]) if debug else None
    dbgP = dout("dbgP", [128, 1024]) if debug else None
    dbgSM = dout("dbgSM", [128, 64]) if debug else None
    dbgU = dout("dbgU", [128, 2048], BF16) if debug else None

    uv_l = [nc.dram_tensor("uv_scratch%d" % l, [16384, 2, D], BF16, kind="Internal").ap() for l in range(4)]
    PEER_MIN_CHUNK = [1, 1, 2, 3]
    wqkv_b = nc.dram_tensor("wqkv_b", [2, 12, 128, 4096], BF16, kind="Internal").ap()
    wo_b = nc.dram_tensor("wo_b", [2, 8, 128, 4096], BF16, kind="Internal").ap()
    wco_b = nc.dram_tensor("wco_b", [2, 8, 128, 4096], BF16, kind="Internal").ap()
    wpq_b = nc.dram_tensor("wpq_b", [4, 8, 128, 4096], BF16, kind="Internal").ap()
    wci_b = nc.dram_tensor("wci_b", [2, 16, 128, 6144], BF16, kind="Internal").ap()

    st = ExitStack()
    with st:
        def sb(name, shape, dt=F32):
            return st.enter_context(nc.sbuf_tensor(name, shape, dt))

        xres = sb("xres", [128, NCH, D])
        identf = sb("identf", [128, 128])
        identb = sb("identb", [128, 128], BF16)
        masks = sb("masks_sb", [128, 3, 256])
        rowmask = sb("rowmask_sb", [128, 4])
        hv = sb("hv_sb", [128, 1])
        negbig = sb("negbig", [128, 1])
        cs = sb("cs_sb", [128, NCH, 64])
        iota16 = sb("iota16", [128, 16])
        xT = sb("xT", [128, 16, 256], BF16)
        wt = [sb("wt%d" % i, [128, 16, 256], BF16) for i in range(2)]
        lnp = sb("lnp", [128, 2, D])
        stats = sb("stats", [128, 4, 6])
        mv = sb("mv", [128, 2])
        rstd = sb("rstd", [128, 1])
        epsb = sb("epsb", [128, 1])
        ARENA = 16560
        arena = sb("arena", [128, ARENA])
        pb = [st.enter_context(nc.psum_tensor("pb%d" % i, [128, 512], F32)) for i in range(8)]

        P = Prog(nc)
        cur = [0]

        def a_reset():
            cur[0] = 0

        def a_f32(n):
            a = cur[0]
            cur[0] += n
            assert cur[0] <= ARENA, cur[0]
            return arena[:, a:a + n]

        def a_bf16(n):
            n32 = (n + 1) // 2
            return a_f32(n32).bitcast(BF16)

        wq = [0]

        P.dma("sp", lambda e: e.dma_start(out=xres[:], in_=xin.rearrange("c p d -> p c d")), writes=["xres%d" % c for c in range(NCH)])
        P.dma("sp", lambda e: e.dma_start(out=masks[:, 0:2, :], in_=masks_d), writes=["masks"])
        P.dma("sp", lambda e: e.dma_start(out=rowmask[:], in_=rowmask_d), writes=["rowmask"])
        P.dma("sp", lambda e: e.dma_start(out=hv[:], in_=hv_d), writes=["hv"])
        P.dma("sp", lambda e: e.dma_start(out=cs[:], in_=cs_d), writes=["cs"])
        P.op("pool", lambda e: e.memset(epsb[:], LN_EPS), writes=["epsb"])
        P.op("pool", lambda e: e.iota(identf[:], pattern=[[1, 128]], base=0, channel_multiplier=-1,
                                      allow_small_or_imprecise_dtypes=True), writes=["identf"])
        P.op("pool", lambda e: e.iota(iota16[:], pattern=[[1, 16]], base=0, channel_multiplier=0,
                                      allow_small_or_imprecise_dtypes=True), writes=["iota16"])
        P.op("dve", lambda e: e.tensor_scalar(out=identb[:], in0=identf[:], scalar1=0.0, scalar2=None, op0=ALU.is_equal),
             reads=["identf"], writes=["identb"])
        P.op("dve", lambda e: e.tensor_scalar(out=identf[:], in0=identf[:], scalar1=0.0, scalar2=None, op0=ALU.is_equal),
             reads=["identf"], writes=["identf"])
        P.op("dve", lambda e: e.tensor_scalar(out=negbig[:], in0=hv[:], scalar1=-1.0, scalar2=1e30, op0=ALU.add, op1=ALU.mult),
             reads=["hv"], writes=["negbig"])
        P.op("dve", lambda e: e.tensor_copy(out=masks[:, 2, :], in_=masks[:, 0, :]), reads=["masks"], writes=["masks"])
        P.op("dve", lambda e: e.tensor_scalar(out=masks[:, 2, 0:128], in0=masks[:, 2, 0:128], scalar1=negbig[:, 0:1], scalar2=None, op0=ALU.add),
             reads=["masks", "negbig"], writes=["masks"])

        def xr(c):
            return "xres%d" % c

        def table_conv_ops(l):
            ops_ = []
            for src, w, nm in ((pu_d, 0, "ub%d" % l), (pv_d, 1, "vb%d" % l)):
                for r in range(8):
                    def go(src=src, w=w, nm=nm, r=r, l=l):
                        P.dma("pool", lambda e: e.dma_start(out=uv_l[l][r * 2048:(r + 1) * 2048, w, :],
                                                            in_=src[l, r * 2048:(r + 1) * 2048, :]), writes=[nm + "_%d" % r])
                    ops_.append(go)
            return ops_

        def conv_w256(src2d, dst_tiles, nm, ntile):
            for ct in range(ntile):
                P.dma("pool", lambda e, ct=ct: e.dma_start(out=dst_tiles[ct].rearrange("p (kc n) -> p kc n", kc=16),
                                                           in_=src2d[:, ct * 256:(ct + 1) * 256].rearrange("(kc p) n -> p kc n", p=128)),
                      writes=["%s_%d" % (nm, ct)])

        def conv_wci(j):
            for cc in range(16):
                src = wci_d[j].rearrange("(kc p) (a c m) -> p kc a c m", p=128, a=3, m=128)[:, :, :, cc, :]
                dst = wci_b[j, cc].rearrange("p (kc a m) -> p kc a m", kc=16, a=3)
                for a in range(3):
                    P.dma("pool", lambda e, src=src, dst=dst, a=a: e.dma_start(out=dst[:, :, a, :], in_=src[:, :, a, :]), writes=["cwci%d_%d_%d" % (j, cc, a)])

        conv_w256(wqkv_d[0], wqkv_b[0], "cqkv0", 12)
        conv_w256(wo_d[0], wo_b[0], "co0", 8)
        conv_w256(wpq_d[0], wpq_b[0], "cpq0", 8)
        for go in table_conv_ops(0):
            go()
        if n_layers > 1:
            conv_wci(0)
            conv_w256(wco_d[0], wco_b[0], "cco0", 8)
            conv_w256(wpq_d[1], wpq_b[1], "cpq1", 8)
        if n_layers > 2:
            conv_w256(wqkv_d[1], wqkv_b[1], "cqkv1", 12)
            conv_w256(wo_d[1], wo_b[1], "co1", 8)
            conv_w256(wpq_d[2], wpq_b[2], "cpq2", 8)
        if n_layers > 3:
            conv_wci(1)
            conv_w256(wco_d[1], wco_b[1], "cco1", 8)
            conv_w256(wpq_d[3], wpq_b[3], "cpq3", 8)

        xT_alias = []

        def build_xT(grp):
            k = 0
            for gi, c in enumerate(grp):
                for q in range(4):
                    bank = 6 + (k % 2)
                    k += 1
                    for r in range(4):
                        kc = q * 4 + r
                        P.op("pe", lambda e, c=c, kc=kc, bank=bank, r=r: e.transpose(
                            out=pb[bank][:, r * 128:(r + 1) * 128], in_=xres[:, c, kc * 128:(kc + 1) * 128], identity=identf[:]),
                            reads=[xr(c), "identf"], writes=["pb%d" % bank])
                    dst = xT[:, q * 4:(q + 1) * 4, gi * 128:(gi + 1) * 128]
                    src = pb[bank][:, :].rearrange("p (r t) -> p r t", r=4)
                    if q % 2 == 0:
                        P.op("act", lambda e, dst=dst, src=src: e.copy(out=dst, in_=src), reads=["pb%d" % bank], writes=["xT"] + xT_alias)
                    else:
                        P.op("dve", lambda e, dst=dst, src=src: e.tensor_copy(out=dst, in_=src), reads=["pb%d" % bank], writes=["xT"] + xT_alias)

        wt_alias = {0: [], 1: []}

        def load_wt(tile_ap, res):
            i = wq[0] % 2
            wq[0] += 1
            P.dma("sp", lambda e, i=i, tile_ap=tile_ap: e.dma_start(out=wt[i][:].rearrange("p k n -> p (k n)"), in_=tile_ap),
                  reads=[res], writes=["wt%d" % i] + wt_alias[i])
            return i

        def load_bcast(dst, src_row, res):
            P.dma("sp", lambda e: e.dma_start(out=dst, in_=src_row.partition_broadcast(128)), writes=[res])

        def layer_norm(c, layer, sub):
            for q in range(4):
                P.op("dve", lambda e, q=q: e.bn_stats(out=stats[:, q, :], in_=xres[:, c, q * 512:(q + 1) * 512]),
                     reads=[xr(c)], writes=["stats"])
            P.op("dve", lambda e: e.bn_aggr(out=mv[:], in_=stats[:].rearrange("p a b -> p (a b)")), reads=["stats"], writes=["mv"])
            P.op("act", lambda e: e.activation(out=rstd[:], in_=mv[:, 1:2], func=AF.Ln, bias=epsb[:, 0:1], scale=1.0),
                 reads=["mv", "epsb"], writes=["rstd"])
            P.op("act", lambda e: e.activation(out=rstd[:], in_=rstd[:], func=AF.Exp, scale=-0.5), reads=["rstd"], writes=["rstd"])
            P.op("dve", lambda e: e.tensor_scalar(out=xres[:, c, :], in0=xres[:, c, :], scalar1=mv[:, 0:1], scalar2=rstd[:, 0:1],
                                                  op0=ALU.subtract, op1=ALU.mult), reads=[xr(c), "mv", "rstd"], writes=[xr(c)])
            P.op("pool", lambda e: e.tensor_tensor(out=xres[:, c, :], in0=xres[:, c, :], in1=lnp[:, 0, :], op=ALU.mult),
                 reads=[xr(c), "lnp"], writes=[xr(c)])
            P.op("pool", lambda e: e.tensor_tensor(out=xres[:, c, :], in0=xres[:, c, :], in1=lnp[:, 1, :], op=ALU.add),
                 reads=[xr(c), "lnp"], writes=[xr(c)])

        def load_ln(layer, sub, extra=()):
            P.dma("sp", lambda e: e.dma_start(out=lnp[:, 0, :], in_=lng_d[layer, sub].partition_broadcast(128)), writes=["lnp"] + list(extra))
            P.dma("sp", lambda e: e.dma_start(out=lnp[:, 1, :], in_=lnb_d[layer, sub].partition_broadcast(128)), writes=["lnp"] + list(extra))

        def out_proj(grp, lhsT_of, lhs_res, w_ap, bias_row, btile, skip=()):
            if all(c in skip for c in grp):
                return
            k = 0
            for ct in range(8):
                wi = load_wt(w_ap[0][ct], "%s_%d" % (w_ap[1], ct))
                if bias_row is not None:
                    bi = ct % 2
                    load_bcast(btile[bi], bias_row[ct * 256:(ct + 1) * 256], "btile%d" % bi)
                for gi, c in enumerate(grp):
                    if c in skip:
                        continue
                    bank = 4 + (k % 2)
                    k += 1
                    for kc in range(16):
                        P.op("pe", lambda e, gi=gi, kc=kc, bank=bank, wi=wi: e.matmul(
                            pb[bank][:, 0:256], lhsT=lhsT_of(gi, kc), rhs=wt[wi][:, kc, :], start=(kc == 0), stop=(kc == 15)),
                            reads=[lhs_res, "wt%d" % wi], writes=["pb%d" % bank])
                    xs = xres[:, c, ct * 256:(ct + 1) * 256]
                    P.op("dve", lambda e, xs=xs, bank=bank: e.scalar_tensor_tensor(
                        out=xs, in0=xs, scalar=ALPHA, in1=pb[bank][:, 0:256], op0=ALU.mult, op1=ALU.add),
                        reads=[xr(c), "pb%d" % bank], writes=[xr(c)])
                    if bias_row is not None:
                        P.op("pool", lambda e, xs=xs, bi=bi: e.tensor_tensor(out=xs, in0=xs, in1=btile[bi], op=ALU.add),
                             reads=[xr(c), "btile%d" % bi], writes=[xr(c)])

        def attention_layer(layer):
            j = layer // 2
            P.barrier()
            a_reset()
            btile = [a_f32(256) for _ in range(2)]
            tmp = [a_f32(256) for _ in range(2)]
            qkr = [a_f32(256) for _ in range(2)]
            rt = [a_f32(128).rearrange("p (h d) -> p h d", d=32) for _ in range(4)]
            QT = a_bf16(2 * 32 * 128)[0:64, :].rearrange("p (g h t) -> p g h t", g=2, h=32)
            KT = [a_bf16(8 * 128)[0:64, :].rearrange("p (h t) -> p h t", h=8) for _ in range(4)]
            Vb = [a_bf16(512) for _ in range(4)]
            SP = [a_f32(1024).rearrange("p (h s) -> p h s", h=4)] * 2
            PT = [a_bf16(8 * 128).rearrange("p (a q) -> p a q", a=8)] * 2
            Oall = a_f32(D)
            OT = a_bf16(16 * 256).rearrange("p (k t) -> p k t", k=16)
            vf = a_f32(512)
            ckf = a_f32(512)
            cvf = vf
            qkb = [a_bf16(256) for _ in range(2)]
            ckb = a_bf16(512)
            pbb = [pb[i][:, :].bitcast(BF16) for i in range(8)]
            sinkb = a_f32(32)
            sm = a_f32(64).rearrange("p (a b) -> p a b", b=4)
            load_bcast(sinkb, sinks_d[j], "sinkb")
            load_ln(layer, 0)
            P.op("pool", lambda e: e.memset(KT[3][:], 0.0), writes=["KT3"])
            P.op("pool", lambda e: e.memset(Vb[3][:], 0.0), writes=["Vb3"])
            prev_slot = [3]
            slot_ctr = [0]

            def softmax_pv(c, gi, kt_prev, v_prev, kt_cur, v_cur, mask_idx, rscale, accumulate, tag):
                for kh in range(8):
                    sb_i = kh % 2
                    S = [pb[0 + 2 * sb_i], pb[1 + 2 * sb_i]]
                    for g4 in range(4):
                        h = kh * 4 + g4
                        bank = S[g4 // 2]
                        off = (g4 % 2) * 256
                        P.op("pe", lambda e, h=h, bank=bank, off=off: e.matmul(
                            bank[:, off:off + 128], lhsT=QT[:, gi, h, :], rhs=kt_prev[0][:, kh, :], start=True, stop=True),
                            reads=["QT", kt_prev[1]], writes=["S%d" % sb_i])
                        P.op("pe", lambda e, h=h, bank=bank, off=off: e.matmul(
                            bank[:, off + 128:off + 256], lhsT=QT[:, gi, h, :], rhs=kt_cur[0][:, kh, :], start=True, stop=True),
                            reads=["QT", kt_cur[1]], writes=["S%d" % sb_i])
                    sp = SP[sb_i]
                    spr = "SP0"
                    mk = masks[:, mask_idx, :]
                    for half in range(2):
                        P.op("dve", lambda e, half=half, sp=sp, S=S, mk=mk: e.scalar_tensor_tensor(
                            out=sp[:, 2 * half:2 * half + 2, :], in0=S[half][:, :].rearrange("p (h s) -> p h s", h=2), scalar=0.125,
                            in1=mk.unsqueeze(1).to_broadcast([128, 2, 256]), op0=ALU.mult, op1=ALU.add),
                            reads=["S%d" % sb_i, "masks"], writes=[spr])
                    st_ = "sm"
                    if debug and layer == 0 and c == NPB - 1 and kh == 0:
                        P.dma("sp", lambda e, sp=sp: e.dma_start(out=dbgSP, in_=sp.rearrange("p h s -> p (h s)")), reads=[spr])
                    P.op("dve", lambda e, sp=sp: e.tensor_reduce(out=sm[:, 0, :], in_=sp, axis=AX.X, op=ALU.max), reads=[spr], writes=[st_])
                    P.op("dve", lambda e, kh=kh: e.tensor_tensor(out=sm[:, 1, :], in0=sm[:, 0, :], in1=sinkb[:, kh * 4:kh * 4 + 4], op=ALU.max),
                         reads=[st_, "sinkb"], writes=[st_])
                    P.op("dve", lambda e: e.tensor_scalar(out=sm[:, 2, :], in0=sm[:, 1, :], scalar1=-1.0, scalar2=None, op0=ALU.mult),
                         reads=[st_], writes=[st_])
                    for g4 in range(4):
                        P.op("act", lambda e, g4=g4, sp=sp: e.activation(out=sp[:, g4, :], in_=sp[:, g4, :], func=AF.Exp,
                                                                         bias=sm[:, 2, g4:g4 + 1], scale=1.0, accum_out=sm[:, 3, g4:g4 + 1]),
                             reads=[spr, st_], writes=[spr, "smsum"])
                    P.op("dve", lambda e, kh=kh: e.tensor_tensor(out=sm[:, 4, :], in0=sinkb[:, kh * 4:kh * 4 + 4], in1=sm[:, 1, :], op=ALU.subtract),
                         reads=[st_, "sinkb"], writes=["sm4"])
                    P.op("act", lambda e: e.activation(out=sm[:, 4, :], in_=sm[:, 4, :], func=AF.Exp), reads=["sm4"], writes=["sm4"])
                    P.op("dve", lambda e: e.tensor_tensor(out=sm[:, 5, :], in0=sm[:, 4, :], in1=sm[:, 3, :], op=ALU.add),
                         reads=["sm4", "smsum"], writes=["sm5"])
                    P.op("dve", lambda e: e.reciprocal(out=sm[:, 6, :], in_=sm[:, 5, :]), reads=["sm5"], writes=["sm6"])
                    if rscale is not None:
                        P.op("dve", lambda e: e.tensor_scalar(out=sm[:, 6, :], in0=sm[:, 6, :], scalar1=rscale, scalar2=None, op0=ALU.mult),
                             reads=["sm6", "rowmask"], writes=["sm6"])
                    if debug and layer == 0 and c == NPB - 1 and kh == 0:
                        P.dma("sp", lambda e, sp=sp: e.dma_start(out=dbgP, in_=sp.rearrange("p h s -> p (h s)")), reads=[spr])
                        P.dma("sp", lambda e: e.dma_start(out=dbgSM, in_=sm.rearrange("p a b -> p (a b)")), reads=["sm", "smsum", "sm4", "sm5", "sm6"])
                    pt = PT[sb_i]
                    ptr = "PT0"
                    for g4 in range(4):
                        bank = pb[4 + (g4 // 2)]
                        for kb in range(2):
                            off = ((g4 % 2) * 2 + kb) * 128
                            P.op("pe", lambda e, g4=g4, kb=kb, bank=bank, off=off, sp=sp: e.transpose(
                                out=bank[:, off:off + 128], in_=sp[:, g4, kb * 128:(kb + 1) * 128], identity=identf[:]),
                                reads=[spr, "identf"], writes=["pb%d" % (4 + g4 // 2)])
                    P.op("act", lambda e, pt=pt: e.copy(out=pt[:, 0:4, :], in_=pb[4][:, :].rearrange("p (a q) -> p a q", a=4)),
                         reads=["pb4"], writes=[ptr])
                    P.op("dve", lambda e, pt=pt: e.tensor_copy(out=pt[:, 4:8, :], in_=pb[5][:, :].rearrange("p (a q) -> p a q", a=4)),
                         reads=["pb5"], writes=[ptr])
                    for g4 in range(4):
                        P.op("pe", lambda e, g4=g4, pt=pt: e.matmul(pb[6][:, g4 * 64:(g4 + 1) * 64], lhsT=pt[:, g4 * 2, :],
                                                                  rhs=v_prev[0][:, kh * 64:(kh + 1) * 64], start=True, stop=False),
                             reads=[ptr, v_prev[1]], writes=["pb6"])
                        P.op("pe", lambda e, g4=g4, pt=pt: e.matmul(pb[6][:, g4 * 64:(g4 + 1) * 64], lhsT=pt[:, g4 * 2 + 1, :],
                                                                  rhs=v_cur[0][:, kh * 64:(kh + 1) * 64], start=False, stop=True),
                             reads=[ptr, v_cur[1]], writes=["pb6"])
                    osl = Oall[:, kh * 256:(kh + 1) * 256].rearrange("p (g d) -> p g d", g=4)
                    opv = pb[6][:, 0:256].rearrange("p (g d) -> p g d", g=4)
                    rb = sm[:, 6, :].unsqueeze(2).to_broadcast([128, 4, 64])
                    if not accumulate:
                        P.op("dve", lambda e, osl=osl, opv=opv, rb=rb: e.tensor_tensor(out=osl, in0=opv, in1=rb, op=ALU.mult),
                             reads=["pb6", "sm6"], writes=["Oall"])
                    else:
                        t4 = tmp[0].rearrange("p (g d) -> p g d", g=4)
                        P.op("dve", lambda e, t4=t4, opv=opv, rb=rb: e.tensor_tensor(out=t4, in0=opv, in1=rb, op=ALU.mult),
                             reads=["pb6", "sm6"], writes=["tmp0"])
                        P.op("dve", lambda e, t4=t4, osl=osl: e.tensor_tensor(out=osl, in0=osl, in1=t4, op=ALU.add),
                             reads=["tmp0", "Oall"], writes=["Oall"])

            kv_only = {0} if layer == 0 else {1}
            groups_l = GROUPS if layer == 0 else [[1]] + GROUPS[1:]
            for grp in groups_l:
                build_xT(grp)
                cur_slots = []
                for gi, c in enumerate(grp):
                    s_ = slot_ctr[0] % 3
                    slot_ctr[0] += 1
                    cur_slots.append(s_)
                k = 0
                for ct in range(12):
                    wi = load_wt(wqkv_b[j, ct], "cqkv%d_%d" % (j, ct))
                    bi = ct % 2
                    load_bcast(btile[bi], bqkv_d[j, ct * 256:(ct + 1) * 256], "btile%d" % bi)
                    for gi, c in enumerate(grp):
                        bank = 4 + (k % 2)
                        k += 1
                        for kc in range(16):
                            P.op("pe", lambda e, gi=gi, kc=kc, bank=bank, wi=wi: e.matmul(
                                pb[bank][:, 0:256], lhsT=xT[:, kc, gi * 128:(gi + 1) * 128], rhs=wt[wi][:, kc, :],
                                start=(kc == 0), stop=(kc == 15)), reads=["xT", "wt%d" % wi], writes=["pb%d" % bank])
                        ti = k % 2
                        if ct < 10:
                            P.op("dve", lambda e, ti=ti, bank=bank, bi=bi: e.tensor_tensor(out=tmp[ti], in0=pb[bank][:, 0:256], in1=btile[bi], op=ALU.add),
                                 reads=["pb%d" % bank, "btile%d" % bi], writes=["tmp%d" % ti])
                            tv = tmp[ti].rearrange("p (h two d) -> p h two d", h=4, two=2)
                            x1 = tv[:, :, 0, :]
                            x2 = tv[:, :, 1, :]
                            cosb = cs[:, c, 0:32].unsqueeze(1).to_broadcast([128, 4, 32])
                            sinb = cs[:, c, 32:64].unsqueeze(1).to_broadcast([128, 4, 32])
                            qv = qkr[ti].rearrange("p (h two d) -> p h two d", h=4, two=2)
                            rr = ["rt0", "rt1", "rt2", "rt3"]
                            P.op("pool", lambda e, x1=x1, cosb=cosb: e.tensor_tensor(out=rt[0], in0=x1, in1=cosb, op=ALU.mult), reads=["tmp%d" % ti, "cs"], writes=[rr[0]])
                            P.op("pool", lambda e, x2=x2, sinb=sinb: e.tensor_tensor(out=rt[1], in0=x2, in1=sinb, op=ALU.mult), reads=["tmp%d" % ti, "cs"], writes=[rr[1]])
                            P.op("dve", lambda e, x2=x2, cosb=cosb: e.tensor_tensor(out=rt[2], in0=x2, in1=cosb, op=ALU.mult), reads=["tmp%d" % ti, "cs"], writes=[rr[2]])
                            P.op("dve", lambda e, x1=x1, sinb=sinb: e.tensor_tensor(out=rt[3], in0=x1, in1=sinb, op=ALU.mult), reads=["tmp%d" % ti, "cs"], writes=[rr[3]])
                            P.op("pool", lambda e, qv=qv: e.tensor_tensor(out=qv[:, :, 0, :], in0=rt[0], in1=rt[1], op=ALU.subtract),
                                 reads=[rr[0], rr[1]], writes=["qkr%d" % ti])
                            P.op("dve", lambda e, qv=qv: e.tensor_tensor(out=qv[:, :, 1, :], in0=rt[2], in1=rt[3], op=ALU.add),
                                 reads=[rr[2], rr[3]], writes=["qkr%d" % ti])
                            tb = 6 + (k % 2)
                            P.op("act", lambda e, ti=ti: e.copy(out=qkb[ti], in_=qkr[ti]), reads=["qkr%d" % ti], writes=["qkb%d" % ti])
                            for hh in range(4):
                                P.op("pe", lambda e, hh=hh, tb=tb, ti=ti: e.transpose(
                                    out=pbb[tb][0:64, hh * 128:(hh + 1) * 128], in_=qkb[ti][:, hh * 64:(hh + 1) * 64], identity=identb[:]),
                                    reads=["qkb%d" % ti, "identb"], writes=["pb%d" % tb])
                            src = pbb[tb][0:64, 0:512].rearrange("p (h t) -> p h t", h=4)
                            if ct < 8:
                                P.op("act", lambda e, src=src, gi=gi, ct=ct: e.copy(out=QT[:, gi, ct * 4:(ct + 1) * 4, :], in_=src),
                                     reads=["pb%d" % tb], writes=["QT"])
                            else:
                                s_ = cur_slots[gi]
                                hb = (ct - 8) * 4
                                P.op("act", lambda e, src=src, s_=s_, hb=hb: e.copy(out=KT[s_][:, hb:hb + 4, :], in_=src),
                                     reads=["pb%d" % tb], writes=["KT%d" % s_])
                                if c == NPB - 1:
                                    P.dma("sp", lambda e, ti=ti, hb=hb: e.dma_start(out=kp_o[j, :, hb * 64:(hb + 4) * 64], in_=qkr[ti]),
                                          reads=["qkr%d" % ti])
                                if c == SCH:
                                    for s in range(4):
                                        P.dma("sp", lambda e, ti=ti, hb=hb, s=s: e.dma_start(
                                            out=ks_o[j, s, 124:128, hb * 64:(hb + 4) * 64], in_=qkr[ti][4 * s:4 * s + 4, :]),
                                            reads=["qkr%d" % ti])
                        else:
                            s_ = cur_slots[gi]
                            vo = (ct - 10) * 256
                            P.op("dve", lambda e, bank=bank, bi=bi, vo=vo: e.tensor_tensor(out=vf[:, vo:vo + 256], in0=pb[bank][:, 0:256], in1=btile[bi], op=ALU.add),
                                 reads=["pb%d" % bank, "btile%d" % bi], writes=["vf"])
                            P.op("act", lambda e, s_=s_, vo=vo: e.copy(out=Vb[s_][:, vo:vo + 256], in_=vf[:, vo:vo + 256]),
                                 reads=["vf"], writes=["Vb%d" % s_])
                            if c == NPB - 1:
                                P.dma("sp", lambda e, vo=vo: e.dma_start(out=vp_o[j, :, vo:vo + 256], in_=vf[:, vo:vo + 256]), reads=["vf"])
                            if c == SCH:
                                for s in range(4):
                                    P.dma("sp", lambda e, vo=vo, s=s: e.dma_start(out=vs_o[j, s, 124:128, vo:vo + 256], in_=vf[4 * s:4 * s + 4, vo:vo + 256]),
                                          reads=["vf"])
                for gi, c in enumerate(grp):
                    s_ = cur_slots[gi]
                    ktc = (KT[s_], "KT%d" % s_)
                    vc = (Vb[s_], "Vb%d" % s_)
                    if c in kv_only:
                        prev_slot[0] = s_
                        continue
                    if c != SCH:
                        ps_ = prev_slot[0]
                        ktp = (KT[ps_], "KT%d" % ps_)
                        vp_ = (Vb[ps_], "Vb%d" % ps_)
                        softmax_pv(c, gi, ktp, vp_, ktc, vc, 2 if c == FIRST_OWN else 0, None, False, "p")
                        prev_slot[0] = s_
                    else:
                        for s in range(4):
                            P.dma("sp", lambda e, s=s: e.dma_start(out=ks_o[j, s, 0:124, :], in_=ck_d[j, s, 4:128, :]))
                            P.dma("sp", lambda e, s=s: e.dma_start(out=vs_o[j, s, 0:124, :], in_=cv_d[j, s, 4:128, :]))
                            P.dma("sp", lambda e, s=s: e.dma_start(out=ckf, in_=ck_d[j, s]), writes=["ckf"])
                            P.dma("sp", lambda e, s=s: e.dma_start(out=cvf, in_=cv_d[j, s]), writes=["vf"])
                            P.op("act", lambda e: e.copy(out=ckb, in_=ckf), reads=["ckf"], writes=["ckb"])
                            for half in range(2):
                                for hh in range(4):
                                    P.op("pe", lambda e, hh=hh, half=half: e.transpose(
                                        out=pbb[7][0:64, hh * 128:(hh + 1) * 128], in_=ckb[:, (half * 4 + hh) * 64:(half * 4 + hh + 1) * 64], identity=identb[:]),
                                        reads=["ckb", "identb"], writes=["pb7"])
                                P.op("act", lambda e, half=half: e.copy(out=KT[3][:, half * 4:half * 4 + 4, :],
                                                                      in_=pbb[7][0:64, 0:512].rearrange("p (h t) -> p h t", h=4)),
                                     reads=["pb7"], writes=["KT3"])
                            P.op("act", lambda e: e.copy(out=Vb[3][:], in_=cvf), reads=["vf"], writes=["Vb3"])
                            softmax_pv(c, gi, (KT[3], "KT3"), (Vb[3], "Vb3"), ktc, vc, 1, rowmask[:, s:s + 1], s > 0, "s")
                    if debug and layer == 0 and c == NPB - 1:
                        P.dma("sp", lambda e: e.dma_start(out=dbgO, in_=Oall), reads=["Oall"])
                    for q in range(4):
                        bank = 4 + (q % 2)
                        for r in range(4):
                            kc = q * 4 + r
                            P.op("pe", lambda e, kc=kc, bank=bank, r=r: e.transpose(
                                out=pb[bank][:, r * 128:(r + 1) * 128], in_=Oall[:, kc * 128:(kc + 1) * 128], identity=identf[:]),
                                reads=["Oall", "identf"], writes=["pb%d" % bank])
                        P.op("act", lambda e, q=q, bank=bank, gi=gi: e.copy(out=OT[:, q * 4:(q + 1) * 4, gi * 128:(gi + 1) * 128],
                                                                          in_=pb[bank][:, :].rearrange("p (r t) -> p r t", r=4)),
                             reads=["pb%d" % bank], writes=["OT"])
                out_proj(grp, lambda gi, kc: OT[:, kc, gi * 128:(gi + 1) * 128], "OT", (wo_b[j], "co%d" % j), bo_d[j], btile, skip=kv_only)
                for c in grp:
                    if c not in kv_only:
                        layer_norm(c, layer, 0)

        def conv_layer(layer):
            j = layer // 2
            P.barrier()
            a_reset()
            w3 = [a_bf16(16 * 3 * 128).rearrange("p (k a m) -> p k a m", k=16, a=3) for _ in range(2)]
            hsb = a_f32(256)
            ubuf = [a_f32(260) for _ in range(2)]
            ctmp = [a_f32(256) for _ in range(2)]
            zT = a_bf16(16 * 256).rearrange("p (k t) -> p k t", k=16)
            ucarry = a_f32(32).rearrange("p (k t) -> p k t", t=2)
            cw = a_f32(48).rearrange("p (k t) -> p k t", t=3)
            stT = a_f32(128).rearrange("p (k s) -> p k s", s=8)
            strow = a_f32(D)
            ubs = [a_f32(24).rearrange("p (s t) -> p s t", t=6) for _ in range(2)]
            usout = a_f32(128).rearrange("p (k s t) -> p k s t", s=4, t=2)
            utr = a_f32(128)
            utro = a_f32(128)
            load_ln(layer, 0)
            P.dma("sp", lambda e: e.dma_start(out=cw, in_=cwT_d[j]), writes=["cw"])
            P.op("pool", lambda e: e.memset(ucarry, 0.0), writes=["ucarry"])
            P.dma("sp", lambda e: e.dma_start(out=strow[0:8, :], in_=stc_d[j]), writes=["strow"])
            for q in range(4):
                for r in range(4):
                    kc = q * 4 + r
                    P.op("pe", lambda e, kc=kc, r=r: e.transpose(out=pb[7][:, r * 8:(r + 1) * 8], in_=strow[0:8, kc * 128:(kc + 1) * 128],
                                                                identity=identf[0:8, 0:8]), reads=["strow", "identf"], writes=["pb7"])
                P.op("act", lambda e, q=q: e.copy(out=stT[:, q * 4:(q + 1) * 4, :], in_=pb[7][:, 0:32].rearrange("p (r s) -> p r s", r=4)),
                     reads=["pb7"], writes=["stT"])
            wk = [0]
            u_only = set() if layer == 1 else {2}
            groups_l = ([[1]] + GROUPS[1:]) if layer == 1 else ([[2], [3]] + GROUPS[2:])
            for grp in groups_l:
                build_xT(grp)
                nt = 128 * len(grp)
                sample = (grp[0] == SCH)
                if sample:
                    P.op("pool", lambda e: e.memset(zT[:], 0.0), writes=["zT"])
                for cc in range(16):
                    wi = wk[0] % 2
                    wk[0] += 1
                    P.dma("sp", lambda e, wi=wi, cc=cc: e.dma_start(out=w3[wi].rearrange("p k a m -> p (k a m)"), in_=wci_b[j, cc]),
                          reads=["cwci%d_%d_%d" % (j, cc, a) for a in range(3)], writes=["w3_%d" % wi])
                    for a in range(3):
                        for kc in range(16):
                            P.op("pe", lambda e, a=a, kc=kc, wi=wi: e.matmul(pb[a][:, 0:nt], lhsT=w3[wi][:, kc, a, :], rhs=xT[:, kc, 0:nt],
                                                                            start=(kc == 0), stop=(kc == 15)),
                                 reads=["xT", "w3_%d" % wi], writes=["pb%d" % a])
                    ui = cc % 2
                    P.op("act", lambda e: e.copy(out=hsb[:, 0:nt], in_=pb[2][:, 0:nt]), reads=["pb2"], writes=["hsb"])
                    if not sample:
                        ub = ubuf[ui]
                        ur = "ubuf%d" % ui
                        P.op("pool", lambda e, ub=ub, cc=cc: e.tensor_copy(out=ub[:, 0:2], in_=ucarry[:, cc, :]), reads=["ucarry"], writes=[ur])
                        P.op("dve", lambda e, ub=ub: e.tensor_tensor(out=ub[:, 2:2 + nt], in0=pb[1][:, 0:nt], in1=hsb[:, 0:nt], op=ALU.mult),
                             reads=["pb1", "hsb"], writes=[ur])
                        if 2 in grp:
                            o2 = 2 + grp.index(2) * 128 + 126
                            P.op("dve", lambda e, ub=ub, o2=o2: e.tensor_scalar(out=ub[:, o2:o2 + 2], in0=ub[:, o2:o2 + 2], scalar1=hv[:, 0:1], scalar2=None, op0=ALU.mult),
                                 reads=[ur, "hv"], writes=[ur])
                        P.op("pool", lambda e, ub=ub, cc=cc: e.tensor_copy(out=ucarry[:, cc, :], in_=ub[:, nt:nt + 2]), reads=[ur], writes=["ucarry"])
                        ct_ = ctmp[ui]
                        cr = "ctmp%d" % ui
                        P.op("dve", lambda e, ub=ub, ct_=ct_, cc=cc: e.tensor_scalar(out=ct_[:, 0:nt], in0=ub[:, 0:nt], scalar1=cw[:, cc, 0:1], scalar2=None, op0=ALU.mult),
                             reads=[ur, "cw"], writes=[cr])
                        P.op("dve", lambda e, ub=ub, ct_=ct_, cc=cc: e.scalar_tensor_tensor(out=ct_[:, 0:nt], in0=ub[:, 1:1 + nt], scalar=cw[:, cc, 1:2], in1=ct_[:, 0:nt],
                                                                                      op0=ALU.mult, op1=ALU.add), reads=[ur, "cw", cr], writes=[cr])
                        P.op("dve", lambda e, ub=ub, ct_=ct_, cc=cc: e.scalar_tensor_tensor(out=ct_[:, 0:nt], in0=ub[:, 2:2 + nt], scalar=cw[:, cc, 2:3], in1=ct_[:, 0:nt],
                                                                                      op0=ALU.mult, op1=ALU.add), reads=[ur, "cw", cr], writes=[cr])
                        P.op("dve", lambda e, ct_=ct_, cc=cc: e.tensor_tensor(out=zT[:, cc, 0:nt], in0=pb[0][:, 0:nt], in1=ct_[:, 0:nt], op=ALU.mult),
                             reads=["pb0", cr], writes=["zT"])
                    else:
                        ub = ubs[ui]
                        ur = "ubs%d" % ui
                        P.op("pool", lambda e, ub=ub, cc=cc: e.tensor_copy(out=ub[:, :, 0:2], in_=stT[:, cc, :].rearrange("p (s t) -> p s t", t=2)),
                             reads=["stT"], writes=[ur])
                        P.op("dve", lambda e, ub=ub: e.tensor_tensor(out=ub[:, :, 2:6], in0=pb[1][:, 0:16].rearrange("p (s t) -> p s t", t=4),
                                                                     in1=hsb[:, 0:16].rearrange("p (s t) -> p s t", t=4), op=ALU.mult),
                             reads=["pb1", "hsb"], writes=[ur])
                        P.op("pool", lambda e, ub=ub, cc=cc: e.tensor_copy(out=usout[:, cc, :, :], in_=ub[:, :, 4:6]), reads=[ur], writes=["usout"])
                        ct_ = ctmp[ui][:, 0:16].rearrange("p (s t) -> p s t", t=4)
                        cr = "ctmp%d" % ui
                        P.op("dve", lambda e, ub=ub, ct_=ct_, cc=cc: e.tensor_scalar(out=ct_, in0=ub[:, :, 0:4], scalar1=cw[:, cc, 0:1], scalar2=None, op0=ALU.mult),
                             reads=[ur, "cw"], writes=[cr])
                        P.op("dve", lambda e, ub=ub, ct_=ct_, cc=cc: e.scalar_tensor_tensor(out=ct_, in0=ub[:, :, 1:5], scalar=cw[:, cc, 1:2], in1=ct_,
                                                                                      op0=ALU.mult, op1=ALU.add), reads=[ur, "cw", cr], writes=[cr])
                        P.op("dve", lambda e, ub=ub, ct_=ct_, cc=cc: e.scalar_tensor_tensor(out=ct_, in0=ub[:, :, 2:6], scalar=cw[:, cc, 2:3], in1=ct_,
                                                                                      op0=ALU.mult, op1=ALU.add), reads=[ur, "cw", cr], writes=[cr])
                        P.op("dve", lambda e, ct_=ct_, cc=cc: e.tensor_tensor(out=zT[:, cc, 0:16].rearrange("p (s t) -> p s t", t=4),
                                                                             in0=pb[0][:, 0:16].rearrange("p (s t) -> p s t", t=4), in1=ct_, op=ALU.mult),
                             reads=["pb0", cr], writes=["zT"])
                if grp[-1] == NPB - 1:
                    P.op("pool", lambda e: e.memset(utr, 0.0), writes=["utr"])
                    P.op("pool", lambda e: e.tensor_copy(out=utr[:, 0:32].rearrange("p (t k) -> p t k", t=2), in_=ucarry.rearrange("p k t -> p t k")),
                         reads=["ucarry"], writes=["utr"])
                    P.op("pe", lambda e: e.transpose(out=pb[3][:, 0:128], in_=utr, identity=identf[:]), reads=["utr", "identf"], writes=["pb3"])
                    P.op("act", lambda e: e.copy(out=utro, in_=pb[3][:, 0:128]), reads=["pb3"], writes=["utro"])
                    P.dma("sp", lambda e: e.dma_start(out=cp_o[j].rearrange("t (k p) -> (t k) p", p=128), in_=utro[0:32, :]), reads=["utro"])
                if sample:
                    P.op("pool", lambda e: e.tensor_copy(out=utr.rearrange("p (s t k) -> p s t k", s=4, t=2), in_=usout.rearrange("p k s t -> p s t k")),
                         reads=["usout"], writes=["utr"])
                    P.op("pe", lambda e: e.transpose(out=pb[3][:, 0:128], in_=utr, identity=identf[:]), reads=["utr", "identf"], writes=["pb3"])
                    P.op("act", lambda e: e.copy(out=utro, in_=pb[3][:, 0:128]), reads=["pb3"], writes=["utro"])
                    P.dma("sp", lambda e: e.dma_start(out=cso_o[j].rearrange("s t (k p) -> (s t k) p", p=128), in_=utro), reads=["utro"])
                out_proj(grp, lambda gi, kc: zT[:, kc, gi * 128:(gi + 1) * 128], "zT", (wco_b[j], "cco%d" % j), None, None, skip=u_only)
                for c in grp:
                    if c not in u_only:
                        layer_norm(c, layer, 0)

        def peer_layer(layer):
            P.barrier()
            a_reset()
            qT = a_bf16(16 * 256).rearrange("p (k t) -> p k t", k=16)
            keysT = a_bf16(16 * 128).rearrange("p (k n) -> p k n", k=16)
            big = a_f32(2048)
            sc = a_f32(2048)
            work = a_f32(512)
            sv = a_f32(256).rearrange("p (a k) -> p a k", k=16)
            si = a_f32(256).bitcast(U32).rearrange("p (a k) -> p a k", k=16)
            sif = a_f32(256).rearrange("p (a k) -> p a k", k=16)
            fv = a_f32(128).rearrange("p (h k) -> p h k", k=16)
            fpu = a_f32(128).bitcast(U32).rearrange("p (h k) -> p h k", k=16)
            k1u = a_f32(128).bitcast(U32).rearrange("p (h k) -> p h k", k=16)
            k2u = a_f32(128).bitcast(U32).rearrange("p (h k) -> p h k", k=16)
            k1f = a_f32(128).rearrange("p (h k) -> p h k", k=16)
            k2f = a_f32(128).rearrange("p (h k) -> p h k", k=16)
            i1f = a_f32(128).rearrange("p (h k) -> p h k", k=16)
            i2f = a_f32(128).rearrange("p (h k) -> p h k", k=16)
            eidx = a_f32(128).bitcast(I32)
            gate = a_f32(128).rearrange("p (h k) -> p h k", k=16)
            gsm = a_f32(16).rearrange("p (a h) -> p a h", h=8)
            hid = a_f32(128)
            ug = [a_bf16(2 * D) for _ in range(2)]
            ug += [wt[0][:, :, :].rearrange("p k n -> p (k n)"), wt[1][:, :, :].rearrange("p k n -> p (k n)"),
                   xT[:, :, :].rearrange("p k n -> p (k n)"), sc.bitcast(BF16), big.bitcast(BF16),
                   lnp[:, 0, :].bitcast(BF16), lnp[:, 1, :].bitcast(BF16)]
            NG = len(ug)
            NPRIV = 2
            wt_alias[0] = ["ug2"]
            wt_alias[1] = ["ug3"]
            xT_alias[:] = ["ug4"]
            SC_AL = ["ug5"]
            BIG_AL = ["ug6"]
            LNP_AL = ["ug7", "ug8"]
            junk = a_bf16(D)
            xb = a_bf16(D)
            diag = [a_bf16(128) for _ in range(4)]
            dgf = [a_f32(128) for _ in range(2)] * 2
            if layer == 0:
                for b_ in range(NPRIV):
                    P.op("pool", lambda e, b_=b_: e.memset(ug[b_], 0.0), writes=["ug%d" % b_])
            for half in range(2):
                P.dma("sp", lambda e, half=half: e.dma_start(out=big.rearrange("p (a d) -> p a d", a=16)[:, half * 8:(half + 1) * 8, :],
                                                          in_=psk_d[layer, half * 8:(half + 1) * 8].rearrange("a n d -> n a d")), writes=["big"] + BIG_AL)
            for q in range(4):
                for r in range(4):
                    hp = q * 4 + r
                    P.op("pe", lambda e, hp=hp, r=r: e.transpose(out=pb[7][:, r * 128:(r + 1) * 128], in_=big[:, hp * 128:(hp + 1) * 128], identity=identf[:]),
                         reads=["big", "identf"], writes=["pb7"])
                P.op("act", lambda e, q=q: e.copy(out=keysT[:, q * 4:(q + 1) * 4, :], in_=pb[7][:, :].rearrange("p (r n) -> p r n", r=4)),
                     reads=["pb7"], writes=["keysT"])
            pending = table_conv_ops(layer + 1) if layer + 1 < n_layers else []
            for grp in GROUPS:
                if all(c < PEER_MIN_CHUNK[layer] for c in grp):
                    continue
                build_xT(grp)
                nt = 128 * len(grp)
                for ct in range(8):
                    wi = load_wt(wpq_b[layer, ct], "cpq%d_%d" % (layer, ct))
                    for m in range(2):
                        hp = ct * 2 + m
                        bank = 4 + (hp % 2)
                        for kc in range(16):
                            P.op("pe", lambda e, kc=kc, m=m, bank=bank, wi=wi: e.matmul(pb[bank][:, 0:nt], lhsT=wt[wi][:, kc, m * 128:(m + 1) * 128],
                                                                                      rhs=xT[:, kc, 0:nt], start=(kc == 0), stop=(kc == 15)),
                                 reads=["xT", "wt%d" % wi], writes=["pb%d" % bank])
                        P.op("act", lambda e, hp=hp, bank=bank: e.copy(out=qT[:, hp, 0:nt], in_=pb[bank][:, 0:nt]), reads=["pb%d" % bank], writes=["qT"])
                for gi, c in enumerate(grp):
                    if c < PEER_MIN_CHUNK[layer]:
                        continue
                    for _ in range(2):
                        if pending:
                            pending.pop(0)()
                    npart = 16 if c == SCH else 128
                    ub_res = ["ub%d_%d" % (layer, r) for r in range(8)]
                    vb_res = ["vb%d_%d" % (layer, r) for r in range(8)]
                    for hp in range(16):
                        bank = hp // 4
                        off = (hp % 4) * 128
                        P.op("pe", lambda e, hp=hp, bank=bank, off=off: e.matmul(pb[bank][:, off:off + 128], lhsT=qT[:, hp, gi * 128:(gi + 1) * 128],
                                                                               rhs=keysT[:, hp, :], start=True, stop=True),
                             reads=["qT", "keysT"], writes=["pb%d" % bank])
                    for b4 in range(4):
                        P.op("act", lambda e, b4=b4: e.copy(out=sc[:, b4 * 512:(b4 + 1) * 512], in_=pb[b4][:, :]), reads=["pb%d" % b4], writes=["sc"] + SC_AL)
                    if debug and layer == 0 and c == NPB - 1:
                        P.dma("sp", lambda e: e.dma_start(out=dbgS, in_=sc), reads=["sc"])
                    SVN = ["sv%d" % i for i in range(16)]
                    SIN = ["si%d" % i for i in range(16)]
                    for q4 in range(4):
                        hps = [q4 * 4 + i for i in range(4)]
                        wk_ = {hp: work[:, (hp % 4) * 128:(hp % 4 + 1) * 128] for hp in hps}
                        for hp in hps:
                            P.op("dve", lambda e, hp=hp: e.max(out=sv[:, hp, 0:8], in_=sc[:, hp * 128:(hp + 1) * 128]), reads=["sc"], writes=[SVN[hp]])
                        for hp in hps:
                            P.op("dve", lambda e, hp=hp: e.max_index(out=si[:, hp, 0:8], in_max=sv[:, hp, 0:8], in_values=sc[:, hp * 128:(hp + 1) * 128]),
                                 reads=["sc", SVN[hp]], writes=[SIN[hp]])
                        for hp in hps:
                            P.op("dve", lambda e, hp=hp: e.match_replace(out=wk_[hp], in_to_replace=sv[:, hp, 0:8], in_values=sc[:, hp * 128:(hp + 1) * 128], imm_value=NEG),
                                 reads=["sc", SVN[hp]], writes=["work%d" % (hp % 4)])
                        for hp in hps:
                            P.op("dve", lambda e, hp=hp: e.max(out=sv[:, hp, 8:16], in_=wk_[hp]), reads=["work%d" % (hp % 4)], writes=[SVN[hp]])
                        for hp in hps:
                            P.op("dve", lambda e, hp=hp: e.max_index(out=si[:, hp, 8:16], in_max=sv[:, hp, 8:16], in_values=wk_[hp]),
                                 reads=["work%d" % (hp % 4), SVN[hp]], writes=[SIN[hp]])
                    P.op("dve", lambda e: e.tensor_copy(out=sif, in_=si), reads=SIN, writes=["sif"])
                    svv = sv.rearrange("p (h two) k -> p h two k", two=2)
                    cand = big.rearrange("p (h a b) -> p h a b", h=8, a=16)
                    P.op("dve", lambda e, svv=svv, cand=cand: e.tensor_tensor(out=cand, in0=svv[:, :, 0, :].unsqueeze(3).to_broadcast([128, 8, 16, 16]),
                                                                            in1=svv[:, :, 1, :].unsqueeze(2).to_broadcast([128, 8, 16, 16]), op=ALU.add),
                         reads=SVN, writes=["big"] + BIG_AL)
                    FVN = ["fv%d" % i for i in range(8)]
                    FPN = ["fp%d" % i for i in range(8)]
                    for q2 in range(4):
                        hs = [q2 * 2, q2 * 2 + 1]
                        wk2 = {h: work[:, (h % 2) * 256:(h % 2 + 1) * 256] for h in hs}
                        wn2 = {h: ["work%d" % ((h % 2) * 2), "work%d" % ((h % 2) * 2 + 1)] for h in hs}
                        for h in hs:
                            P.op("dve", lambda e, h=h: e.max(out=fv[:, h, 0:8], in_=big[:, h * 256:(h + 1) * 256]), reads=["big"], writes=[FVN[h]])
                        for h in hs:
                            P.op("dve", lambda e, h=h: e.max_index(out=fpu[:, h, 0:8], in_max=fv[:, h, 0:8], in_values=big[:, h * 256:(h + 1) * 256]),
                                 reads=["big", FVN[h]], writes=[FPN[h]])
                        for h in hs:
                            P.op("dve", lambda e, h=h: e.match_replace(out=wk2[h], in_to_replace=fv[:, h, 0:8], in_values=big[:, h * 256:(h + 1) * 256], imm_value=NEG),
                                 reads=["big", FVN[h]], writes=wn2[h])
                        for h in hs:
                            P.op("dve", lambda e, h=h: e.max(out=fv[:, h, 8:16], in_=wk2[h]), reads=wn2[h], writes=[FVN[h]])
                        for h in hs:
                            P.op("dve", lambda e, h=h: e.max_index(out=fpu[:, h, 8:16], in_max=fv[:, h, 8:16], in_values=wk2[h]),
                                 reads=wn2[h] + [FVN[h]], writes=[FPN[h]])
                    P.op("dve", lambda e: e.tensor_single_scalar(out=k1u, in_=fpu, scalar=4, op=ALU.logical_shift_right), reads=FPN, writes=["k1u"])
                    P.op("dve", lambda e: e.tensor_single_scalar(out=k2u, in_=fpu, scalar=15, op=ALU.bitwise_and), reads=FPN, writes=["k2u"])
                    P.op("dve", lambda e: e.tensor_copy(out=k1f, in_=k1u), reads=["k1u"], writes=["k1f"])
                    P.op("dve", lambda e: e.tensor_copy(out=k2f, in_=k2u), reads=["k2u"], writes=["k2f"])
                    oh = sc.rearrange("p (h a b) -> p h a b", h=8, a=16)
                    siv = sif.rearrange("p (h two) k -> p h two k", two=2)
                    io = iota16[:, :].unsqueeze(1).unsqueeze(1).to_broadcast([128, 8, 16, 16])
                    for which, kf, dst in ((0, k1f, i1f), (1, k2f, i2f)):
                        P.op("dve", lambda e, kf=kf, oh=oh, io=io: e.tensor_tensor(out=oh, in0=kf.unsqueeze(3).to_broadcast([128, 8, 16, 16]), in1=io, op=ALU.is_equal),
                             reads=["k1f", "k2f", "iota16", "sc"], writes=["sc"])
                        P.op("dve", lambda e, which=which, oh=oh, siv=siv: e.tensor_tensor(out=oh, in0=oh, in1=siv[:, :, which, :].unsqueeze(2).to_broadcast([128, 8, 16, 16]), op=ALU.mult),
                             reads=["sc", "sif"], writes=["sc"])
                        P.op("dve", lambda e, oh=oh, dst=dst: e.tensor_reduce(out=dst, in_=oh, axis=AX.X, op=ALU.add), reads=["sc"], writes=["i12"])
                    P.op("dve", lambda e: e.scalar_tensor_tensor(out=i1f, in0=i1f, scalar=128.0, in1=i2f, op0=ALU.mult, op1=ALU.add), reads=["i12"], writes=["i12"])
                    P.op("dve", lambda e: e.tensor_scalar(out=i1f, in0=i1f, scalar1=0.0, scalar2=None, op0=ALU.add), reads=["i12"], writes=["i12"])
                    P.op("dve", lambda e: e.tensor_copy(out=eidx, in_=i1f.rearrange("p h k -> p (h k)")), reads=["i12"], writes=["eidx"])
                    P.op("dve", lambda e: e.tensor_tensor(out=gate, in0=fv, in1=fv[:, :, 0:1].to_broadcast([128, 8, 16]), op=ALU.subtract), reads=FVN, writes=["gate"])
                    P.op("act", lambda e: e.activation(out=gate, in_=gate, func=AF.Exp), reads=["gate"], writes=["gate"])
                    P.op("dve", lambda e: e.tensor_reduce(out=gsm[:, 0, :], in_=gate, axis=AX.X, op=ALU.add), reads=["gate"], writes=["gsm"])
                    P.op("dve", lambda e: e.reciprocal(out=gsm[:, 1, :], in_=gsm[:, 0, :]), reads=["gsm"], writes=["gsm"])
                    P.op("dve", lambda e: e.tensor_tensor(out=gate, in0=gate, in1=gsm[:, 1, :].unsqueeze(2).to_broadcast([128, 8, 16]), op=ALU.mult),
                         reads=["gate", "gsm"], writes=["gate"])
                    P.op("act", lambda e: e.copy(out=xb, in_=xres[:, c, :]), reads=[xr(c)], writes=["xb"])
                    uv_res = ["ub%d_%d" % (layer, r) for r in range(8)] + ["vb%d_%d" % (layer, r) for r in range(8)]
                    gflat = gate.rearrange("p h k -> p (h k)")
                    SKEW = 0
                    tails = []

                    def make_tail(jj, b, dgi):
                        def tail():
                            P.op("act", lambda e: e.activation(out=diag[dgi], in_=dgf[dgi], func=AF.Copy, scale=gflat[:, jj:jj + 1]),
                                 reads=["dgf%d" % dgi, "gate"], writes=["diag%d" % dgi])
                            for q in range(4):
                                P.op("pe", lambda e, q=q: e.matmul(pb[q][:, :], lhsT=diag[dgi], rhs=ug[b][:, D + q * 512:D + (q + 1) * 512],
                                                                  start=(jj == 0), stop=(jj == 127)),
                                     reads=["diag%d" % dgi, "ug%d" % b], writes=["pb%d" % q])
                        return tail

                    for jj in range(128):
                        b = jj % (NPRIV if c == SCH else NG)
                        dgi = jj % 4
                        if SKEW and len(tails) >= SKEW:
                            tails.pop(0)()
                        P.dma("pool", lambda e, jj=jj, b=b: e.indirect_dma_start(out=ug[b][0:npart, :], out_offset=None,
                                                                             in_=uv_l[layer].rearrange("e w d -> e (w d)"),
                                                                             in_offset=bass.IndirectOffsetOnAxis(ap=eidx[0:npart, jj:jj + 1], axis=0)),
                              reads=["eidx"] + uv_res, writes=["ug%d" % b])
                        P.op("dve", lambda e, jj=jj, b=b: e.scalar_tensor_tensor(out=junk, in0=ug[b][:, 0:D], scalar=1.0, in1=xb, op0=ALU.mult, op1=ALU.mult,
                                                                              accum_out=hid[:, jj:jj + 1]),
                             reads=["ug%d" % b, "xb"], writes=["junk", "hid%d" % (jj % 8)])
                        P.op("act", lambda e, jj=jj, dgi=dgi: e.activation(out=dgf[dgi], in_=identf[:], func=AF.Gelu, scale=hid[:, jj:jj + 1]),
                             reads=["hid%d" % (jj % 8), "identf"], writes=["dgf%d" % dgi])
                        tails.append(make_tail(jj, b, dgi))
                        if not SKEW:
                            tails.pop(0)()
                    while tails:
                        tails.pop(0)()
                    if debug and layer == 0 and c == NPB - 1:
                        P.dma("sp", lambda e: e.dma_start(out=dbgE, in_=eidx), reads=["eidx"])
                        P.dma("sp", lambda e: e.dma_start(out=dbgG, in_=gate.rearrange("p h k -> p (h k)")), reads=["gate"])
                        P.dma("sp", lambda e: e.dma_start(out=dbgH, in_=hid), reads=["hid%d" % i for i in range(8)])
                    for q in range(4):
                        xs = xres[:, c, q * 512:(q + 1) * 512]
                        P.op("dve", lambda e, xs=xs, q=q: e.scalar_tensor_tensor(out=xs, in0=xs, scalar=ALPHA, in1=pb[q][:, :], op0=ALU.mult, op1=ALU.add),
                             reads=[xr(c), "pb%d" % q], writes=[xr(c)])
                    load_ln(layer, 1, LNP_AL)
                    layer_norm(c, layer, 1)
            while pending:
                pending.pop(0)()
            wt_alias[0] = []
            wt_alias[1] = []
            xT_alias[:] = []

        for layer in range(n_layers):
            if layer % 2 == 0:
                attention_layer(layer)
            else:
                conv_layer(layer)
            if debug and layer == 0:
                P.barrier()
                for i in range(8):
                    P.dma("sp", lambda e, i=i: e.dma_start(out=dbg_o[i], in_=xres[:, FIRST_OWN + i, :]), reads=[xr(FIRST_OWN + i)])
                P.dma("sp", lambda e: e.dma_start(out=dbg_o[8], in_=xres[:, SCH, :]), reads=[xr(SCH)])
            peer_layer(layer)
        P.barrier()
        for i in range(8):
            P.dma("sp", lambda e, i=i: e.dma_start(out=y_o[i], in_=xres[:, FIRST_OWN + i, :]), reads=[xr(FIRST_OWN + i)])
        P.dma("sp", lambda e: e.dma_start(out=y_o[8], in_=xres[:, SCH, :]), reads=[xr(SCH)])
        P.emit(st)
    return nc


def _consts():
    i = np.arange(128)[:, None]
    s = np.arange(128)[None, :]
    maskP = np.concatenate([np.where(s >= i, 0.0, NEG), np.where(s <= i, 0.0, NEG)], axis=1)
    prevS = np.where(s >= (i % 4), 0.0, NEG)
    ownS = np.where((i < 16) & (s < 16) & (s // 4 == i // 4) & (s <= i), 0.0, NEG)
    maskS = np.concatenate([prevS, ownS], axis=1)
    masks = np.stack([maskP, maskS], axis=1).astype(np.float32)
    rowmask = ((i // 4 == np.arange(4)[None, :]) & (i < 16)).astype(np.float32)
    return masks, rowmask


def _rope_table(pos):
    inv = np.power(np.float32(10000.0), -np.arange(32, dtype=np.float32) * np.float32(2.0) / np.float32(64))
    ang = pos.astype(np.float32)[:, None] * inv[None, :]
    return np.concatenate([np.cos(ang), np.sin(ang)], axis=1).astype(np.float32)


def make_in_maps(inputs, cores=range(8)):
    f = lambda a: np.ascontiguousarray(np.asarray(a, dtype=np.float32))
    xp = f(inputs["x_prompt"])
    xs = f(inputs["x_sample"])
    ck = f(inputs["cache_k"]).reshape(2, 32, 128, 512)
    cv = f(inputs["cache_v"]).reshape(2, 32, 128, 512)
    stc = f(inputs["state_conv"])
    masks, rowmask = _consts()
    shared = dict(
        w_qkv=f(inputs["w_qkv"]), b_qkv=f(inputs["b_qkv"]), w_o=f(inputs["w_o"]), b_o=f(inputs["b_o"]),
        attn_sinks=f(inputs["attn_sinks"]).reshape(2, 32), w_conv_in=f(inputs["w_conv_in"]),
        conv_wT=np.ascontiguousarray(f(inputs["conv_w"]).reshape(2, 3, 16, 128).transpose(0, 3, 2, 1)),
        w_conv_out=f(inputs["w_conv_out"]), w_peer_q=f(inputs["w_peer_q"]),
        peer_sub_keys=f(inputs["peer_sub_keys"]).reshape(4, 16, 128, 128),
        peer_u=f(inputs["peer_u"]), peer_v=f(inputs["peer_v"]), ln_g=f(inputs["ln_g"]), ln_b=f(inputs["ln_b"]),
        masks=masks, rowmask=rowmask)
    maps = []
    for c in cores:
        b = c // 4
        b0 = (c % 4) * 8
        xin = np.zeros((NCH, 128, D), np.float32)
        cs = np.zeros((NCH, 128, 64), np.float32)
        for jj in range(NPB):
            g = b0 - 3 + jj
            pos = np.arange(128) + max(g, 0) * 128
            cs[jj] = _rope_table(pos)
            if g >= 0:
                xin[jj] = xp[b, g * 128:(g + 1) * 128]
        xin[SCH, 0:16] = xs[4 * c:4 * c + 4].reshape(16, D)
        cs[SCH] = _rope_table(16384 + (np.arange(128) % 4))
        m = dict(shared)
        m.update(xin=xin, cs=np.ascontiguousarray(cs.transpose(1, 0, 2)),
                 hv=np.full((128, 1), 0.0 if b0 == 0 else 1.0, np.float32),
                 ck=np.ascontiguousarray(ck[:, 4 * c:4 * c + 4]), cv=np.ascontiguousarray(cv[:, 4 * c:4 * c + 4]),
                 stc=np.ascontiguousarray(stc[:, 4 * c:4 * c + 4].reshape(2, 8, D)))
        maps.append(m)
    return maps


def assemble(results):
    yp = np.zeros((2, 4096, D), np.float32)
    ys = np.zeros((32, 4, D), np.float32)
    kp = np.zeros((2, 2, 128, 8, 64), np.float32)
    vp = np.zeros((2, 2, 128, 8, 64), np.float32)
    cp = np.zeros((2, 2, 2, D), np.float32)
    ks = np.zeros((2, 32, 128, 8, 64), np.float32)
    vs = np.zeros((2, 32, 128, 8, 64), np.float32)
    cso = np.zeros((2, 32, 2, D), np.float32)
    for c, r in enumerate(results):
        b = c // 4
        q = c % 4
        yp[b, q * 1024:(q + 1) * 1024] = r["y"][0:8].reshape(1024, D)
        ys[4 * c:4 * c + 4] = r["y"][8, 0:16].reshape(4, 4, D)
        if q == 3:
            kp[:, b] = r["kp"].reshape(2, 128, 8, 64)
            vp[:, b] = r["vp"].reshape(2, 128, 8, 64)
            cp[:, b] = r["cp"]
        ks[:, 4 * c:4 * c + 4] = r["ks"].reshape(2, 4, 128, 8, 64)
        vs[:, 4 * c:4 * c + 4] = r["vs"].reshape(2, 4, 128, 8, 64)
        cso[:, 4 * c:4 * c + 4] = r["cso"]
    return yp, ys, kp, vp, cp, ks, vs, cso


def kernel(**inputs):
    nc = build_program(4)
    maps = make_in_maps(inputs)
    res = run_bass_kernel_spmd(nc, maps, core_ids=list(range(8)))
    return assemble(res.results)
```

```python
import types
import numpy as np
from contextlib import ExitStack
import concourse.bass as bass
import concourse.mybir as mybir
from concourse.bass_utils import run_bass_kernel_spmd

F32 = mybir.dt.float32
BF16 = mybir.dt.bfloat16
I32 = mybir.dt.int32
U32 = mybir.dt.uint32
AF = mybir.ActivationFunctionType
ALU = mybir.AluOpType
AX = mybir.AxisListType

D = 2048
NCH = 12
NPB = 11
SCH = 11
FIRST_OWN = 3
ALPHA = float(8 ** 0.25)
LN_EPS = 1e-5
NEG = -1e30
GROUPS = [[0, 1], [2, 3], [4, 5], [6, 7], [8, 9], [10], [11]]
ENGS = ("pe", "act", "dve", "pool", "sp")


def _freeze(fn):
    if fn.__closure__ is None:
        return fn
    cells = []
    for c in fn.__closure__:
        try:
            cells.append(types.CellType(c.cell_contents))
        except ValueError:
            cells.append(c)
    return types.FunctionType(fn.__code__, fn.__globals__, fn.__name__, fn.__defaults__, tuple(cells))


class Prog:
    def __init__(self, nc, ring=16):
        self.nc = nc
        self.ops = []
        self.last_write = {}
        self.readers = {}
        self.ring = ring
        self.pending_barrier = {}

    def op(self, eng, fn, reads=(), writes=(), dma=False):
        deps = set()
        for r in reads:
            if r in self.last_write:
                deps.add(self.last_write[r])
        for w in writes:
            if w in self.last_write:
                deps.add(self.last_write[w])
            for rd in self.readers.get(w, ()):
                deps.add(rd)
        if eng in self.pending_barrier:
            deps.update(self.pending_barrier.pop(eng))
        i = len(self.ops)
        self.ops.append(dict(eng=eng, fn=_freeze(fn), deps=sorted(deps), dma=dma, sig=None))
        for r in reads:
            self.readers.setdefault(r, []).append(i)
        for w in writes:
            self.last_write[w] = i
            self.readers[w] = []
        return i

    def dma(self, eng, fn, reads=(), writes=()):
        return self.op(eng, fn, reads, writes, dma=True)

    def barrier(self):
        last = {}
        dmas = []
        for i, o in enumerate(self.ops):
            if o["dma"]:
                dmas.append(i)
            else:
                last[o["eng"]] = i
        start = getattr(self, "_bar_start", 0)
        dd = [i for i in dmas if i >= start]
        self._bar_start = len(self.ops)
        deps = set(last.values()) | set(dd)
        for e in ENGS:
            s = set(self.pending_barrier.get(e, ()))
            self.pending_barrier[e] = s | deps

    def emit(self, stack):
        nc = self.nc
        ops = self.ops
        needed = [False] * len(ops)
        for o in ops:
            for d in o["deps"]:
                p = ops[d]
                if p["dma"]:
                    continue
                if p["eng"] == "pe" and o["eng"] == "pe" and not o["dma"]:
                    continue
                needed[d] = True
        esem = {e: stack.enter_context(nc.semaphore("s_" + e)) for e in ENGS}
        dsem = {e: [stack.enter_context(nc.semaphore("d_%s%d" % (e, k))) for k in range(self.ring)]
                for e in ("sp", "pool", "act")}
        cnt = {e: 0 for e in ENGS}
        dcnt = {e: 0 for e in dsem}
        dval = {e: [0] * self.ring for e in dsem}
        for i, o in enumerate(ops):
            if o["dma"]:
                e = o["eng"]
                k = dcnt[e] % self.ring
                dcnt[e] += 1
                o["prev"] = (dsem[e][k], dval[e][k])
                dval[e][k] += 16
                o["sig"] = (dsem[e][k], dval[e][k])
            elif needed[i]:
                cnt[o["eng"]] += 1
                o["sig"] = (esem[o["eng"]], cnt[o["eng"]])
        per = {e: [] for e in ENGS}
        for i, o in enumerate(ops):
            per[o["eng"]].append(i)
        final = []
        for e in dsem:
            for k in range(self.ring):
                if dval[e][k]:
                    final.append((dsem[e][k], dval[e][k]))

        def run(e, engh):
            waited = {}
            for i in per[e]:
                o = ops[i]
                ws = []
                if o["dma"] and o["prev"][1] > 0:
                    ws.append(o["prev"])
                for d in o["deps"]:
                    p = ops[d]
                    if p["sig"] is None:
                        continue
                    if (not p["dma"]) and p["eng"] == "pe" and e == "pe" and not o["dma"]:
                        continue
                    ws.append(p["sig"])
                best = {}
                for s, v in ws:
                    key = id(s)
                    if waited.get(key, 0) >= v:
                        continue
                    if key not in best or best[key][1] < v:
                        best[key] = (s, v)
                for key, (s, v) in best.items():
                    engh.wait_ge(s, v)
                    waited[key] = v
                ins = o["fn"](engh)
                if o["sig"] is not None:
                    ins.then_inc(o["sig"][0], 16 if o["dma"] else 1)
            if e == "sp":
                for s, v in final:
                    engh.wait_ge(s, v)

        block = stack.enter_context(nc.Block())

        @block.tensor
        def _(t):
            run("pe", t)

        @block.scalar
        def _(t):
            run("act", t)

        @block.vector
        def _(t):
            run("dve", t)

        @block.gpsimd
        def _(t):
            run("pool", t)

        @block.sync
        def _(t):
            run("sp", t)


def build_program(n_layers=4, debug=False):
    nc = bass.Bass("TRN2", target_bir_lowering=False)

    def din(name, shape, dtype=F32):
        return nc.dram_tensor(name, shape, dtype, kind="ExternalInput").ap()

    def dout(name, shape, dtype=F32):
        return nc.dram_tensor(name, shape, dtype, kind="ExternalOutput").ap()

    xin = din("xin", [NCH, 128, D])
    cs_d = din("cs", [128, NCH, 64])
    masks_d = din("masks", [128, 2, 256])
    rowmask_d = din("rowmask", [128, 4])
    hv_d = din("hv", [128, 1])
    ck_d = din("ck", [2, 4, 128, 512])
    cv_d = din("cv", [2, 4, 128, 512])
    stc_d = din("stc", [2, 8, D])
    wqkv_d = din("w_qkv", [2, D, 3072])
    bqkv_d = din("b_qkv", [2, 3072])
    wo_d = din("w_o", [2, D, D])
    bo_d = din("b_o", [2, D])
    sinks_d = din("attn_sinks", [2, 32])
    wci_d = din("w_conv_in", [2, D, 6144])
    cwT_d = din("conv_wT", [2, 128, 16, 3])
    wco_d = din("w_conv_out", [2, D, D])
    wpq_d = din("w_peer_q", [4, D, D])
    psk_d = din("peer_sub_keys", [4, 16, 128, 128])
    pu_d = din("peer_u", [4, 16384, D])
    pv_d = din("peer_v", [4, 16384, D])
    lng_d = din("ln_g", [4, 2, D])
    lnb_d = din("ln_b", [4, 2, D])

    y_o = dout("y", [9, 128, D])
    kp_o = dout("kp", [2, 128, 512])
    vp_o = dout("vp", [2, 128, 512])
    cp_o = dout("cp", [2, 2, D])
    ks_o = dout("ks", [2, 4, 128, 512])
    vs_o = dout("vs", [2, 4, 128, 512])
    cso_o = dout("cso", [2, 4, 2, D])
    dbg_o = dout("dbg", [9, 128, D]) if debug else None
    dbgO = dout("dbgO", [128, D]) if debug else None
    dbgE = dout("dbgE", [128, 128], I32) if debug else None
    dbgG = dout("dbgG", [128, 128]) if debug else None
    dbgH = dout("dbgH", [128, 128]) if debug else None
    dbgS = dout("dbgS", [128, 2048]) if debug else None
    dbgSP = dout("dbgSP", [128, 1024]) if debug else None
    dbgP = dout("dbgP", [128, 1024]) if debug else None
    dbgSM = dout("dbgSM", [128, 64]) if debug else None
    dbgU = dout("dbgU", [128, 2048], BF16) if debug else None

    uv_l = [nc.dram_tensor("uv_scratch%d" % l, [16384, 2, D], BF16, kind="Internal").ap() for l in range(4)]
    PEER_MIN_CHUNK = [1, 1, 2, 3]
    wqkv_b = nc.dram_tensor("wqkv_b", [2, 12, 128, 4096], BF16, kind="Internal").ap()
    wo_b = nc.dram_tensor("wo_b", [2, 8, 128, 4096], BF16, kind="Internal").ap()
    wco_b = nc.dram_tensor("wco_b", [2, 8, 128, 4096], BF16, kind="Internal").ap()
    wpq_b = nc.dram_tensor("wpq_b", [4, 8, 128, 4096], BF16, kind="Internal").ap()
    wci_b = nc.dram_tensor("wci_b", [2, 16, 128, 6144], BF16, kind="Internal").ap()

    st = ExitStack()
    with st:
        def sb(name, shape, dt=F32):
            return st.enter_context(nc.sbuf_tensor(name, shape, dt))

        xres = sb("xres", [128, NCH, D])
        identf = sb("identf", [128, 128])
        identb = sb("identb", [128, 128], BF16)
        masks = sb("masks_sb", [128, 3, 256])
        rowmask = sb("rowmask_sb", [128, 4])
        hv = sb("hv_sb", [128, 1])
        negbig = sb("negbig", [128, 1])
        cs = sb("cs_sb", [128, NCH, 64])
        iota16 = sb("iota16", [128, 16])
        xT = sb("xT", [128, 16, 256], BF16)
        wt = [sb("wt%d" % i, [128, 16, 256], BF16) for i in range(2)]
        lnp = sb("lnp", [128, 2, D])
        stats = sb("stats", [128, 4, 6])
        mv = sb("mv", [128, 2])
        rstd = sb("rstd", [128, 1])
        epsb = sb("epsb", [128, 1])
        ARENA = 16560
        arena = sb("arena", [128, ARENA])
        pb = [st.enter_context(nc.psum_tensor("pb%d" % i, [128, 512], F32)) for i in range(8)]

        P = Prog(nc)
        cur = [0]

        def a_reset():
            cur[0] = 0

        def a_f32(n):
            a = cur[0]
            cur[0] += n
            assert cur[0] <= ARENA, cur[0]
            return arena[:, a:a + n]

        def a_bf16(n):
            n32 = (n + 1) // 2
            return a_f32(n32).bitcast(BF16)

        wq = [0]

        P.dma("sp", lambda e: e.dma_start(out=xres[:], in_=xin.rearrange("c p d -> p c d")), writes=["xres%d" % c for c in range(NCH)])
        P.dma("sp", lambda e: e.dma_start(out=masks[:, 0:2, :], in_=masks_d), writes=["masks"])
        P.dma("sp", lambda e: e.dma_start(out=rowmask[:], in_=rowmask_d), writes=["rowmask"])
        P.dma("sp", lambda e: e.dma_start(out=hv[:], in_=hv_d), writes=["hv"])
        P.dma("sp", lambda e: e.dma_start(out=cs[:], in_=cs_d), writes=["cs"])
        P.op("pool", lambda e: e.memset(epsb[:], LN_EPS), writes=["epsb"])
        P.op("pool", lambda e: e.iota(identf[:], pattern=[[1, 128]], base=0, channel_multiplier=-1,
                                      allow_small_or_imprecise_dtypes=True), writes=["identf"])
        P.op("pool", lambda e: e.iota(iota16[:], pattern=[[1, 16]], base=0, channel_multiplier=0,
                                      allow_small_or_imprecise_dtypes=True), writes=["iota16"])
        P.op("dve", lambda e: e.tensor_scalar(out=identb[:], in0=identf[:], scalar1=0.0, scalar2=None, op0=ALU.is_equal),
             reads=["identf"], writes=["identb"])
        P.op("dve", lambda e: e.tensor_scalar(out=identf[:], in0=identf[:], scalar1=0.0, scalar2=None, op0=ALU.is_equal),
             reads=["identf"], writes=["identf"])
        P.op("dve", lambda e: e.tensor_scalar(out=negbig[:], in0=hv[:], scalar1=-1.0, scalar2=1e30, op0=ALU.add, op1=ALU.mult),
             reads=["hv"], writes=["negbig"])
        P.op("dve", lambda e: e.tensor_copy(out=masks[:, 2, :], in_=masks[:, 0, :]), reads=["masks"], writes=["masks"])
        P.op("dve", lambda e: e.tensor_scalar(out=masks[:, 2, 0:128], in0=masks[:, 2, 0:128], scalar1=negbig[:, 0:1], scalar2=None, op0=ALU.add),
             reads=["masks", "negbig"], writes=["masks"])

        def xr(c):
            return "xres%d" % c

        def table_conv_ops(l):
            ops_ = []
            for src, w, nm in ((pu_d, 0, "ub%d" % l), (pv_d, 1, "vb%d" % l)):
                for r in range(8):
                    def go(src=src, w=w, nm=nm, r=r, l=l):
                        P.dma("pool", lambda e: e.dma_start(out=uv_l[l][r * 2048:(r + 1) * 2048, w, :],
                                                            in_=src[l, r * 2048:(r + 1) * 2048, :]), writes=[nm + "_%d" % r])
                    ops_.append(go)
            return ops_

        def conv_w256(src2d, dst_tiles, nm, ntile):
            for ct in range(ntile):
                P.dma("pool", lambda e, ct=ct: e.dma_start(out=dst_tiles[ct].rearrange("p (kc n) -> p kc n", kc=16),
                                                           in_=src2d[:, ct * 256:(ct + 1) * 256].rearrange("(kc p) n -> p kc n", p=128)),
                      writes=["%s_%d" % (nm, ct)])

        def conv_wci(j):
            for cc in range(16):
                src = wci_d[j].rearrange("(kc p) (a c m) -> p kc a c m", p=128, a=3, m=128)[:, :, :, cc, :]
                dst = wci_b[j, cc].rearrange("p (kc a m) -> p kc a m", kc=16, a=3)
                for a in range(3):
                    P.dma("pool", lambda e, src=src, dst=dst, a=a: e.dma_start(out=dst[:, :, a, :], in_=src[:, :, a, :]), writes=["cwci%d_%d_%d" % (j, cc, a)])

        conv_w256(wqkv_d[0], wqkv_b[0], "cqkv0", 12)
        conv_w256(wo_d[0], wo_b[0], "co0", 8)
        conv_w256(wpq_d[0], wpq_b[0], "cpq0", 8)
        for go in table_conv_ops(0):
            go()
        if n_layers > 1:
            conv_wci(0)
            conv_w256(wco_d[0], wco_b[0], "cco0", 8)
            conv_w256(wpq_d[1], wpq_b[1], "cpq1", 8)
        if n_layers > 2:
            conv_w256(wqkv_d[1], wqkv_b[1], "cqkv1", 12)
            conv_w256(wo_d[1], wo_b[1], "co1", 8)
            conv_w256(wpq_d[2], wpq_b[2], "cpq2", 8)
        if n_layers > 3:
            conv_wci(1)
            conv_w256(wco_d[1], wco_b[1], "cco1", 8)
            conv_w256(wpq_d[3], wpq_b[3], "cpq3", 8)

        xT_alias = []

        def build_xT(grp):
            k = 0
            for gi, c in enumerate(grp):
                for q in range(4):
                    bank = 6 + (k % 2)
                    k += 1
                    for r in range(4):
                        kc = q * 4 + r
                        P.op("pe", lambda e, c=c, kc=kc, bank=bank, r=r: e.transpose(
                            out=pb[bank][:, r * 128:(r + 1) * 128], in_=xres[:, c, kc * 128:(kc + 1) * 128], identity=identf[:]),
                            reads=[xr(c), "identf"], writes=["pb%d" % bank])
                    dst = xT[:, q * 4:(q + 1) * 4, gi * 128:(gi + 1) * 128]
                    src = pb[bank][:, :].rearrange("p (r t) -> p r t", r=4)
                    if q % 2 == 0:
                        P.op("act", lambda e, dst=dst, src=src: e.copy(out=dst, in_=src), reads=["pb%d" % bank], writes=["xT"] + xT_alias)
                    else:
                        P.op("dve", lambda e, dst=dst, src=src: e.tensor_copy(out=dst, in_=src), reads=["pb%d" % bank], writes=["xT"] + xT_alias)

        wt_alias = {0: [], 1: []}

        def load_wt(tile_ap, res):
            i = wq[0] % 2
            wq[0] += 1
            P.dma("sp", lambda e, i=i, tile_ap=tile_ap: e.dma_start(out=wt[i][:].rearrange("p k n -> p (k n)"), in_=tile_ap),
                  reads=[res], writes=["wt%d" % i] + wt_alias[i])
            return i

        def load_bcast(dst, src_row, res):
            P.dma("sp", lambda e: e.dma_start(out=dst, in_=src_row.partition_broadcast(128)), writes=[res])

        def layer_norm(c, layer, sub):
            for q in range(4):
                P.op("dve", lambda e, q=q: e.bn_stats(out=stats[:, q, :], in_=xres[:, c, q * 512:(q + 1) * 512]),
                     reads=[xr(c)], writes=["stats"])
            P.op("dve", lambda e: e.bn_aggr(out=mv[:], in_=stats[:].rearrange("p a b -> p (a b)")), reads=["stats"], writes=["mv"])
            P.op("act", lambda e: e.activation(out=rstd[:], in_=mv[:, 1:2], func=AF.Ln, bias=epsb[:, 0:1], scale=1.0),
                 reads=["mv", "epsb"], writes=["rstd"])
            P.op("act", lambda e: e.activation(out=rstd[:], in_=rstd[:], func=AF.Exp, scale=-0.5), reads=["rstd"], writes=["rstd"])
            P.op("dve", lambda e: e.tensor_scalar(out=xres[:, c, :], in0=xres[:, c, :], scalar1=mv[:, 0:1], scalar2=rstd[:, 0:1],
                                                  op0=ALU.subtract, op1=ALU.mult), reads=[xr(c), "mv", "rstd"], writes=[xr(c)])
            P.op("pool", lambda e: e.tensor_tensor(out=xres[:, c, :], in0=xres[:, c, :], in1=lnp[:, 0, :], op=ALU.mult),
                 reads=[xr(c), "lnp"], writes=[xr(c)])
            P.op("pool", lambda e: e.tensor_tensor(out=xres[:, c, :], in0=xres[:, c, :], in1=lnp[:, 1, :], op=ALU.add),
                 reads=[xr(c), "lnp"], writes=[xr(c)])

        def load_ln(layer, sub, extra=()):
            P.dma("sp", lambda e: e.dma_start(out=lnp[:, 0, :], in_=lng_d[layer, sub].partition_broadcast(128)), writes=["lnp"] + list(extra))
            P.dma("sp", lambda e: e.dma_start(out=lnp[:, 1, :], in_=lnb_d[layer, sub].partition_broadcast(128)), writes=["lnp"] + list(extra))

        def out_proj(grp, lhsT_of, lhs_res, w_ap, bias_row, btile, skip=()):
            if all(c in skip for c in grp):
                return
            k = 0
            for ct in range(8):
                wi = load_wt(w_ap[0][ct], "%s_%d" % (w_ap[1], ct))
                if bias_row is not None:
                    bi = ct % 2
                    load_bcast(btile[bi], bias_row[ct * 256:(ct + 1) * 256], "btile%d" % bi)
                for gi, c in enumerate(grp):
                    if c in skip:
                        continue
                    bank = 4 + (k % 2)
                    k += 1
                    for kc in range(16):
                        P.op("pe", lambda e, gi=gi, kc=kc, bank=bank, wi=wi: e.matmul(
                            pb[bank][:, 0:256], lhsT=lhsT_of(gi, kc), rhs=wt[wi][:, kc, :], start=(kc == 0), stop=(kc == 15)),
                            reads=[lhs_res, "wt%d" % wi], writes=["pb%d" % bank])
                    xs = xres[:, c, ct * 256:(ct + 1) * 256]
                    P.op("dve", lambda e, xs=xs, bank=bank: e.scalar_tensor_tensor(
                        out=xs, in0=xs, scalar=ALPHA, in1=pb[bank][:, 0:256], op0=ALU.mult, op1=ALU.add),
                        reads=[xr(c), "pb%d" % bank], writes=[xr(c)])
                    if bias_row is not None:
                        P.op("pool", lambda e, xs=xs, bi=bi: e.tensor_tensor(out=xs, in0=xs, in1=btile[bi], op=ALU.add),
                             reads=[xr(c), "btile%d" % bi], writes=[xr(c)])

        def attention_layer(layer):
            j = layer // 2
            P.barrier()
            a_reset()
            btile = [a_f32(256) for _ in range(2)]
            tmp = [a_f32(256) for _ in range(2)]
            qkr = [a_f32(256) for _ in range(2)]
            rt = [a_f32(128).rearrange("p (h d) -> p h d", d=32) for _ in range(4)]
            QT = a_bf16(2 * 32 * 128)[0:64, :].rearrange("p (g h t) -> p g h t", g=2, h=32)
            KT = [a_bf16(8 * 128)[0:64, :].rearrange("p (h t) -> p h t", h=8) for _ in range(4)]
            Vb = [a_bf16(512) for _ in range(4)]
            SP = [a_f32(1024).rearrange("p (h s) -> p h s", h=4)] * 2
            PT = [a_bf16(8 * 128).rearrange("p (a q) -> p a q", a=8)] * 2
            Oall = a_f32(D)
            OT = a_bf16(16 * 256).rearrange("p (k t) -> p k t", k=16)
            vf = a_f32(512)
            ckf = a_f32(512)
            cvf = vf
            qkb = [a_bf16(256) for _ in range(2)]
            ckb = a_bf16(512)
            pbb = [pb[i][:, :].bitcast(BF16) for i in range(8)]
            sinkb = a_f32(32)
            sm = a_f32(64).rearrange("p (a b) -> p a b", b=4)
            load_bcast(sinkb, sinks_d[j], "sinkb")
            load_ln(layer, 0)
            P.op("pool", lambda e: e.memset(KT[3][:], 0.0), writes=["KT3"])
            P.op("pool", lambda e: e.memset(Vb[3][:], 0.0), writes=["Vb3"])
            prev_slot = [3]
            slot_ctr = [0]

            def softmax_pv(c, gi, kt_prev, v_prev, kt_cur, v_cur, mask_idx, rscale, accumulate, tag):
                for kh in range(8):
                    sb_i = kh % 2
                    S = [pb[0 + 2 * sb_i], pb[1 + 2 * sb_i]]
                    for g4 in range(4):
                        h = kh * 4 + g4
                        bank = S[g4 // 2]
                        off = (g4 % 2) * 256
                        P.op("pe", lambda e, h=h, bank=bank, off=off: e.matmul(
                            bank[:, off:off + 128], lhsT=QT[:, gi, h, :], rhs=kt_prev[0][:, kh, :], start=True, stop=True),
                            reads=["QT", kt_prev[1]], writes=["S%d" % sb_i])
                        P.op("pe", lambda e, h=h, bank=bank, off=off: e.matmul(
                            bank[:, off + 128:off + 256], lhsT=QT[:, gi, h, :], rhs=kt_cur[0][:, kh, :], start=True, stop=True),
                            reads=["QT", kt_cur[1]], writes=["S%d" % sb_i])
                    sp = SP[sb_i]
                    spr = "SP0"
                    mk = masks[:, mask_idx, :]
                    for half in range(2):
                        P.op("dve", lambda e, half=half, sp=sp, S=S, mk=mk: e.scalar_tensor_tensor(
                            out=sp[:, 2 * half:2 * half + 2, :], in0=S[half][:, :].rearrange("p (h s) -> p h s", h=2), scalar=0.125,
                            in1=mk.unsqueeze(1).to_broadcast([128, 2, 256]), op0=ALU.mult, op1=ALU.add),
                            reads=["S%d" % sb_i, "masks"], writes=[spr])
                    st_ = "sm"
                    if debug and layer == 0 and c == NPB - 1 and kh == 0:
                        P.dma("sp", lambda e, sp=sp: e.dma_start(out=dbgSP, in_=sp.rearrange("p h s -> p (h s)")), reads=[spr])
                    P.op("dve", lambda e, sp=sp: e.tensor_reduce(out=sm[:, 0, :], in_=sp, axis=AX.X, op=ALU.max), reads=[spr], writes=[st_])
                    P.op("dve", lambda e, kh=kh: e.tensor_tensor(out=sm[:, 1, :], in0=sm[:, 0, :], in1=sinkb[:, kh * 4:kh * 4 + 4], op=ALU.max),
                         reads=[st_, "sinkb"], writes=[st_])
                    P.op("dve", lambda e: e.tensor_scalar(out=sm[:, 2, :], in0=sm[:, 1, :], scalar1=-1.0, scalar2=None, op0=ALU.mult),
                         reads=[st_], writes=[st_])
                    for g4 in range(4):
                        P.op("act", lambda e, g4=g4, sp=sp: e.activation(out=sp[:, g4, :], in_=sp[:, g4, :], func=AF.Exp,
                                                                         bias=sm[:, 2, g4:g4 + 1], scale=1.0, accum_out=sm[:, 3, g4:g4 + 1]),
                             reads=[spr, st_], writes=[spr, "smsum"])
                    P.op("dve", lambda e, kh=kh: e.tensor_tensor(out=sm[:, 4, :], in0=sinkb[:, kh * 4:kh * 4 + 4], in1=sm[:, 1, :], op=ALU.subtract),
                         reads=[st_, "sinkb"], writes=["sm4"])
                    P.op("act", lambda e: e.activation(out=sm[:, 4, :], in_=sm[:, 4, :], func=AF.Exp), reads=["sm4"], writes=["sm4"])
                    P.op("dve", lambda e: e.tensor_tensor(out=sm[:, 5, :], in0=sm[:, 4, :], in1=sm[:, 3, :], op=ALU.add),
                         reads=["sm4", "smsum"], writes=["sm5"])
                    P.op("dve", lambda e: e.reciprocal(out=sm[:, 6, :], in_=sm[:, 5, :]), reads=["sm5"], writes=["sm6"])
                    if rscale is not None:
                        P.op("dve", lambda e: e.tensor_scalar(out=sm[:, 6, :], in0=sm[:, 6, :], scalar1=rscale, scalar2=None, op0=ALU.mult),
                             reads=["sm6", "rowmask"], writes=["sm6"])
                    if debug and layer == 0 and c == NPB - 1 and kh == 0:
                        P.dma("sp", lambda e, sp=sp: e.dma_start(out=dbgP, in_=sp.rearrange("p h s -> p (h s)")), reads=[spr])
                        P.dma("sp", lambda e: e.dma_start(out=dbgSM, in_=sm.rearrange("p a b -> p (a b)")), reads=["sm", "smsum", "sm4", "sm5", "sm6"])
                    pt = PT[sb_i]
                    ptr = "PT0"
                    for g4 in range(4):
                        bank = pb[4 + (g4 // 2)]
                        for kb in range(2):
                            off = ((g4 % 2) * 2 + kb) * 128
                            P.op("pe", lambda e, g4=g4, kb=kb, bank=bank, off=off, sp=sp: e.transpose(
                                out=bank[:, off:off + 128], in_=sp[:, g4, kb * 128:(kb + 1) * 128], identity=identf[:]),
                                reads=[spr, "identf"], writes=["pb%d" % (4 + g4 // 2)])
                    P.op("act", lambda e, pt=pt: e.copy(out=pt[:, 0:4, :], in_=pb[4][:, :].rearrange("p (a q) -> p a q", a=4)),
                         reads=["pb4"], writes=[ptr])
                    P.op("dve", lambda e, pt=pt: e.tensor_copy(out=pt[:, 4:8, :], in_=pb[5][:, :].rearrange("p (a q) -> p a q", a=4)),
                         reads=["pb5"], writes=[ptr])
                    for g4 in range(4):
                        P.op("pe", lambda e, g4=g4, pt=pt: e.matmul(pb[6][:, g4 * 64:(g4 + 1) * 64], lhsT=pt[:, g4 * 2, :],
                                                                  rhs=v_prev[0][:, kh * 64:(kh + 1) * 64], start=True, stop=False),
                             reads=[ptr, v_prev[1]], writes=["pb6"])
                        P.op("pe", lambda e, g4=g4, pt=pt: e.matmul(pb[6][:, g4 * 64:(g4 + 1) * 64], lhsT=pt[:, g4 * 2 + 1, :],
                                                                  rhs=v_cur[0][:, kh * 64:(kh + 1) * 64], start=False, stop=True),
                             reads=[ptr, v_cur[1]], writes=["pb6"])
                    osl = Oall[:, kh * 256:(kh + 1) * 256].rearrange("p (g d) -> p g d", g=4)
                    opv = pb[6][:, 0:256].rearrange("p (g d) -> p g d", g=4)
                    rb = sm[:, 6, :].unsqueeze(2).to_broadcast([128, 4, 64])
                    if not accumulate:
                        P.op("dve", lambda e, osl=osl, opv=opv, rb=rb: e.tensor_tensor(out=osl, in0=opv, in1=rb, op=ALU.mult),
                             reads=["pb6", "sm6"], writes=["Oall"])
                    else:
                        t4 = tmp[0].rearrange("p (g d) -> p g d", g=4)
                        P.op("dve", lambda e, t4=t4, opv=opv, rb=rb: e.tensor_tensor(out=t4, in0=opv, in1=rb, op=ALU.mult),
                             reads=["pb6", "sm6"], writes=["tmp0"])
                        P.op("dve", lambda e, t4=t4, osl=osl: e.tensor_tensor(out=osl, in0=osl, in1=t4, op=ALU.add),
                             reads=["tmp0", "Oall"], writes=["Oall"])

            kv_only = {0} if layer == 0 else {1}
            groups_l = GROUPS if layer == 0 else [[1]] + GROUPS[1:]
            for grp in groups_l:
                build_xT(grp)
                cur_slots = []
                for gi, c in enumerate(grp):
                    s_ = slot_ctr[0] % 3
                    slot_ctr[0] += 1
                    cur_slots.append(s_)
                k = 0
                for ct in range(12):
                    wi = load_wt(wqkv_b[j, ct], "cqkv%d_%d" % (j, ct))
                    bi = ct % 2
                    load_bcast(btile[bi], bqkv_d[j, ct * 256:(ct + 1) * 256], "btile%d" % bi)
                    for gi, c in enumerate(grp):
                        bank = 4 + (k % 2)
                        k += 1
                        for kc in range(16):
                            P.op("pe", lambda e, gi=gi, kc=kc, bank=bank, wi=wi: e.matmul(
                                pb[bank][:, 0:256], lhsT=xT[:, kc, gi * 128:(gi + 1) * 128], rhs=wt[wi][:, kc, :],
                                start=(kc == 0), stop=(kc == 15)), reads=["xT", "wt%d" % wi], writes=["pb%d" % bank])
                        ti = k % 2
                        if ct < 10:
                            P.op("dve", lambda e, ti=ti, bank=bank, bi=bi: e.tensor_tensor(out=tmp[ti], in0=pb[bank][:, 0:256], in1=btile[bi], op=ALU.add),
                                 reads=["pb%d" % bank, "btile%d" % bi], writes=["tmp%d" % ti])
                            tv = tmp[ti].rearrange("p (h two d) -> p h two d", h=4, two=2)
                            x1 = tv[:, :, 0, :]
                            x2 = tv[:, :, 1, :]
                            cosb = cs[:, c, 0:32].unsqueeze(1).to_broadcast([128, 4, 32])
                            sinb = cs[:, c, 32:64].unsqueeze(1).to_broadcast([128, 4, 32])
                            qv = qkr[ti].rearrange("p (h two d) -> p h two d", h=4, two=2)
                            rr = ["rt0", "rt1", "rt2", "rt3"]
                            P.op("pool", lambda e, x1=x1, cosb=cosb: e.tensor_tensor(out=rt[0], in0=x1, in1=cosb, op=ALU.mult), reads=["tmp%d" % ti, "cs"], writes=[rr[0]])
                            P.op("pool", lambda e, x2=x2, sinb=sinb: e.tensor_tensor(out=rt[1], in0=x2, in1=sinb, op=ALU.mult), reads=["tmp%d" % ti, "cs"], writes=[rr[1]])
                            P.op("dve", lambda e, x2=x2, cosb=cosb: e.tensor_tensor(out=rt[2], in0=x2, in1=cosb, op=ALU.mult), reads=["tmp%d" % ti, "cs"], writes=[rr[2]])
                            P.op("dve", lambda e, x1=x1, sinb=sinb: e.tensor_tensor(out=rt[3], in0=x1, in1=sinb, op=ALU.mult), reads=["tmp%d" % ti, "cs"], writes=[rr[3]])
                            P.op("pool", lambda e, qv=qv: e.tensor_tensor(out=qv[:, :, 0, :], in0=rt[0], in1=rt[1], op=ALU.subtract),
                                 reads=[rr[0], rr[1]], writes=["qkr%d" % ti])
                            P.op("dve", lambda e, qv=qv: e.tensor_tensor(out=qv[:, :, 1, :], in0=rt[2], in1=rt[3], op=ALU.add),
                                 reads=[rr[2], rr[3]], writes=["qkr%d" % ti])
                            tb = 6 + (k % 2)
                            P.op("act", lambda e, ti=ti: e.copy(out=qkb[ti], in_=qkr[ti]), reads=["qkr%d" % ti], writes=["qkb%d" % ti])
                            for hh in range(4):
                                P.op("pe", lambda e, hh=hh, tb=tb, ti=ti: e.transpose(
                                    out=pbb[tb][0:64, hh * 128:(hh + 1) * 128], in_=qkb[ti][:, hh * 64:(hh + 1) * 64], identity=identb[:]),
                                    reads=["qkb%d" % ti, "identb"], writes=["pb%d" % tb])
                            src = pbb[tb][0:64, 0:512].rearrange("p (h t) -> p h t", h=4)
                            if ct < 8:
                                P.op("act", lambda e, src=src, gi=gi, ct=ct: e.copy(out=QT[:, gi, ct * 4:(ct + 1) * 4, :], in_=src),
                                     reads=["pb%d" % tb], writes=["QT"])
                            else:
                                s_ = cur_slots[gi]
                                hb = (ct - 8) * 4
                                P.op("act", lambda e, src=src, s_=s_, hb=hb: e.copy(out=KT[s_][:, hb:hb + 4, :], in_=src),
                                     reads=["pb%d" % tb], writes=["KT%d" % s_])
                                if c == NPB - 1:
                                    P.dma("sp", lambda e, ti=ti, hb=hb: e.dma_start(out=kp_o[j, :, hb * 64:(hb + 4) * 64], in_=qkr[ti]),
                                          reads=["qkr%d" % ti])
                                if c == SCH:
                                    for s in range(4):
                                        P.dma("sp", lambda e, ti=ti, hb=hb, s=s: e.dma_start(
                                            out=ks_o[j, s, 124:128, hb * 64:(hb + 4) * 64], in_=qkr[ti][4 * s:4 * s + 4, :]),
                                            reads=["qkr%d" % ti])
                        else:
                            s_ = cur_slots[gi]
                            vo = (ct - 10) * 256
                            P.op("dve", lambda e, bank=bank, bi=bi, vo=vo: e.tensor_tensor(out=vf[:, vo:vo + 256], in0=pb[bank][:, 0:256], in1=btile[bi], op=ALU.add),
                                 reads=["pb%d" % bank, "btile%d" % bi], writes=["vf"])
                            P.op("act", lambda e, s_=s_, vo=vo: e.copy(out=Vb[s_][:, vo:vo + 256], in_=vf[:, vo:vo + 256]),
                                 reads=["vf"], writes=["Vb%d" % s_])
                            if c == NPB - 1:
                                P.dma("sp", lambda e, vo=vo: e.dma_start(out=vp_o[j, :, vo:vo + 256], in_=vf[:, vo:vo + 256]), reads=["vf"])
                            if c == SCH:
                                for s in range(4):
                                    P.dma("sp", lambda e, vo=vo, s=s: e.dma_start(out=vs_o[j, s, 124:128, vo:vo + 256], in_=vf[4 * s:4 * s + 4, vo:vo + 256]),
                                          reads=["vf"])
                for gi, c in enumerate(grp):
                    s_ = cur_slots[gi]
                    ktc = (KT[s_], "KT%d" % s_)
                    vc = (Vb[s_], "Vb%d" % s_)
                    if c in kv_only:
                        prev_slot[0] = s_
                        continue
                    if c != SCH:
                        ps_ = prev_slot[0]
                        ktp = (KT[ps_], "KT%d" % ps_)
                        vp_ = (Vb[ps_], "Vb%d" % ps_)
                        softmax_pv(c, gi, ktp, vp_, ktc, vc, 2 if c == FIRST_OWN else 0, None, False, "p")
                        prev_slot[0] = s_
                    else:
                        for s in range(4):
                            P.dma("sp", lambda e, s=s: e.dma_start(out=ks_o[j, s, 0:124, :], in_=ck_d[j, s, 4:128, :]))
                            P.dma("sp", lambda e, s=s: e.dma_start(out=vs_o[j, s, 0:124, :], in_=cv_d[j, s, 4:128, :]))
                            P.dma("sp", lambda e, s=s: e.dma_start(out=ckf, in_=ck_d[j, s]), writes=["ckf"])
                            P.dma("sp", lambda e, s=s: e.dma_start(out=cvf, in_=cv_d[j, s]), writes=["vf"])
                            P.op("act", lambda e: e.copy(out=ckb, in_=ckf), reads=["ckf"], writes=["ckb"])
                            for half in range(2):
                                for hh in range(4):
                                    P.op("pe", lambda e, hh=hh, half=half: e.transpose(
                                        out=pbb[7][0:64, hh * 128:(hh + 1) * 128], in_=ckb[:, (half * 4 + hh) * 64:(half * 4 + hh + 1) * 64], identity=identb[:]),
                                        reads=["ckb", "identb"], writes=["pb7"])
                                P.op("act", lambda e, half=half: e.copy(out=KT[3][:, half * 4:half * 4 + 4, :],
                                                                      in_=pbb[7][0:64, 0:512].rearrange("p (h t) -> p h t", h=4)),
                                     reads=["pb7"], writes=["KT3"])
                            P.op("act", lambda e: e.copy(out=Vb[3][:], in_=cvf), reads=["vf"], writes=["Vb3"])
                            softmax_pv(c, gi, (KT[3], "KT3"), (Vb[3], "Vb3"), ktc, vc, 1, rowmask[:, s:s + 1], s > 0, "s")
                    if debug and layer == 0 and c == NPB - 1:
                        P.dma("sp", lambda e: e.dma_start(out=dbgO, in_=Oall), reads=["Oall"])
                    for q in range(4):
                        bank = 4 + (q % 2)
                        for r in range(4):
                            kc = q * 4 + r
                            P.op("pe", lambda e, kc=kc, bank=bank, r=r: e.transpose(
                                out=pb[bank][:, r * 128:(r + 1) * 128], in_=Oall[:, kc * 128:(kc + 1) * 128], identity=identf[:]),
                                reads=["Oall", "identf"], writes=["pb%d" % bank])
                        P.op("act", lambda e, q=q, bank=bank, gi=gi: e.copy(out=OT[:, q * 4:(q + 1) * 4, gi * 128:(gi + 1) * 128],
                                                                          in_=pb[bank][:, :].rearrange("p (r t) -> p r t", r=4)),
                             reads=["pb%d" % bank], writes=["OT"])
                out_proj(grp, lambda gi, kc: OT[:, kc, gi * 128:(gi + 1) * 128], "OT", (wo_b[j], "co%d" % j), bo_d[j], btile, skip=kv_only)
                for c in grp:
                    if c not in kv_only:
                        layer_norm(c, layer, 0)

        def conv_layer(layer):
            j = layer // 2
            P.barrier()
            a_reset()
            w3 = [a_bf16(16 * 3 * 128).rearrange("p (k a m) -> p k a m", k=16, a=3) for _ in range(2)]
            hsb = a_f32(256)
            ubuf = [a_f32(260) for _ in range(2)]
            ctmp = [a_f32(256) for _ in range(2)]
            zT = a_bf16(16 * 256).rearrange("p (k t) -> p k t", k=16)
            ucarry = a_f32(32).rearrange("p (k t) -> p k t", t=2)
            cw = a_f32(48).rearrange("p (k t) -> p k t", t=3)
            stT = a_f32(128).rearrange("p (k s) -> p k s", s=8)
            strow = a_f32(D)
            ubs = [a_f32(24).rearrange("p (s t) -> p s t", t=6) for _ in range(2)]
            usout = a_f32(128).rearrange("p (k s t) -> p k s t", s=4, t=2)
            utr = a_f32(128)
            utro = a_f32(128)
            load_ln(layer, 0)
            P.dma("sp", lambda e: e.dma_start(out=cw, in_=cwT_d[j]), writes=["cw"])
            P.op("pool", lambda e: e.memset(ucarry, 0.0), writes=["ucarry"])
            P.dma("sp", lambda e: e.dma_start(out=strow[0:8, :], in_=stc_d[j]), writes=["strow"])
            for q in range(4):
                for r in range(4):
                    kc = q * 4 + r
                    P.op("pe", lambda e, kc=kc, r=r: e.transpose(out=pb[7][:, r * 8:(r + 1) * 8], in_=strow[0:8, kc * 128:(kc + 1) * 128],
                                                                identity=identf[0:8, 0:8]), reads=["strow", "identf"], writes=["pb7"])
                P.op("act", lambda e, q=q: e.copy(out=stT[:, q * 4:(q + 1) * 4, :], in_=pb[7][:, 0:32].rearrange("p (r s) -> p r s", r=4)),
                     reads=["pb7"], writes=["stT"])
            wk = [0]
            u_only = set() if layer == 1 else {2}
            groups_l = ([[1]] + GROUPS[1:]) if layer == 1 else ([[2], [3]] + GROUPS[2:])
            for grp in groups_l:
                build_xT(grp)
                nt = 128 * len(grp)
                sample = (grp[0] == SCH)
                if sample:
                    P.op("pool", lambda e: e.memset(zT[:], 0.0), writes=["zT"])
                for cc in range(16):
                    wi = wk[0] % 2
                    wk[0] += 1
                    P.dma("sp", lambda e, wi=wi, cc=cc: e.dma_start(out=w3[wi].rearrange("p k a m -> p (k a m)"), in_=wci_b[j, cc]),
                          reads=["cwci%d_%d_%d" % (j, cc, a) for a in range(3)], writes=["w3_%d" % wi])
                    for a in range(3):
                        for kc in range(16):
                            P.op("pe", lambda e, a=a, kc=kc, wi=wi: e.matmul(pb[a][:, 0:nt], lhsT=w3[wi][:, kc, a, :], rhs=xT[:, kc, 0:nt],
                                                                            start=(kc == 0), stop=(kc == 15)),
                                 reads=["xT", "w3_%d" % wi], writes=["pb%d" % a])
                    ui = cc % 2
                    P.op("act", lambda e: e.copy(out=hsb[:, 0:nt], in_=pb[2][:, 0:nt]), reads=["pb2"], writes=["hsb"])
                    if not sample:
                        ub = ubuf[ui]
                        ur = "ubuf%d" % ui
                        P.op("pool", lambda e, ub=ub, cc=cc: e.tensor_copy(out=ub[:, 0:2], in_=ucarry[:, cc, :]), reads=["ucarry"], writes=[ur])
                        P.op("dve", lambda e, ub=ub: e.tensor_tensor(out=ub[:, 2:2 + nt], in0=pb[1][:, 0:nt], in1=hsb[:, 0:nt], op=ALU.mult),
                             reads=["pb1", "hsb"], writes=[ur])
                        if 2 in grp:
                            o2 = 2 + grp.index(2) * 128 + 126
                            P.op("dve", lambda e, ub=ub, o2=o2: e.tensor_scalar(out=ub[:, o2:o2 + 2], in0=ub[:, o2:o2 + 2], scalar1=hv[:, 0:1], scalar2=None, op0=ALU.mult),
                                 reads=[ur, "hv"], writes=[ur])
                        P.op("pool", lambda e, ub=ub, cc=cc: e.tensor_copy(out=ucarry[:, cc, :], in_=ub[:, nt:nt + 2]), reads=[ur], writes=["ucarry"])
                        ct_ = ctmp[ui]
                        cr = "ctmp%d" % ui
                        P.op("dve", lambda e, ub=ub, ct_=ct_, cc=cc: e.tensor_scalar(out=ct_[:, 0:nt], in0=ub[:, 0:nt], scalar1=cw[:, cc, 0:1], scalar2=None, op0=ALU.mult),
                             reads=[ur, "cw"], writes=[cr])
                        P.op("dve", lambda e, ub=ub, ct_=ct_, cc=cc: e.scalar_tensor_tensor(out=ct_[:, 0:nt], in0=ub[:, 1:1 + nt], scalar=cw[:, cc, 1:2], in1=ct_[:, 0:nt],
                                                                                      op0=ALU.mult, op1=ALU.add), reads=[ur, "cw", cr], writes=[cr])
                        P.op("dve", lambda e, ub=ub, ct_=ct_, cc=cc: e.scalar_tensor_tensor(out=ct_[:, 0:nt], in0=ub[:, 2:2 + nt], scalar=cw[:, cc, 2:3], in1=ct_[:, 0:nt],
                                                                                      op0=ALU.mult, op1=ALU.add), reads=[ur, "cw", cr], writes=[cr])
                        P.op("dve", lambda e, ct_=ct_, cc=cc: e.tensor_tensor(out=zT[:, cc, 0:nt], in0=pb[0][:, 0:nt], in1=ct_[:, 0:nt], op=ALU.mult),
                             reads=["pb0", cr], writes=["zT"])
                    else:
                        ub = ubs[ui]
                        ur = "ubs%d" % ui
                        P.op("pool", lambda e, ub=ub, cc=cc: e.tensor_copy(out=ub[:, :, 0:2], in_=stT[:, cc, :].rearrange("p (s t) -> p s t", t=2)),
                             reads=["stT"], writes=[ur])
                        P.op("dve", lambda e, ub=ub: e.tensor_tensor(out=ub[:, :, 2:6], in0=pb[1][:, 0:16].rearrange("p (s t) -> p s t", t=4),
                                                                     in1=hsb[:, 0:16].rearrange("p (s t) -> p s t", t=4), op=ALU.mult),
                             reads=["pb1", "hsb"], writes=[ur])
                        P.op("pool", lambda e, ub=ub, cc=cc: e.tensor_copy(out=usout[:, cc, :, :], in_=ub[:, :, 4:6]), reads=[ur], writes=["usout"])
                        ct_ = ctmp[ui][:, 0:16].rearrange("p (s t) -> p s t", t=4)
                        cr = "ctmp%d" % ui
                        P.op("dve", lambda e, ub=ub, ct_=ct_, cc=cc: e.tensor_scalar(out=ct_, in0=ub[:, :, 0:4], scalar1=cw[:, cc, 0:1], scalar2=None, op0=ALU.mult),
                             reads=[ur, "cw"], writes=[cr])
                        P.op("dve", lambda e, ub=ub, ct_=ct_, cc=cc: e.scalar_tensor_tensor(out=ct_, in0=ub[:, :, 1:5], scalar=cw[:, cc, 1:2], in1=ct_,
                                                                                      op0=ALU.mult, op1=ALU.add), reads=[ur, "cw", cr], writes=[cr])
                        P.op("dve", lambda e, ub=ub, ct_=ct_, cc=cc: e.scalar_tensor_tensor(out=ct_, in0=ub[:, :, 2:6], scalar=cw[:, cc, 2:3], in1=ct_,
                                                                                      op0=ALU.mult, op1=ALU.add), reads=[ur, "cw", cr], writes=[cr])
                        P.op("dve", lambda e, ct_=ct_, cc=cc: e.tensor_tensor(out=zT[:, cc, 0:16].rearrange("p (s t) -> p s t", t=4),
                                                                             in0=pb[0][:, 0:16].rearrange("p (s t) -> p s t", t=4), in1=ct_, op=ALU.mult),
                             reads=["pb0", cr], writes=["zT"])
                if grp[-1] == NPB - 1:
                    P.op("pool", lambda e: e.memset(utr, 0.0), writes=["utr"])
                    P.op("pool", lambda e: e.tensor_copy(out=utr[:, 0:32].rearrange("p (t k) -> p t k", t=2), in_=ucarry.rearrange("p k t -> p t k")),
                         reads=["ucarry"], writes=["utr"])
                    P.op("pe", lambda e: e.transpose(out=pb[3][:, 0:128], in_=utr, identity=identf[:]), reads=["utr", "identf"], writes=["pb3"])
                    P.op("act", lambda e: e.copy(out=utro, in_=pb[3][:, 0:128]), reads=["pb3"], writes=["utro"])
                    P.dma("sp", lambda e: e.dma_start(out=cp_o[j].rearrange("t (k p) -> (t k) p", p=128), in_=utro[0:32, :]), reads=["utro"])
                if sample:
                    P.op("pool", lambda e: e.tensor_copy(out=utr.rearrange("p (s t k) -> p s t k", s=4, t=2), in_=usout.rearrange("p k s t -> p s t k")),
                         reads=["usout"], writes=["utr"])
                    P.op("pe", lambda e: e.transpose(out=pb[3][:, 0:128], in_=utr, identity=identf[:]), reads=["utr", "identf"], writes=["pb3"])
                    P.op("act", lambda e: e.copy(out=utro, in_=pb[3][:, 0:128]), reads=["pb3"], writes=["utro"])
                    P.dma("sp", lambda e: e.dma_start(out=cso_o[j].rearrange("s t (k p) -> (s t k) p", p=128), in_=utro), reads=["utro"])
                out_proj(grp, lambda gi, kc: zT[:, kc, gi * 128:(gi + 1) * 128], "zT", (wco_b[j], "cco%d" % j), None, None, skip=u_only)
                for c in grp:
                    if c not in u_only:
                        layer_norm(c, layer, 0)

        def peer_layer(layer):
            P.barrier()
            a_reset()
            qT = a_bf16(16 * 256).rearrange("p (k t) -> p k t", k=16)
            keysT = a_bf16(16 * 128).rearrange("p (k n) -> p k n", k=16)
            big = a_f32(2048)
            sc = a_f32(2048)
            work = a_f32(512)
            sv = a_f32(256).rearrange("p (a k) -> p a k", k=16)
            si = a_f32(256).bitcast(U32).rearrange("p (a k) -> p a k", k=16)
            sif = a_f32(256).rearrange("p (a k) -> p a k", k=16)
            fv = a_f32(128).rearrange("p (h k) -> p h k", k=16)
            fpu = a_f32(128).bitcast(U32).rearrange("p (h k) -> p h k", k=16)
            k1u = a_f32(128).bitcast(U32).rearrange("p (h k) -> p h k", k=16)
            k2u = a_f32(128).bitcast(U32).rearrange("p (h k) -> p h k", k=16)
            k1f = a_f32(128).rearrange("p (h k) -> p h k", k=16)
            k2f = a_f32(128).rearrange("p (h k) -> p h k", k=16)
            i1f = a_f32(128).rearrange("p (h k) -> p h k", k=16)
            i2f = a_f32(128).rearrange("p (h k) -> p h k", k=16)
            eidx = a_f32(128).bitcast(I32)
            gate = a_f32(128).rearrange("p (h k) -> p h k", k=16)
            gsm = a_f32(16).rearrange("p (a h) -> p a h", h=8)
            hid = a_f32(128)
            ug = [a_bf16(2 * D) for _ in range(2)]
            ug += [wt[0][:, :, :].rearrange("p k n -> p (k n)"), wt[1][:, :, :].rearrange("p k n -> p (k n)"),
                   xT[:, :, :].rearrange("p k n -> p (k n)"), sc.bitcast(BF16), big.bitcast(BF16),
                   lnp[:, 0, :].bitcast(BF16), lnp[:, 1, :].bitcast(BF16)]
            NG = len(ug)
            NPRIV = 2
            wt_alias[0] = ["ug2"]
            wt_alias[1] = ["ug3"]
            xT_alias[:] = ["ug4"]
            SC_AL = ["ug5"]
            BIG_AL = ["ug6"]
            LNP_AL = ["ug7", "ug8"]
            junk = a_bf16(D)
            xb = a_bf16(D)
            diag = [a_bf16(128) for _ in range(4)]
            dgf = [a_f32(128) for _ in range(2)] * 2
            if layer == 0:
                for b_ in range(NPRIV):
                    P.op("pool", lambda e, b_=b_: e.memset(ug[b_], 0.0), writes=["ug%d" % b_])
            for half in range(2):
                P.dma("sp", lambda e, half=half: e.dma_start(out=big.rearrange("p (a d) -> p a d", a=16)[:, half * 8:(half + 1) * 8, :],
                                                          in_=psk_d[layer, half * 8:(half + 1) * 8].rearrange("a n d -> n a d")), writes=["big"] + BIG_AL)
            for q in range(4):
                for r in range(4):
                    hp = q * 4 + r
                    P.op("pe", lambda e, hp=hp, r=r: e.transpose(out=pb[7][:, r * 128:(r + 1) * 128], in_=big[:, hp * 128:(hp + 1) * 128], identity=identf[:]),
                         reads=["big", "identf"], writes=["pb7"])
                P.op("act", lambda e, q=q: e.copy(out=keysT[:, q * 4:(q + 1) * 4, :], in_=pb[7][:, :].rearrange("p (r n) -> p r n", r=4)),
                     reads=["pb7"], writes=["keysT"])
            pending = table_conv_ops(layer + 1) if layer + 1 < n_layers else []
            for grp in GROUPS:
                if all(c < PEER_MIN_CHUNK[layer] for c in grp):
                    continue
                build_xT(grp)
                nt = 128 * len(grp)
                for ct in range(8):
                    wi = load_wt(wpq_b[layer, ct], "cpq%d_%d" % (layer, ct))
                    for m in range(2):
                        hp = ct * 2 + m
                        bank = 4 + (hp % 2)
                        for kc in range(16):
                            P.op("pe", lambda e, kc=kc, m=m, bank=bank, wi=wi: e.matmul(pb[bank][:, 0:nt], lhsT=wt[wi][:, kc, m * 128:(m + 1) * 128],
                                                                                      rhs=xT[:, kc, 0:nt], start=(kc == 0), stop=(kc == 15)),
                                 reads=["xT", "wt%d" % wi], writes=["pb%d" % bank])
                        P.op("act", lambda e, hp=hp, bank=bank: e.copy(out=qT[:, hp, 0:nt], in_=pb[bank][:, 0:nt]), reads=["pb%d" % bank], writes=["qT"])
                for gi, c in enumerate(grp):
                    if c < PEER_MIN_CHUNK[layer]:
                        continue
                    for _ in range(2):
                        if pending:
                            pending.pop(0)()
                    npart = 16 if c == SCH else 128
                    ub_res = ["ub%d_%d" % (layer, r) for r in range(8)]
                    vb_res = ["vb%d_%d" % (layer, r) for r in range(8)]
                    for hp in range(16):
                        bank = hp // 4
                        off = (hp % 4) * 128
                        P.op("pe", lambda e, hp=hp, bank=bank, off=off: e.matmul(pb[bank][:, off:off + 128], lhsT=qT[:, hp, gi * 128:(gi + 1) * 128],
                                                                               rhs=keysT[:, hp, :], start=True, stop=True),
                             reads=["qT", "keysT"], writes=["pb%d" % bank])
                    for b4 in range(4):
                        P.op("act", lambda e, b4=b4: e.copy(out=sc[:, b4 * 512:(b4 + 1) * 512], in_=pb[b4][:, :]), reads=["pb%d" % b4], writes=["sc"] + SC_AL)
                    if debug and layer == 0 and c == NPB - 1:
                        P.dma("sp", lambda e: e.dma_start(out=dbgS, in_=sc), reads=["sc"])
                    SVN = ["sv%d" % i for i in range(16)]
                    SIN = ["si%d" % i for i in range(16)]
                    for q4 in range(4):
                        hps = [q4 * 4 + i for i in range(4)]
                        wk_ = {hp: work[:, (hp % 4) * 128:(hp % 4 + 1) * 128] for hp in hps}
                        for hp in hps:
                            P.op("dve", lambda e, hp=hp: e.max(out=sv[:, hp, 0:8], in_=sc[:, hp * 128:(hp + 1) * 128]), reads=["sc"], writes=[SVN[hp]])
                        for hp in hps:
                            P.op("dve", lambda e, hp=hp: e.max_index(out=si[:, hp, 0:8], in_max=sv[:, hp, 0:8], in_values=sc[:, hp * 128:(hp + 1) * 128]),
                                 reads=["sc", SVN[hp]], writes=[SIN[hp]])
                        for hp in hps:
                            P.op("dve", lambda e, hp=hp: e.match_replace(out=wk_[hp], in_to_replace=sv[:, hp, 0:8], in_values=sc[:, hp * 128:(hp + 1) * 128], imm_value=NEG),
                                 reads=["sc", SVN[hp]], writes=["work%d" % (hp % 4)])
                        for hp in hps:
                            P.op("dve", lambda e, hp=hp: e.max(out=sv[:, hp, 8:16], in_=wk_[hp]), reads=["work%d" % (hp % 4)], writes=[SVN[hp]])
                        for hp in hps:
                            P.op("dve", lambda e, hp=hp: e.max_index(out=si[:, hp, 8:16], in_max=sv[:, hp, 8:16], in_values=wk_[hp]),
                                 reads=["work%d" % (hp % 4), SVN[hp]], writes=[SIN[hp]])
                    P.op("dve", lambda e: e.tensor_copy(out=sif, in_=si), reads=SIN, writes=["sif"])
                    svv = sv.rearrange("p (h two) k -> p h two k", two=2)
                    cand = big.rearrange("p (h a b) -> p h a b", h=8, a=16)
                    P.op("dve", lambda e, svv=svv, cand=cand: e.tensor_tensor(out=cand, in0=svv[:, :, 0, :].unsqueeze(3).to_broadcast([128, 8, 16, 16]),
                                                                            in1=svv[:, :, 1, :].unsqueeze(2).to_broadcast([128, 8, 16, 16]), op=ALU.add),
                         reads=SVN, writes=["big"] + BIG_AL)
                    FVN = ["fv%d" % i for i in range(8)]
                    FPN = ["fp%d" % i for i in range(8)]
                    for q2 in range(4):
                        hs = [q2 * 2, q2 * 2 + 1]
                        wk2 = {h: work[:, (h % 2) * 256:(h % 2 + 1) * 256] for h in hs}
                        wn2 = {h: ["work%d" % ((h % 2) * 2), "work%d" % ((h % 2) * 2 + 1)] for h in hs}
                        for h in hs:
                            P.op("dve", lambda e, h=h: e.max(out=fv[:, h, 0:8], in_=big[:, h * 256:(h + 1) * 256]), reads=["big"], writes=[FVN[h]])
                        for h in hs:
                            P.op("dve", lambda e, h=h: e.max_index(out=fpu[:, h, 0:8], in_max=fv[:, h, 0:8], in_values=big[:, h * 256:(h + 1) * 256]),
                                 reads=["big", FVN[h]], writes=[FPN[h]])
                        for h in hs:
                            P.op("dve", lambda e, h=h: e.match_replace(out=wk2[h], in_to_replace=fv[:, h, 0:8], in_values=big[:, h * 256:(h + 1) * 256], imm_value=NEG),
                                 reads=["big", FVN[h]], writes=wn2[h])
                        for h in hs:
                            P.op("dve", lambda e, h=h: e.max(out=fv[:, h, 8:16], in_=wk2[h]), reads=wn2[h], writes=[FVN[h]])
                        for h in hs:
                            P.op("dve", lambda e, h=h: e.max_index(out=fpu[:, h, 8:16], in_max=fv[:, h, 8:16], in_values=wk2[h]),
                                 reads=wn2[h] + [FVN[h]], writes=[FPN[h]])
                    P.op("dve", lambda e: e.tensor_single_scalar(out=k1u, in_=fpu, scalar=4, op=ALU.logical_shift_right), reads=FPN, writes=["k1u"])
                    P.op("dve", lambda e: e.tensor_single_scalar(out=k2u, in_=fpu, scalar=15, op=ALU.bitwise_and), reads=FPN, writes=["k2u"])
                    P.op("dve", lambda e: e.tensor_copy(out=k1f, in_=k1u), reads=["k1u"], writes=["k1f"])
                    P.op("dve", lambda e: e.tensor_copy(out=k2f, in_=k2u), reads=["k2u"], writes=["k2f"])
                    oh = sc.rearrange("p (h a b) -> p h a b", h=8, a=16)
                    siv = sif.rearrange("p (h two) k -> p h two k", two=2)
                    io = iota16[:, :].unsqueeze(1).unsqueeze(1).to_broadcast([128, 8, 16, 16])
                    for which, kf, dst in ((0, k1f, i1f), (1, k2f, i2f)):
                        P.op("dve", lambda e, kf=kf, oh=oh, io=io: e.tensor_tensor(out=oh, in0=kf.unsqueeze(3).to_broadcast([128, 8, 16, 16]), in1=io, op=ALU.is_equal),
                             reads=["k1f", "k2f", "iota16", "sc"], writes=["sc"])
                        P.op("dve", lambda e, which=which, oh=oh, siv=siv: e.tensor_tensor(out=oh, in0=oh, in1=siv[:, :, which, :].unsqueeze(2).to_broadcast([128, 8, 16, 16]), op=ALU.mult),
                             reads=["sc", "sif"], writes=["sc"])
                        P.op("dve", lambda e, oh=oh, dst=dst: e.tensor_reduce(out=dst, in_=oh, axis=AX.X, op=ALU.add), reads=["sc"], writes=["i12"])
                    P.op("dve", lambda e: e.scalar_tensor_tensor(out=i1f, in0=i1f, scalar=128.0, in1=i2f, op0=ALU.mult, op1=ALU.add), reads=["i12"], writes=["i12"])
                    P.op("dve", lambda e: e.tensor_scalar(out=i1f, in0=i1f, scalar1=0.0, scalar2=None, op0=ALU.add), reads=["i12"], writes=["i12"])
                    P.op("dve", lambda e: e.tensor_copy(out=eidx, in_=i1f.rearrange("p h k -> p (h k)")), reads=["i12"], writes=["eidx"])
                    P.op("dve", lambda e: e.tensor_tensor(out=gate, in0=fv, in1=fv[:, :, 0:1].to_broadcast([128, 8, 16]), op=ALU.subtract), reads=FVN, writes=["gate"])
                    P.op("act", lambda e: e.activation(out=gate, in_=gate, func=AF.Exp), reads=["gate"], writes=["gate"])
                    P.op("dve", lambda e: e.tensor_reduce(out=gsm[:, 0, :], in_=gate, axis=AX.X, op=ALU.add), reads=["gate"], writes=["gsm"])
                    P.op("dve", lambda e: e.reciprocal(out=gsm[:, 1, :], in_=gsm[:, 0, :]), reads=["gsm"], writes=["gsm"])
                    P.op("dve", lambda e: e.tensor_tensor(out=gate, in0=gate, in1=gsm[:, 1, :].unsqueeze(2).to_broadcast([128, 8, 16]), op=ALU.mult),
                         reads=["gate", "gsm"], writes=["gate"])
                    P.op("act", lambda e: e.copy(out=xb, in_=xres[:, c, :]), reads=[xr(c)], writes=["xb"])
                    uv_res = ["ub%d_%d" % (layer, r) for r in range(8)] + ["vb%d_%d" % (layer, r) for r in range(8)]
                    gflat = gate.rearrange("p h k -> p (h k)")
                    SKEW = 0
                    tails = []

                    def make_tail(jj, b, dgi):
                        def tail():
                            P.op("act", lambda e: e.activation(out=diag[dgi], in_=dgf[dgi], func=AF.Copy, scale=gflat[:, jj:jj + 1]),
                                 reads=["dgf%d" % (dgi % 2), "gate"], writes=["diag%d" % dgi])
                            for q in range(4):
                                P.op("pe", lambda e, q=q: e.matmul(pb[q][:, :], lhsT=diag[dgi], rhs=ug[b][:, D + q * 512:D + (q + 1) * 512],
                                                                  start=(jj == 0), stop=(jj == 127)),
                                     reads=["diag%d" % dgi, "ug%d" % b], writes=["pb%d" % q])
                        return tail

                    for jj in range(128):
                        b = jj % (NPRIV if c == SCH else NG)
                        dgi = jj % 4
                        if SKEW and len(tails) >= SKEW:
                            tails.pop(0)()
                        P.dma("pool", lambda e, jj=jj, b=b: e.indirect_dma_start(out=ug[b][0:npart, :], out_offset=None,
                                                                             in_=uv_l[layer].rearrange("e w d -> e (w d)"),
                                                                             in_offset=bass.IndirectOffsetOnAxis(ap=eidx[0:npart, jj:jj + 1], axis=0)),
                              reads=["eidx"] + uv_res, writes=["ug%d" % b])
                        P.op("dve", lambda e, jj=jj, b=b: e.scalar_tensor_tensor(out=junk, in0=ug[b][:, 0:D], scalar=1.0, in1=xb, op0=ALU.mult, op1=ALU.mult,
                                                                              accum_out=hid[:, jj:jj + 1]),
                             reads=["ug%d" % b, "xb"], writes=["junk", "hid%d" % (jj % 8)])
                        P.op("act", lambda e, jj=jj, dgi=dgi: e.activation(out=dgf[dgi], in_=identf[:], func=AF.Gelu, scale=hid[:, jj:jj + 1]),
                             reads=["hid%d" % (jj % 8), "identf"], writes=["dgf%d" % (dgi % 2)])
                        tails.append(make_tail(jj, b, dgi))
                        if not SKEW:
                            tails.pop(0)()
                    while tails:
                        tails.pop(0)()
                    if debug and layer == 0 and c == NPB - 1:
                        P.dma("sp", lambda e: e.dma_start(out=dbgE, in_=eidx), reads=["eidx"])
                        P.dma("sp", lambda e: e.dma_start(out=dbgG, in_=gate.rearrange("p h k -> p (h k)")), reads=["gate"])
                        P.dma("sp", lambda e: e.dma_start(out=dbgH, in_=hid), reads=["hid%d" % i for i in range(8)])
                    for q in range(4):
                        xs = xres[:, c, q * 512:(q + 1) * 512]
                        P.op("dve", lambda e, xs=xs, q=q: e.scalar_tensor_tensor(out=xs, in0=xs, scalar=ALPHA, in1=pb[q][:, :], op0=ALU.mult, op1=ALU.add),
                             reads=[xr(c), "pb%d" % q], writes=[xr(c)])
                    load_ln(layer, 1, LNP_AL)
                    layer_norm(c, layer, 1)
            while pending:
                pending.pop(0)()
            wt_alias[0] = []
            wt_alias[1] = []
            xT_alias[:] = []

        for layer in range(n_layers):
            if layer % 2 == 0:
                attention_layer(layer)
            else:
                conv_layer(layer)
            if debug and layer == 0:
                P.barrier()
                for i in range(8):
                    P.dma("sp", lambda e, i=i: e.dma_start(out=dbg_o[i], in_=xres[:, FIRST_OWN + i, :]), reads=[xr(FIRST_OWN + i)])
                P.dma("sp", lambda e: e.dma_start(out=dbg_o[8], in_=xres[:, SCH, :]), reads=[xr(SCH)])
            peer_layer(layer)
        P.barrier()
        for i in range(8):
            P.dma("sp", lambda e, i=i: e.dma_start(out=y_o[i], in_=xres[:, FIRST_OWN + i, :]), reads=[xr(FIRST_OWN + i)])
        P.dma("sp", lambda e: e.dma_start(out=y_o[8], in_=xres[:, SCH, :]), reads=[xr(SCH)])
        P.emit(st)
    return nc


def _consts():
    i = np.arange(128)[:, None]
    s = np.arange(128)[None, :]
    maskP = np.concatenate([np.where(s >= i, 0.0, NEG), np.where(s <= i, 0.0, NEG)], axis=1)
    prevS = np.where(s >= (i % 4), 0.0, NEG)
    ownS = np.where((i < 16) & (s < 16) & (s // 4 == i // 4) & (s <= i), 0.0, NEG)
    maskS = np.concatenate([prevS, ownS], axis=1)
    masks = np.stack([maskP, maskS], axis=1).astype(np.float32)
    rowmask = ((i // 4 == np.arange(4)[None, :]) & (i < 16)).astype(np.float32)
    return masks, rowmask


def _rope_table(pos):
    inv = np.power(np.float32(10000.0), -np.arange(32, dtype=np.float32) * np.float32(2.0) / np.float32(64))
    ang = pos.astype(np.float32)[:, None] * inv[None, :]
    return np.concatenate([np.cos(ang), np.sin(ang)], axis=1).astype(np.float32)


def make_in_maps(inputs, cores=range(8)):
    f = lambda a: np.ascontiguousarray(np.asarray(a, dtype=np.float32))
    xp = f(inputs["x_prompt"])
    xs = f(inputs["x_sample"])
    ck = f(inputs["cache_k"]).reshape(2, 32, 128, 512)
    cv = f(inputs["cache_v"]).reshape(2, 32, 128, 512)
    stc = f(inputs["state_conv"])
    masks, rowmask = _consts()
    shared = dict(
        w_qkv=f(inputs["w_qkv"]), b_qkv=f(inputs["b_qkv"]), w_o=f(inputs["w_o"]), b_o=f(inputs["b_o"]),
        attn_sinks=f(inputs["attn_sinks"]).reshape(2, 32), w_conv_in=f(inputs["w_conv_in"]),
        conv_wT=np.ascontiguousarray(f(inputs["conv_w"]).reshape(2, 3, 16, 128).transpose(0, 3, 2, 1)),
        w_conv_out=f(inputs["w_conv_out"]), w_peer_q=f(inputs["w_peer_q"]),
        peer_sub_keys=f(inputs["peer_sub_keys"]).reshape(4, 16, 128, 128),
        peer_u=f(inputs["peer_u"]), peer_v=f(inputs["peer_v"]), ln_g=f(inputs["ln_g"]), ln_b=f(inputs["ln_b"]),
        masks=masks, rowmask=rowmask)
    maps = []
    for c in cores:
        b = c // 4
        b0 = (c % 4) * 8
        xin = np.zeros((NCH, 128, D), np.float32)
        cs = np.zeros((NCH, 128, 64), np.float32)
        for jj in range(NPB):
            g = b0 - 3 + jj
            pos = np.arange(128) + max(g, 0) * 128
            cs[jj] = _rope_table(pos)
            if g >= 0:
                xin[jj] = xp[b, g * 128:(g + 1) * 128]
        xin[SCH, 0:16] = xs[4 * c:4 * c + 4].reshape(16, D)
        cs[SCH] = _rope_table(16384 + (np.arange(128) % 4))
        m = dict(shared)
        m.update(xin=xin, cs=np.ascontiguousarray(cs.transpose(1, 0, 2)),
                 hv=np.full((128, 1), 0.0 if b0 == 0 else 1.0, np.float32),
                 ck=np.ascontiguousarray(ck[:, 4 * c:4 * c + 4]), cv=np.ascontiguousarray(cv[:, 4 * c:4 * c + 4]),
                 stc=np.ascontiguousarray(stc[:, 4 * c:4 * c + 4].reshape(2, 8, D)))
        maps.append(m)
    return maps


def assemble(results):
    yp = np.zeros((2, 4096, D), np.float32)
    ys = np.zeros((32, 4, D), np.float32)
    kp = np.zeros((2, 2, 128, 8, 64), np.float32)
    vp = np.zeros((2, 2, 128, 8, 64), np.float32)
    cp = np.zeros((2, 2, 2, D), np.float32)
    ks = np.zeros((2, 32, 128, 8, 64), np.float32)
    vs = np.zeros((2, 32, 128, 8, 64), np.float32)
    cso = np.zeros((2, 32, 2, D), np.float32)
    for c, r in enumerate(results):
        b = c // 4
        q = c % 4
        yp[b, q * 1024:(q + 1) * 1024] = r["y"][0:8].reshape(1024, D)
        ys[4 * c:4 * c + 4] = r["y"][8, 0:16].reshape(4, 4, D)
        if q == 3:
            kp[:, b] = r["kp"].reshape(2, 128, 8, 64)
            vp[:, b] = r["vp"].reshape(2, 128, 8, 64)
            cp[:, b] = r["cp"]
        ks[:, 4 * c:4 * c + 4] = r["ks"].reshape(2, 4, 128, 8, 64)
        vs[:, 4 * c:4 * c + 4] = r["vs"].reshape(2, 4, 128, 8, 64)
        cso[:, 4 * c:4 * c + 4] = r["cso"]
    return yp, ys, kp, vp, cp, ks, vs, cso


def kernel(**inputs):
    nc = build_program(4)
    maps = make_in_maps(inputs)
    res = run_bass_kernel_spmd(nc, maps, core_ids=list(range(8)))
    return assemble(res.results)
```
